# Optimizing a Trainium2 kernel written in Bass

```python
import jax, jax.numpy as jnp
from jax import lax
import numpy as np

D_MODEL = 1024
BATCH = 8
SEQ = 2048
DEPTH = 4

EXPAND = 2
D_MIX = EXPAND * D_MODEL
D_FOURIER = D_MIX // 2
D_GLA = D_MIX - D_FOURIER
N_FOURIER_GROUPS = 4
FOURIER_GROUP = D_FOURIER // N_FOURIER_GROUPS
N_GLA_HEADS = 4
D_GLA_KEY = D_GLA // 2
GLA_HEAD_K = D_GLA_KEY // N_GLA_HEADS
GLA_HEAD_V = D_GLA // N_GLA_HEADS
GATE_RANK = 16
GATE_LOGIT_NORMALIZER = 16.0
CHUNK = 64
EPS = 1e-6
SPLIT_SIZES = (D_FOURIER, D_FOURIER, D_GLA_KEY, D_GLA_KEY, D_GLA, D_GLA, GATE_RANK, GATE_RANK)
D_IN_PROJ = D_FOURIER * 2 + D_GLA_KEY * 2 + D_GLA * 2 + GATE_RANK * 2

kernel_name = "hymba_fnet_bigla_adaln_encoder"


def _split_points():
    pts, acc = [], 0
    for s in SPLIT_SIZES[:-1]:
        acc += s
        pts.append(acc)
    return pts


def rmsnorm(x, g):
    xf = x.astype(jnp.float32)
    y = xf * lax.rsqrt(jnp.mean(xf * xf, axis=-1, keepdims=True) + EPS) * g.astype(jnp.float32)
    return y.astype(x.dtype)


def fourier_mix(u, w_map):
    b, l, _ = u.shape
    ug = u.reshape(b, l, N_FOURIER_GROUPS, FOURIER_GROUP).astype(jnp.float32)
    spec = jnp.fft.fft2(ug, axes=(1, 3), norm="ortho").real
    y = jnp.einsum("blgc,gcd->blgd", spec, w_map.astype(jnp.float32))
    return y.reshape(b, l, D_FOURIER).astype(u.dtype)


def gla_direction(q, k, v, log_a, strict):
    b, h, l, dk = q.shape
    dv = v.shape[-1]
    n = l // CHUNK

    def chunks(t):
        return t.reshape(b, h, n, CHUNK, t.shape[-1])

    q, k, v, log_a = chunks(q), chunks(k), chunks(v), chunks(log_a)
    cum = jnp.cumsum(log_a, axis=3)
    ref = cum[:, :, :, CHUNK // 2:CHUNK // 2 + 1]
    last = cum[:, :, :, -1:]
    qi = q * jnp.exp(cum - ref)
    ki = k * jnp.exp(ref - cum)
    scores = jnp.einsum("bhncd,bhnsd->bhncs", qi, ki)
    mask = jnp.tril(jnp.ones((CHUNK, CHUNK), dtype=bool), k=-1 if strict else 0)
    scores = jnp.where(mask, scores, 0.0)
    o_intra = jnp.einsum("bhncs,bhnsv->bhncv", scores, v)
    q_dec = q * jnp.exp(cum)
    k_dec = k * jnp.exp(last - cum)
    chunk_decay = jnp.exp(last[:, :, :, 0, :])

    def step(state, inp):
        qd, kd, vc, dec = inp
        o = jnp.einsum("bhcd,bhdv->bhcv", qd, state)
        state = dec[..., None] * state + jnp.einsum("bhcd,bhcv->bhdv", kd, vc)
        return state, o

    xs = (jnp.moveaxis(q_dec, 2, 0), jnp.moveaxis(k_dec, 2, 0),
          jnp.moveaxis(v, 2, 0), jnp.moveaxis(chunk_decay, 2, 0))
    s0 = jnp.zeros((b, h, dk, dv), jnp.float32)
    _, o_inter = lax.scan(step, s0, xs)
    o_inter = jnp.moveaxis(o_inter, 0, 2)
    return (o_intra + o_inter).reshape(b, h, l, dv)


def gla_mix(q, k, v, gf_low, gb_low, w_af, b_af, w_ab, b_ab, norm_g):
    b, l, _ = q.shape

    def heads(t, d):
        return t.astype(jnp.float32).reshape(b, l, N_GLA_HEADS, d).transpose(0, 2, 1, 3)

    qh = heads(q, GLA_HEAD_K) * (GLA_HEAD_K ** -0.5)
    kh = heads(k, GLA_HEAD_K)
    vh = heads(v, GLA_HEAD_V)
    log_af = jax.nn.log_sigmoid(gf_low.astype(jnp.float32) @ w_af.astype(jnp.float32)
                                + b_af.astype(jnp.float32)) / GATE_LOGIT_NORMALIZER
    log_ab = jax.nn.log_sigmoid(gb_low.astype(jnp.float32) @ w_ab.astype(jnp.float32)
                                + b_ab.astype(jnp.float32)) / GATE_LOGIT_NORMALIZER
    o_fwd = gla_direction(qh, kh, vh, heads(log_af, GLA_HEAD_K), strict=False)
    flip = lambda t: jnp.flip(t, axis=2)
    o_bwd = flip(gla_direction(flip(qh), flip(kh), flip(vh),
                               flip(heads(log_ab, GLA_HEAD_K)), strict=True))
    o = o_fwd + o_bwd
    o = o * lax.rsqrt(jnp.mean(o * o, axis=-1, keepdims=True) + EPS)
    o = o.transpose(0, 2, 1, 3).reshape(b, l, D_GLA) * norm_g.astype(jnp.float32)
    return o.astype(q.dtype)


def setup_inputs(seed: int = 0) -> dict:
    key = jax.random.key(seed)
    ks = jax.random.split(key, 16)
    f32 = jnp.float32
    nrm = lambda k, shape, s: jax.random.normal(k, shape, f32) * s
    return {
        "x": nrm(ks[0], (BATCH, SEQ, D_MODEL), 1.0),
        "c": nrm(ks[1], (BATCH, D_MODEL), 1.0),
        "norm_g": 1.0 + nrm(ks[2], (DEPTH, D_MODEL), 0.02),
        "w_ada": nrm(ks[3], (DEPTH, D_MODEL, 3 * D_MODEL), D_MODEL ** -0.5),
        "b_ada": nrm(ks[4], (DEPTH, 3 * D_MODEL), 0.02),
        "w_in": nrm(ks[5], (DEPTH, D_MODEL, D_IN_PROJ), D_MODEL ** -0.5),
        "w_fmap": nrm(ks[6], (DEPTH, N_FOURIER_GROUPS, FOURIER_GROUP, FOURIER_GROUP), FOURIER_GROUP ** -0.5),
        "w_af": nrm(ks[7], (DEPTH, GATE_RANK, D_GLA_KEY), GATE_RANK ** -0.5),
        "b_af": nrm(ks[8], (DEPTH, D_GLA_KEY), 0.1),
        "w_ab": nrm(ks[9], (DEPTH, GATE_RANK, D_GLA_KEY), GATE_RANK ** -0.5),
        "b_ab": nrm(ks[10], (DEPTH, D_GLA_KEY), 0.1),
        "gla_norm_g": 1.0 + nrm(ks[11], (DEPTH, D_GLA), 0.02),
        "w_out": nrm(ks[12], (DEPTH, D_MIX, D_MODEL), D_MIX ** -0.5),
        "final_g": 1.0 + nrm(ks[13], (D_MODEL,), 0.02),
    }


def reference(x, c, norm_g, w_ada, b_ada, w_in, w_fmap, w_af, b_af, w_ab, b_ab,
              gla_norm_g, w_out, final_g):
    pts = _split_points()
    c_act = jax.nn.silu(c)
    for layer in range(DEPTH):
        mod = c_act @ w_ada[layer] + b_ada[layer]
        shift, scale, gate = jnp.split(mod, 3, axis=-1)
        h = rmsnorm(x, norm_g[layer]) * (1.0 + scale[:, None, :]) + shift[:, None, :]
        proj = h @ w_in[layer]
        u_f, z_f, q, k, v, r, gf_low, gb_low = jnp.split(proj, pts, axis=-1)
        y_f = fourier_mix(u_f, w_fmap[layer]) * jax.nn.silu(z_f)
        y_g = gla_mix(q, k, v, gf_low, gb_low, w_af[layer], b_af[layer],
                      w_ab[layer], b_ab[layer], gla_norm_g[layer]) * jax.nn.silu(r)
        y = jnp.concatenate([y_f, y_g], axis=-1) @ w_out[layer]
        x = x + gate[:, None, :] * y
    return rmsnorm(x, final_g)
```

```python
import contextlib
import numpy as np
import ml_dtypes
import concourse.bass as bass
import concourse.mybir as mybir
from concourse.bass_utils import run_bass_kernel_spmd

F32 = mybir.dt.float32
BF16 = mybir.dt.bfloat16
AF = mybir.ActivationFunctionType
ALU = mybir.AluOpType
AX = mybir.AxisListType

L = 2048
D = 1024
DEPTH = 4
DIN = 5152
EPS = 1e-6
ENGS = ["pe", "act", "dve", "pool", "sp"]
UNIQUE_POOL_SEMS = False


class Op:
    __slots__ = ("fn", "deps", "dma", "sig", "cnt")

    def __init__(self, fn, deps, dma):
        self.fn = fn
        self.deps = deps
        self.dma = dma
        self.sig = False
        self.cnt = 0


class Prog:
    def __init__(self, nc, n_chan=24):
        self.nc = nc
        self.ops = {e: [] for e in ENGS}
        self.tok = {}
        self.n_chan = n_chan
        self.chan_cnt = [0] * n_chan
        self.rr = 0
        self.rr_pool = 0
        self.arena_base = {}

    def _st(self, t):
        st = self.tok.get(t)
        if st is None:
            base = []
            if isinstance(t, tuple) and t[0] in self.arena_base:
                base = list(self.arena_base[t[0]])
            st = {"w": None, "r": base}
            self.tok[t] = st
        return st

    def arena_reset(self, tag):
        ev = set(self.arena_base.get(tag, []))
        for t in list(self.tok.keys()):
            if isinstance(t, tuple) and t[0] == tag:
                st = self.tok.pop(t)
                if st["w"] is not None:
                    ev.add(st["w"])
                ev.update(st["r"])
        best = {}
        for e in ev:
            key = (e[0], e[1])
            if key not in best or e[2] > best[key][2]:
                best[key] = e
        self.arena_base[tag] = list(best.values())

    def add(self, eng, fn, reads=(), writes=(), dma=False, chan=None):
        idx = len(self.ops[eng])
        deps = set()
        d = None
        if dma:
            if chan is None and eng == "pool" and UNIQUE_POOL_SEMS:
                chan = len(self.chan_cnt)
                self.chan_cnt.append(0)
            elif chan is None:
                half = self.n_chan // 2
                if eng == "pool":
                    chan = half + self.rr_pool
                    self.rr_pool = (self.rr_pool + 1) % (self.n_chan - half)
                else:
                    chan = self.rr
                    self.rr = (self.rr + 1) % half
            k = self.chan_cnt[chan]
            self.chan_cnt[chan] += 1
            d = (chan, k)
            if k > 0:
                deps.add(("dma", chan, k - 1))
            me = ("dma", chan, k)
        else:
            me = ("eng", eng, idx)
        for t in reads:
            st = self._st(t)
            if st["w"] is not None:
                deps.add(st["w"])
        for t in writes:
            st = self._st(t)
            if st["w"] is not None:
                deps.add(st["w"])
            deps.update(st["r"])
        for t in reads:
            self.tok[t]["r"].append(me)
        for t in writes:
            st = self.tok[t]
            st["w"] = me
            st["r"] = []
        deps.discard(me)
        self.ops[eng].append(Op(fn, deps, d))
        return me

    def pe(self, fn, r=(), w=()):
        return self.add("pe", fn, r, w)

    def act(self, fn, r=(), w=()):
        return self.add("act", fn, r, w)

    def dve(self, fn, r=(), w=()):
        return self.add("dve", fn, r, w)

    def pool(self, fn, r=(), w=()):
        return self.add("pool", fn, r, w)

    def dma(self, q, fn, r=(), w=(), chan=None):
        return self.add(q, fn, r, w, dma=True, chan=chan)

    def _skip(self, e, i, dep):
        _, de, di = dep
        if de != e:
            return False
        if e == "pe":
            return True
        return (i - di) > 3

    def emit(self, final_waits=()):
        nc = self.nc
        for e in ENGS:
            for i, op in enumerate(self.ops[e]):
                for dep in op.deps:
                    if dep[0] == "eng" and not self._skip(e, i, dep):
                        self.ops[dep[1]][dep[2]].sig = True
        for e in ENGS:
            c = 0
            for op in self.ops[e]:
                if op.sig and op.dma is None:
                    c += 1
                op.cnt = c
        stack = contextlib.ExitStack()
        sems = {e: stack.enter_context(nc.semaphore("s_" + e)) for e in ENGS}
        dsems = [stack.enter_context(nc.semaphore("d_%d" % i)) for i in range(len(self.chan_cnt))]
        block = stack.enter_context(nc.Block())

        def run_engine(e, eng):
            seen = {}
            for i, op in enumerate(self.ops[e]):
                need = {}
                for dep in op.deps:
                    if dep[0] == "eng":
                        if self._skip(e, i, dep):
                            continue
                        key = ("e", dep[1])
                        val = self.ops[dep[1]][dep[2]].cnt
                    else:
                        key = ("d", dep[1])
                        val = 16 * (dep[2] + 1)
                    if val > need.get(key, 0):
                        need[key] = val
                for key, val in need.items():
                    if seen.get(key, 0) >= val:
                        continue
                    seen[key] = val
                    s = sems[key[1]] if key[0] == "e" else dsems[key[1]]
                    eng.wait_ge(s, val)
                ins = op.fn(eng)
                if op.dma is not None:
                    ins.then_inc(dsems[op.dma[0]], 16)
                elif op.sig:
                    ins.then_inc(sems[e], 1)
            if e == "sp":
                for ev in final_waits:
                    eng.wait_ge(dsems[ev[1]], 16 * (ev[2] + 1))

        @block.tensor
        def _(eng):
            run_engine("pe", eng)

        @block.scalar
        def _(eng):
            run_engine("act", eng)

        @block.vector
        def _(eng):
            run_engine("dve", eng)

        @block.gpsimd
        def _(eng):
            run_engine("pool", eng)

        @block.sync
        def _(eng):
            run_engine("sp", eng)

        stack.close()


_CONSTS = None


def _consts():
    global _CONSTS
    if _CONSTS is not None:
        return _CONSTS
    bf = ml_dtypes.bfloat16
    l = np.arange(1024, dtype=np.int64)[:, None]
    k = np.arange(2048, dtype=np.int64)[None, :]
    ang = 2.0 * np.pi * ((l * k) % 2048).astype(np.float64) / 2048.0
    sc = 1.0 / np.sqrt(2048.0)
    cosT = np.cos(ang) * sc
    nsinT = -np.sin(ang) * sc
    tab = np.stack([cosT, nsinT], axis=1)
    tab = tab.reshape(8, 128, 2, 4, 512).transpose(0, 1, 3, 2, 4)
    tab = np.ascontiguousarray(tab).astype(bf)
    alt = (((-1.0) ** np.arange(2048)) * sc).reshape(1, 2048).astype(bf)
    c = np.arange(256, dtype=np.int64)
    a2 = 2.0 * np.pi * ((c[:, None] * c[None, :]) % 256).astype(np.float64) / 256.0
    cs = np.stack([np.cos(a2) / 16.0, np.sin(a2) / 16.0], axis=1)
    cs = cs.reshape(2, 128, 2, 256).transpose(1, 0, 2, 3)
    cs256 = np.ascontiguousarray(cs).astype(bf)
    ident = np.eye(128, dtype=np.float32)
    identb = np.eye(128).astype(bf)
    s = np.arange(128)[:, None]
    t = np.arange(128)[None, :]
    maskf = (t >= s).astype(np.float32).astype(bf)
    maskb = (s > t).astype(np.float32).astype(bf)
    onesb = np.ones((128, 128), dtype=np.float32).astype(bf)
    one11 = np.ones((1, 1), dtype=np.float32)
    _CONSTS = dict(tab=tab, alt=alt, cs256=cs256, ident=ident, identb=identb, maskf=maskf,
                   maskb=maskb, onesb=onesb, one11=one11)
    return _CONSTS


def build_nc(depth=DEPTH):
    nc = bass.Bass("TRN2", target_bir_lowering=False)

    def din(name, shape, dt=F32):
        return nc.dram_tensor(name, list(shape), dt, kind="ExternalInput").ap()

    x_d = din("x", [L, D])
    vrows_d = din("vrows", [112, 128])
    bada_d = din("b_ada", [96, 128])
    wada_d = din("w_ada", [DEPTH, D, 3072])
    win_d = din("w_in", [DEPTH, D, DIN])
    wfm_d = din("w_fmap", [DEPTH, 4, 256, 256])
    waf_d = din("w_af", [DEPTH, 16, 512])
    wab_d = din("w_ab", [DEPTH, 16, 512])
    wout_d = din("w_out", [DEPTH, 2048, D])
    tab_d = din("tab", [8, 128, 4, 2, 512], BF16)
    alt_d = din("alt", [1, 2048], BF16)
    cs_d = din("cs256", [128, 2, 2, 256], BF16)
    ident_d = din("ident", [128, 128])
    identb_d = din("identb", [128, 128], BF16)
    maskf_d = din("maskf", [128, 128], BF16)
    maskb_d = din("maskb", [128, 128], BF16)
    onesb_d = din("onesb", [128, 128], BF16)
    one11_d = din("one11", [1, 1])
    y_d = nc.dram_tensor("y", [L, D], F32, kind="ExternalOutput").ap()

    es = contextlib.ExitStack()

    def sb(name, shape, dt):
        return es.enter_context(nc.sbuf_tensor("sb_" + name, list(shape), dt))

    xT = sb("xT", [128, 8, L], F32)
    hT = sb("hT", [128, 8, L], BF16)
    AR = sb("AR", [128, 13312], F32)
    WA = sb("WA", [128, 6144], F32)
    WO = sb("WO", [128, 2048], F32)
    BB = sb("BB", [128, 3072], F32)
    cs256 = sb("cs256", [128, 2, 2, 256], BF16)
    ident = sb("ident", [128, 128], F32)
    identb = sb("identb", [128, 128], BF16)
    maskf = sb("maskf", [128, 128], BF16)
    maskb = sb("maskb", [128, 128], BF16)
    onesb = sb("onesb", [128, 128], BF16)
    one11 = sb("one11", [1, 1], F32)
    alt = sb("alt", [1, 2048], BF16)
    vrows = sb("vrows", [112, 128], F32)
    vcols = sb("vcols", [128, 112], F32)
    vneg = sb("vneg", [128, 32], F32)
    cact = sb("cact", [128, 8], BF16)
    vrows2 = sb("vrows2", [96, 128], F32)
    vcols2 = sb("vcols2", [128, 96], F32)
    modT2 = sb("modT2", [128, 2, 24], F32)
    Acol2 = sb("Acol2", [128, 2, 8], F32)
    a1tmp = sb("a1tmp", [128, 8], F32)
    rpos = sb("rpos", [128, 17], F32)
    rneg = sb("rneg", [128, 17], F32)
    dtmp = sb("dtmp", [128, 16], F32)
    dcol2 = sb("dcol2", [128, 2, 16], F32)
    ssr2 = sb("ssr2", [128, 16], F32)
    ssq = sb("ssq", [128, 16], F32)
    ssr = sb("ssr", [128, 16], F32)
    a1024 = sb("a1024", [1, 512], BF16)
    u1024 = sb("u1024", [128, 2], BF16)

    psb = [es.enter_context(nc.psum_tensor("ps%d" % i, [128, 512], F32)) for i in range(7)]
    pst = es.enter_context(nc.psum_tensor("pst", [128, 1024], BF16))

    P = Prog(nc)

    def PB(b):
        return [("ps", b)]

    PSS = [4, 6, 5]
    PSO = [3, 2]
    PSR = [5, 4]
    ps6b = psb[6][:, :].bitcast(BF16)
    TRB = [(pst, 7), (ps6b, 6)]

    def carve(base, off_b, nbytes, dt, pattern=None, **kw):
        v = base[:, off_b // 4:(off_b + nbytes) // 4]
        if dt != F32:
            v = v.bitcast(dt)
        if pattern is not None:
            v = v.rearrange(pattern, **kw)
        return v

    K = 1024
    for (dst, src, tokn) in [(ident, ident_d, "ident"), (identb, identb_d, "identb"), (maskf, maskf_d, "maskf"),
                             (maskb, maskb_d, "maskb"), (onesb, onesb_d, "onesb"), (one11, one11_d, "one11"),
                             (alt, alt_d, "alt"), (vrows, vrows_d, "vrows"), (vrows2, bada_d, "vrows2")]:
        P.dma("sp", lambda e, dst=dst, src=src: e.dma_start(out=dst[:], in_=src), w=[tokn])
    P.dma("sp", lambda e: e.dma_start(out=cs256[:], in_=cs_d), w=["cs256"])
    P.pe(lambda e: e.transpose(psb[2][:, 0:112], vrows[:, :], ident[0:112, 0:112]), r=["vrows", "ident"], w=PB(2))
    P.dve(lambda e: e.tensor_copy(out=vcols[:], in_=psb[2][:, 0:112]), r=PB(2), w=["vcols"])
    P.pe(lambda e: e.transpose(psb[2][:, 0:96], vrows2[:, :], ident[0:96, 0:96]), r=["vrows2", "ident"], w=PB(2))
    P.dve(lambda e: e.tensor_copy(out=vcols2[:], in_=psb[2][:, 0:96]), r=PB(2), w=["vcols2"])
    P.dve(lambda e: e.tensor_scalar(out=vneg[:], in0=vcols[:, 64:96], scalar1=-1.0, scalar2=None, op0=ALU.mult),
          r=["vcols"], w=["vneg"])
    P.act(lambda e: e.activation(out=cact[:], in_=vcols[:, 104:112], func=AF.Silu), r=["vcols"], w=["cact"])

    tmpR = sb("tmpR", [128, 2, 512], F32)
    rescnt = [0]

    def residual(dt, tt, b, mT, mtok):
        i = rescnt[0]
        rescnt[0] += 1
        if True:
            P.dve(lambda e: e.scalar_tensor_tensor(
                out=xT[:, dt, tt * 512:(tt + 1) * 512], in0=psb[b][:, :], scalar=mT[:, 16 + dt:17 + dt],
                in1=xT[:, dt, tt * 512:(tt + 1) * 512], op0=ALU.mult, op1=ALU.add),
                r=PB(b) + [mtok, ("xT", dt, tt)], w=[("xT", dt, tt)])
        else:
            sl = (i // 2) % 2
            P.act(lambda e: e.activation(out=tmpR[:, sl, :], in_=psb[b][:, :], func=AF.Identity,
                                         scale=mT[:, 16 + dt:17 + dt]),
                  r=PB(b) + [mtok], w=[("tmpR", sl)])
            P.pool(lambda e: e.tensor_tensor(out=xT[:, dt, tt * 512:(tt + 1) * 512], in0=xT[:, dt, tt * 512:(tt + 1) * 512],
                                             in1=tmpR[:, sl, :], op=ALU.add),
                   r=[("tmpR", sl), ("xT", dt, tt)], w=[("xT", dt, tt)])

    projbank = [0]

    def nextbank():
        projbank[0] ^= 1
        return projbank[0]

    obank = [0]

    def nextobank(banks):
        obank[0] = (obank[0] + 1) % len(banks)
        return banks[obank[0]]

    RB = [2, 3]

    def rms_sums(tt, par, use_act=False):
        sq = [carve(AR, 40960 + (par * 2 + i) * 1024, 1024, BF16) for i in range(2)]
        rt = carve(AR, 45056 + par * 2048, 2048, F32)
        rstd = carve(AR, 49152 + par * 2048, 2048, F32)
        bk = RB[par]
        for kc in range(8):
            s_ = kc % 2
            if use_act and kc % 2 == 0:
                P.act(lambda e, kc=kc, s_=s_: e.activation(out=sq[s_], in_=xT[:, kc, tt * 512:(tt + 1) * 512], func=AF.Square),
                      r=[("xT", kc, tt)], w=[("A", "sq", par, s_)])
            else:
                P.pool(lambda e, kc=kc, s_=s_: e.tensor_tensor(out=sq[s_], in0=xT[:, kc, tt * 512:(tt + 1) * 512],
                                                              in1=xT[:, kc, tt * 512:(tt + 1) * 512], op=ALU.mult),
                       r=[("xT", kc, tt)], w=[("A", "sq", par, s_)])
            P.pe(lambda e, kc=kc, s_=s_: e.matmul(psb[bk][:, :], onesb[:, :], sq[s_], start=(kc == 0), stop=(kc == 7)),
                 r=[("A", "sq", par, s_), "onesb"], w=PB(bk))
        P.act(lambda e: e.activation(out=rt, in_=psb[bk][:, :], func=AF.Sqrt, scale=1.0 / D, bias=EPS),
              r=PB(bk), w=[("A", "rt", par)])
        P.dve(lambda e: e.reciprocal(out=rstd, in_=rt), r=[("A", "rt", par)], w=[("A", "rstd", par)])
        return rstd

    modraw = sb("modraw", [128, 24], F32)

    def mod_chunk_dma(l, ci, Wa, tok):
        P.dma("pool", lambda e: e.dma_start(
            out=Wa, in_=wada_d[l, :, ci * 512:(ci + 1) * 512].rearrange("(k p) n -> p k n", p=128)), w=[tok])

    def mod_chunk_mm(l, ci, Wa, tok):
        sl = ci % 2
        b = nextbank()
        for kc in range(8):
            P.pe(lambda e, kc=kc: e.matmul(psb[b][0:1, :], cact[:, kc:kc + 1], Wa[:, kc, :],
                                          start=(kc == 0), stop=(kc == 7)),
                 r=[tok, "cact"], w=PB(b))
        P.dve(lambda e: e.tensor_copy(out=tmpR[0:1, sl, :], in_=psb[b][0:1, :]), r=PB(b), w=[("tmpR", sl)])
        for q in range(4):
            P.pe(lambda e, q=q: e.matmul(psb[2][:, q:q + 1], tmpR[0:1, sl, q * 128:(q + 1) * 128],
                                        one11[0:1, 0:1], start=True, stop=True),
                 r=[("tmpR", sl), "one11"], w=PB(2))
        P.dve(lambda e: e.tensor_copy(out=modraw[:, ci * 4:(ci + 1) * 4], in_=psb[2][:, 0:4]), r=PB(2), w=["modraw"])

    def mod_finish(l):
        mp = l % 2
        P.dve(lambda e: e.tensor_tensor(out=modT2[:, mp, :], in0=modraw[:, :], in1=vcols2[:, l * 24:(l + 1) * 24],
                                        op=ALU.add), r=["modraw", "vcols2"], w=[("modT", mp)])
        P.dve(lambda e: e.tensor_scalar(out=a1tmp[:], in0=modT2[:, mp, 8:16], scalar1=1.0, scalar2=None, op0=ALU.add),
              r=[("modT", mp)], w=["a1tmp"])
        P.dve(lambda e: e.tensor_tensor(out=Acol2[:, mp, :], in0=a1tmp[:], in1=vcols[:, l * 8:(l + 1) * 8], op=ALU.mult),
              r=["a1tmp", "vcols"], w=[("Acol", mp)])

    if True:
        P.arena_reset("W")
        for ci in range(6):
            s_ = ci % 2
            Wa = carve(WA, s_ * 8192, 8192, BF16, "p (k n) -> p k n", n=512)
            mod_chunk_dma(0, ci, Wa, ("W", "wa", s_))
            mod_chunk_mm(0, ci, Wa, ("W", "wa", s_))
        mod_finish(0)

    P.arena_reset("A")
    xst = [carve(AR, i * 4096, 4096, F32) for i in range(4)]
    for j in range(16):
        s = j % 4
        P.dma("sp", lambda e, j=j, s=s: e.dma_start(out=xst[s], in_=x_d[j * 128:(j + 1) * 128, :]),
              w=[("A", "xst", s)])
        for g in range(2):
            b = g
            for q in range(4):
                dc = g * 4 + q
                P.pe(lambda e, b=b, q=q, dc=dc, s=s: e.transpose(psb[b][:, q * 128:(q + 1) * 128],
                                                             xst[s][:, dc * 128:(dc + 1) * 128], ident[:, :]),
                     r=[("A", "xst", s), "ident"], w=PB(b))
            wtok = [("xT", 4 * g + q, j // 4) for q in range(4)]
            if g == 0:
                P.act(lambda e, b=b, g=g, j=j: e.activation(out=xT[:, 4 * g:4 * g + 4, j * 128:(j + 1) * 128],
                                                           in_=psb[b][:, :].rearrange("p (a c) -> p a c", c=128),
                                                           func=AF.Copy), r=PB(b), w=wtok)
            else:
                P.dve(lambda e, b=b, g=g, j=j: e.tensor_copy(out=xT[:, 4 * g:4 * g + 4, j * 128:(j + 1) * 128],
                                                            in_=psb[b][:, :].rearrange("p (a c) -> p a c", c=128)),
                      r=PB(b), w=wtok)


    for l in range(depth):
        mT = modT2[:, l % 2, :]
        mtok = ("modT", l % 2)

        P.arena_reset("A")
        xr = [carve(AR, i * 2048, 2048, F32) for i in range(2)]

        def rms_apply(tt, par, rstd):
            for kc in range(8):
                s_ = kc % 2
                P.dve(lambda e, kc=kc, s_=s_: e.tensor_tensor(
                    out=xr[s_], in0=xT[:, kc, tt * 512:(tt + 1) * 512], in1=rstd, op=ALU.mult),
                    r=[("xT", kc, tt), ("A", "rstd", par)], w=[("A", "xr", s_)])
                P.act(lambda e, kc=kc, s_=s_, mT=mT, lp=l % 2: e.activation(
                    out=hT[:, kc, tt * 512:(tt + 1) * 512], in_=xr[s_], func=AF.Identity,
                    scale=Acol2[:, lp, kc:kc + 1], bias=mT[:, kc:kc + 1]),
                    r=[("A", "xr", s_), ("Acol", l % 2), mtok], w=[("hT", tt)])

        r0 = rms_sums(0, 0, True)
        r1 = rms_sums(1, 1, True)
        rms_apply(0, 0, r0)
        r2 = rms_sums(2, 0)
        rms_apply(1, 1, r1)
        r3 = rms_sums(3, 1)
        rms_apply(2, 0, r2)
        rms_apply(3, 1, r3)

        P.arena_reset("A")
        P.arena_reset("W")
        P.arena_reset("B")
        P.arena_reset("WO")
        gT = carve(BB, 0, 8192, BF16, "p (w t) -> p w t", t=L)[0:16]
        waf = carve(BB, 8192, 2048, BF16, "p (w n) -> p w n", n=512)[0:16]
        Wg = carve(BB, 10240, 512, BF16, "p (k n) -> p k n", n=32)
        qT = carve(AR, 0, 4096, BF16)
        kT = carve(AR, 4096, 4096, BF16)
        vh = carve(AR, 8192, 8192, BF16, "p (j v) -> p j v", v=256)
        cum = carve(AR, 16384, 8192, F32)
        e1 = carve(AR, 24576, 8192, F32)
        ofwd = carve(AR, 32768, 8192, BF16, "p (j v) -> p j v", v=256)
        ygT = carve(AR, 40960, 8192, BF16, "p (f t) -> p f t", t=L)
        ygTf = carve(AR, 40960, 8192, F32)
        Pb = [carve(AR, 49152, 4096, BF16), carve(WO, 4096, 4096, BF16), carve(WA, 19968, 4096, BF16)]
        Ptag = [("A", "P1"), ("WO", "P2"), ("W", "P3")]
        Wqk = carve(WA, 0, 4096, BF16, "p (k n) -> p k n", n=256)
        Wv = carve(WA, 4096, 4096, BF16, "p (k n) -> p k n", n=256)
        Wr = carve(WA, 8192, 4096, BF16, "p (k n) -> p k n", n=256)
        Wo = carve(WO, 0, 4096, BF16, "p (f d) -> p f d", d=D)
        statef = [carve(WA, 12288 + i * 1024, 1024, F32) for i in range(2)]
        stateb = [carve(WA, 14336 + i * 512, 512, BF16) for i in range(2)]
        srt = [carve(WA, 15360 + i * 1024, 1024, F32) for i in range(2)]
        osq = [carve(WA, 17408 + i * 512, 512, BF16) for i in range(2)]
        og = [carve(WA, 18944 + i * 512, 512, BF16) for i in range(2)]
        cumv = cum.rearrange("p (n c) -> p n c", c=128)
        e1v = e1.rearrange("p (n c) -> p n c", c=128)
        ptk = lambda i, hh: Ptag[i] + (hh,)
        bothP = lambda i: [ptk(i, 0), ptk(i, 1)]

        def win_dma(dst, c0, n, tok, l=l):
            P.dma("pool", lambda e: e.dma_start(
                out=dst, in_=win_d[l, :, c0:c0 + n].rearrange("(k p) n -> p k n", p=128)), w=[tok])

        def load_wqk(h):
            win_dma(Wqk[:, :, 0:128], 2048 + h * 128, 128, ("W", "wqk"))
            win_dma(Wqk[:, :, 128:256], 2560 + h * 128, 128, ("W", "wqk"))

        def load_wv(h):
            win_dma(Wv, 3072 + h * 256, 256, ("W", "wv"))

        def load_wr(h):
            win_dma(Wr, 4096 + h * 256, 256, ("W", "wr"))

        def load_wo(h, l=l):
            P.dma("pool", lambda e: e.dma_start(
                out=Wo, in_=wout_d[l, 1024 + h * 256:1024 + (h + 1) * 256, :].rearrange("(f p) d -> p f d", p=128)),
                w=[("WO", "wo")])

        def proj_qk_tiles(h):
            tiles = []
            for which in range(2):
                dst = qT if which == 0 else kT
                for tt in range(4):
                    st = {}

                    def mm(which=which, tt=tt, st=st):
                        b = nextbank()
                        st["b"] = b
                        for kc in range(8):
                            P.pe(lambda e, kc=kc, b=b: e.matmul(
                                psb[b][:, :], Wqk[:, kc, which * 128:(which + 1) * 128], hT[:, kc, tt * 512:(tt + 1) * 512],
                                start=(kc == 0), stop=(kc == 7)), r=[("W", "wqk"), ("hT", tt)], w=PB(b))

                    def ev(which=which, tt=tt, st=st, dst=dst):
                        b = st["b"]
                        P.act(lambda e: e.activation(
                            out=dst[:, tt * 512:(tt + 1) * 512], in_=psb[b][:, :], func=AF.Copy,
                            scale=(128.0 ** -0.5 if which == 0 else 1.0)),
                            r=PB(b), w=[("A", "qk", which)])
                    tiles.append((mm, ev))
            return tiles

        def proj_qk(h):
            for mm, ev in proj_qk_tiles(h):
                mm()
                ev()

        def proj_v(h):
            for jj in range(8):
                b = nextbank()
                for sub in range(2):
                    j = jj * 2 + sub
                    for kc in range(8):
                        P.pe(lambda e, j=j, sub=sub, kc=kc, b=b: e.matmul(
                            psb[b][:, sub * 256:(sub + 1) * 256], hT[:, kc, j * 128:(j + 1) * 128], Wv[:, kc, :],
                            start=(kc == 0), stop=(kc == 7)), r=[("W", "wv"), ("hT", j // 4)], w=PB(b))
                P.dve(lambda e, jj=jj, b=b: e.tensor_copy(out=vh[:, 2 * jj:2 * jj + 2, :],
                                                         in_=psb[b][:, :].rearrange("p (a v) -> p a v", v=256)),
                      r=PB(b), w=[("A", "vh")])

        def stage1a_pre(h, dr):
            par = dr
            nbcol = (0 if dr == 0 else 16) + l * 4 + h
            for tt in range(4):
                b = nextbank()
                P.pe(lambda e, tt=tt, b=b: e.matmul(
                    psb[b][:, :], waf[0:16, dr, h * 128:(h + 1) * 128], gT[0:16, dr, tt * 512:(tt + 1) * 512],
                    start=True, stop=True), r=[("B", "waf"), ("B", "gT")], w=PB(b))
                P.act(lambda e, tt=tt, b=b: e.activation(
                    out=e1[:, tt * 512:(tt + 1) * 512], in_=psb[b][:, :], func=AF.Exp, scale=-1.0,
                    bias=vneg[:, nbcol:nbcol + 1]), r=PB(b) + ["vneg"], w=[("A", "e1")])
            P.act(lambda e: e.activation(out=e1[:, :], in_=e1[:, :], func=AF.Ln, bias=1.0),
                  r=[("A", "e1")], w=[("A", "e1")])

        def stage1a_post(h, dr):
            par = dr
            if dr == 0:
                P.dve(lambda e: e.tensor_tensor_scan(out=cum[:, :], data0=e1[:, :], data1=e1[:, :], initial=0.0,
                                                     op0=ALU.add, op1=ALU.max),
                      r=[("A", "e1")], w=[("A", "cum")])
                P.dve(lambda e: e.memset(rpos[:, 0:1], 0.0), w=["rpos"])
                P.dve(lambda e: e.tensor_copy(out=rpos[:, 1:17], in_=cum[:, 127:2048:128]),
                      r=[("A", "cum")], w=["rpos"])
            else:
                P.dve(lambda e: e.memset(cum[:, 0:1], 0.0), w=[("A", "cum")])
                P.dve(lambda e: e.tensor_tensor_scan(out=cum[:, 1:2048], data0=e1[:, 0:2047], data1=e1[:, 0:2047],
                                                     initial=0.0, op0=ALU.add, op1=ALU.max),
                      r=[("A", "e1")], w=[("A", "cum")])
                P.dve(lambda e: e.tensor_copy(out=rpos[:, 0:16], in_=cum[:, 0:2048:128]),
                      r=[("A", "cum")], w=["rpos"])
                P.dve(lambda e: e.tensor_copy(out=rpos[:, 16:17], in_=cum[:, 2047:2048]),
                      r=[("A", "cum")], w=["rpos"])
            P.dve(lambda e: e.tensor_tensor(out=dtmp[:], in0=rpos[:, 1:17], in1=rpos[:, 0:16], op=ALU.subtract),
                  r=["rpos"], w=["dtmp"])
            P.act(lambda e: e.activation(out=dcol2[:, par, :], in_=dtmp[:], func=AF.Exp, scale=-1.0 / 16),
                  r=["dtmp"], w=[("dcol", par)])
            P.dve(lambda e: e.tensor_tensor(out=e1v, in0=cumv, in1=rpos[:, 0:16].unsqueeze(2).to_broadcast([128, 16, 128]),
                                            op=ALU.subtract), r=[("A", "cum"), "rpos"], w=[("A", "e1")])
            P.dve(lambda e: e.tensor_tensor(out=cumv, in0=cumv, in1=rpos[:, 1:17].unsqueeze(2).to_broadcast([128, 16, 128]),
                                            op=ALU.subtract), r=[("A", "cum"), "rpos"], w=[("A", "cum")])
            P.act(lambda e: e.activation(out=cum[:, :], in_=cum[:, :], func=AF.Exp, scale=1.0 / 16),
                  r=[("A", "cum")], w=[("A", "cum")])

        def stage1b(h, dr):
            srcs = [qT, kT, kT] if dr == 0 else [kT, qT, qT]
            srct = [0, 1, 1] if dr == 0 else [1, 0, 0]
            ygtok = [("A", "ygT", tt) for tt in range(4)]
            P.act(lambda e: e.activation(out=ygTf[:, :], in_=e1[:, :], func=AF.Exp, scale=-1.0 / 16),
                  r=[("A", "e1")], w=ygtok)
            P.act(lambda e: e.activation(out=e1[:, :], in_=e1[:, :], func=AF.Exp, scale=1.0 / 16),
                  r=[("A", "e1")], w=[("A", "e1")])
            P.dve(lambda e, src=srcs[2]: e.tensor_tensor(out=Pb[2], in0=cum[:, :], in1=src[:, :], op=ALU.mult),
                  r=[("A", "cum"), ("A", "qk", srct[2])], w=bothP(2))
            P.dve(lambda e, src=srcs[0]: e.tensor_tensor(out=Pb[0], in0=ygTf[:, :], in1=src[:, :], op=ALU.mult),
                  r=ygtok + [("A", "qk", srct[0])], w=bothP(0))
            P.dve(lambda e, src=srcs[1]: e.tensor_tensor(out=Pb[1], in0=e1[:, :], in1=src[:, :], op=ALU.mult),
                  r=[("A", "e1"), ("A", "qk", srct[1])], w=bothP(1))

        def stage2(h, dr, side=None):
            par = dr
            side = list(side) if side else []
            pend = []
            if dr == 0:
                kside, qside, interq, ksrc = 1, 0, 0, 2
            else:
                kside, qside, interq, ksrc = 0, 1, 2, 0
            mask = maskf if dr == 0 else maskb
            for hh in range(2):
                for cc in range(8):
                    n = hh * 8 + cc
                    bnk = 2 if cc < 4 else 6
                    P.pe(lambda e, n=n, cc=cc, bnk=bnk: e.matmul(
                        psb[bnk][:, (cc % 4) * 128:(cc % 4 + 1) * 128], Pb[kside][:, n * 128:(n + 1) * 128],
                        Pb[qside][:, n * 128:(n + 1) * 128], start=True, stop=True),
                        r=[ptk(kside, hh), ptk(qside, hh)], w=PB(bnk))
                for q4 in range(2):
                    bnk = 2 if q4 == 0 else 6
                    c0 = hh * 1024 + q4 * 512
                    P.dve(lambda e, bnk=bnk, c0=c0: e.tensor_tensor(
                        out=Pb[1][:, c0:c0 + 512].rearrange("p (a c) -> p a c", c=128),
                        in0=psb[bnk][:, :].rearrange("p (a c) -> p a c", c=128),
                        in1=mask[:, :].unsqueeze(1).to_broadcast([128, 4, 128]), op=ALU.mult),
                        r=PB(bnk) + ["maskf", "maskb"], w=[ptk(1, hh)])
            for g4 in range(4):
                hh = g4 // 2
                for cc in range(4):
                    n = g4 * 4 + cc
                    P.pe(lambda e, n=n, cc=cc, g4=g4: e.transpose(TRB[g4 % 2][0][:, cc * 128:(cc + 1) * 128],
                                                                Pb[ksrc][:, n * 128:(n + 1) * 128], identb[:, :]),
                         r=[ptk(ksrc, hh), "identb"], w=PB(TRB[g4 % 2][1]))
                P.act(lambda e, g4=g4: e.activation(out=Pb[ksrc][:, g4 * 512:(g4 + 1) * 512],
                                                    in_=TRB[g4 % 2][0][:, 0:512], func=AF.Copy),
                      r=PB(TRB[g4 % 2][1]), w=[ptk(ksrc, hh)])
            order = list(range(16)) if dr == 0 else list(range(15, -1, -1))

            def smm(ci):
                n = order[ci]
                bnk = PSS[ci % 3]
                P.pe(lambda e, n=n, bnk=bnk: e.matmul(psb[bnk][:, 0:256], Pb[ksrc][:, n * 128:(n + 1) * 128], vh[:, n, :],
                                                     start=True, stop=True),
                     r=[ptk(ksrc, n // 8), ("A", "vh")], w=PB(bnk))

            smm(0)
            smm(1)
            sidx = 0
            for ci, n in enumerate(order):
                first = (ci == 0)
                last = (ci == 15)
                sl = ci % 2
                hh = n // 8
                if ci + 2 <= 14:
                    smm(ci + 2)
                if pend:
                    pend.pop(0)()
                if side and ci % 4 == 0:
                    mm_, ev_ = side.pop(0)
                    mm_()
                    pend.append(ev_)
                P.pe(lambda e, n=n, first=first, sl=sl: e.matmul(psb[PSO[sl]][:, 0:256], Pb[1][:, n * 128:(n + 1) * 128],
                                                                vh[:, n, :], start=True, stop=first),
                     r=[ptk(1, hh), ("A", "vh")], w=PB(PSO[sl]))
                if not first:
                    P.pe(lambda e, n=n, sl=sl, sidx=sidx: e.matmul(
                        psb[PSO[sl]][:, 0:256], Pb[interq][:, n * 128:(n + 1) * 128], stateb[sidx],
                        start=False, stop=True),
                        r=[ptk(interq, hh), ("W", "stateb", sidx)], w=PB(PSO[sl]))
                if not last:
                    bnk = PSS[ci % 3]
                    nidx = 1 - sidx
                    if first:
                        P.dve(lambda e, nidx=nidx, bnk=bnk: e.tensor_copy(out=statef[nidx], in_=psb[bnk][:, 0:256]),
                              r=PB(bnk), w=[("W", "statef", nidx)])
                    else:
                        P.dve(lambda e, nidx=nidx, sidx=sidx, n=n, bnk=bnk: e.scalar_tensor_tensor(
                            out=statef[nidx], in0=statef[sidx], scalar=dcol2[:, par, n:n + 1],
                            in1=psb[bnk][:, 0:256], op0=ALU.mult, op1=ALU.add),
                            r=PB(bnk) + [("W", "statef", sidx), ("dcol", par)], w=[("W", "statef", nidx)])
                    P.act(lambda e, nidx=nidx: e.activation(out=stateb[nidx], in_=statef[nidx], func=AF.Copy),
                          r=[("W", "statef", nidx)], w=[("W", "stateb", nidx)])
                    sidx = nidx
                if dr == 0:
                    P.act(lambda e, n=n, sl=sl: e.activation(out=ofwd[:, n, :], in_=psb[PSO[sl]][:, 0:256], func=AF.Copy),
                          r=PB(PSO[sl]), w=[("A", "ofwd", n)])
                else:
                    P.dve(lambda e, n=n, sl=sl: e.tensor_tensor(out=ofwd[:, n, :], in0=psb[PSO[sl]][:, 0:256],
                                                               in1=ofwd[:, n, :], op=ALU.add),
                          r=PB(PSO[sl]) + [("A", "ofwd", n)], w=[("A", "ofwd", n)])
            flush_side(side, pend)

        def flush_side(side, pend):
            for ev_ in pend:
                ev_()
            for mm_, ev_ in side:
                mm_()
                ev_()

        def finalize_pre1(h):
            for hh in range(2):
                P.act(lambda e, hh=hh: e.activation(
                    out=Pb[hh], in_=ofwd[:, hh * 8:(hh + 1) * 8, :].rearrange("p j v -> p (j v)"), func=AF.Square),
                    r=[("A", "ofwd", n) for n in range(hh * 8, hh * 8 + 8)], w=bothP(hh))

        def finalize_pre2(h):
            for hh in range(2):
                P.dve(lambda e, hh=hh: e.tensor_reduce(out=ssq[:, hh * 8:(hh + 1) * 8],
                                                      in_=Pb[hh].rearrange("p (j v) -> p j v", v=256), axis=AX.X, op=ALU.add),
                      r=bothP(hh), w=["ssq"])
            P.act(lambda e: e.activation(out=ssr[:, :], in_=ssq[:, :], func=AF.Sqrt, scale=1.0 / 256, bias=EPS),
                  r=["ssq"], w=["ssr"])
            P.dve(lambda e: e.reciprocal(out=ssr2[:, :], in_=ssr[:, :]), r=["ssr"], w=["ssr2"])

        def finalize(h):
            gc = 32 + l * 8 + h * 2

            def rproj(n):
                sl = n % 2
                for kc in range(8):
                    P.pe(lambda e, n=n, kc=kc, sl=sl: e.matmul(psb[PSR[sl]][:, 0:256], hT[:, kc, n * 128:(n + 1) * 128],
                                                              Wr[:, kc, :], start=(kc == 0), stop=(kc == 7)),
                         r=[("W", "wr"), ("hT", n // 4)], w=PB(PSR[sl]))

            def gate(n):
                sl = n % 2
                P.act(lambda e, sl=sl: e.activation(out=srt[sl], in_=psb[PSR[sl]][:, 0:256], func=AF.Silu),
                      r=PB(PSR[sl]), w=[("W", "srt", sl)])
                P.dve(lambda e, n=n, sl=sl: e.scalar_tensor_tensor(out=og[sl], in0=ofwd[:, n, :], scalar=ssr2[:, n:n + 1],
                                                                  in1=srt[sl], op0=ALU.mult, op1=ALU.mult),
                      r=[("A", "ofwd", n), "ssr2", ("W", "srt", sl)], w=[("W", "og", sl)])

            def trans(n):
                sl = n % 2
                for f2 in range(2):
                    P.pe(lambda e, f2=f2, sl=sl: e.transpose(TRB[sl][0][:, f2 * 128:(f2 + 1) * 128],
                                                            og[sl][:, f2 * 128:(f2 + 1) * 128], identb[:, :]),
                         r=[("W", "og", sl), "identb"], w=PB(TRB[sl][1]))

            def evac(n):
                sl = n % 2
                for f2 in range(2):
                    P.act(lambda e, f2=f2, n=n, sl=sl: e.activation(
                        out=ygT[:, f2, n * 128:(n + 1) * 128], in_=TRB[sl][0][:, f2 * 128:(f2 + 1) * 128],
                        func=AF.Identity, scale=vcols[:, gc + f2:gc + f2 + 1]),
                        r=PB(TRB[sl][1]) + ["vcols"], w=[("A", "ygT", n // 4)])

            rproj(0)
            rproj(1)
            gate(0)
            for n in range(16):
                trans(n)
                if n + 2 < 16:
                    rproj(n + 2)
                if n + 1 < 16:
                    gate(n + 1)
                evac(n)

        def outproj(h):
            for dt in range(8):
                for tt in range(4):
                    b = nextobank([0, 1, 3, 4])
                    for f2 in range(2):
                        P.pe(lambda e, dt=dt, tt=tt, f2=f2, b=b: e.matmul(
                            psb[b][:, :], Wo[:, f2, dt * 128:(dt + 1) * 128], ygT[:, f2, tt * 512:(tt + 1) * 512],
                            start=(f2 == 0), stop=(f2 == 1)), r=[("WO", "wo"), ("A", "ygT", tt)], w=PB(b))
                    residual(dt, tt, b, mT, mtok)

        P.dma("pool", lambda e, l=l: e.dma_start(out=Wg, in_=win_d[l, :, 5120:5152].rearrange("(k p) n -> p k n", p=128)),
              w=[("B", "Wg")])
        P.dma("pool", lambda e, l=l: e.dma_start(out=waf[:, 0, :], in_=waf_d[l]), w=[("B", "waf")])
        P.dma("pool", lambda e, l=l: e.dma_start(out=waf[:, 1, :], in_=wab_d[l]), w=[("B", "waf")])
        load_wqk(0)
        load_wv(0)
        load_wr(0)
        load_wo(0)
        for which in range(2):
            for tt in range(4):
                b = nextbank()
                for kc in range(8):
                    P.pe(lambda e, which=which, tt=tt, kc=kc, b=b: e.matmul(
                        psb[b][0:16, :], Wg[:, kc, which * 16:(which + 1) * 16], hT[:, kc, tt * 512:(tt + 1) * 512],
                        start=(kc == 0), stop=(kc == 7)), r=[("B", "Wg"), ("hT", tt)], w=PB(b))
                P.act(lambda e, which=which, tt=tt, b=b: e.activation(out=gT[:, which, tt * 512:(tt + 1) * 512],
                                                                     in_=psb[b][0:16, :], func=AF.Copy),
                      r=PB(b), w=[("B", "gT")])
        stage1a_pre(0, 0)
        stage1a_post(0, 0)
        proj_qk(0)
        load_wqk(1)
        proj_v(0)
        load_wv(1)
        for h in range(4):
            stage1b(h, 0)
            stage1a_pre(h, 1)
            stage2(h, 0)
            stage1a_post(h, 1)
            stage1b(h, 1)
            if h < 3:
                stage1a_pre(h + 1, 0)
            stage2(h, 1, side=(proj_qk_tiles(h + 1) if h < 3 else None))
            if h < 2:
                load_wqk(h + 2)
            finalize_pre1(h)
            if h < 3:
                stage1a_post(h + 1, 0)
            finalize_pre2(h)
            if h < 3:
                proj_v(h + 1)
                if h < 2:
                    load_wv(h + 2)
            finalize(h)
            if h < 3:
                load_wr(h + 1)
            outproj(h)
            if h < 3:
                load_wo(h + 1)

        P.arena_reset("A")
        P.arena_reset("W")
        P.arena_reset("B")
        P.arena_reset("WO")
        MM = carve(BB, 0, 8192, BF16, "p (w g c d) -> p w g c d", w=2, g=4, c=2)
        wfm = carve(BB, 8192, 4096, BF16, "p (g c d) -> p g c d", g=4, c=2)
        P.dma("pool", lambda e, l=l: e.dma_start(out=wfm, in_=wfm_d[l].rearrange("g (c p) d -> p g c d", p=128)),
              w=[("B", "wfm")])
        for g in range(4):
            for cto in range(2):
                b = nextbank()
                for which in range(2):
                    for ctp in range(2):
                        P.pe(lambda e, g=g, cto=cto, which=which, ctp=ctp, b=b: e.matmul(
                            psb[b][:, which * 256:(which + 1) * 256], cs256[:, ctp, which, cto * 128:(cto + 1) * 128],
                            wfm[:, g, ctp, :], start=(ctp == 0), stop=(ctp == 1)),
                            r=["cs256", ("B", "wfm")], w=PB(b))
                P.act(lambda e, g=g, cto=cto, b=b: e.activation(
                    out=MM[:, :, g, cto, :], in_=psb[b][:, :].rearrange("p (w d) -> p w d", d=256), func=AF.Copy),
                    r=PB(b), w=[("B", "MM")])
        UT = carve(AR, 0, 4104, BF16)
        Up = carve(AR, 4104, 4096, BF16, "p (c t) -> p c t", t=1024)
        Um = carve(AR, 8200, 4096, BF16, "p (c t) -> p c t", t=1024)
        AB = carve(AR, 12296, 16384, BF16, "p (j w c) -> p j w c", w=2, c=512)
        zy = carve(AR, 28680, 16384, BF16, "p (c t) -> p c t", t=L)
        NR = 3
        ring = [carve(AR, 45064 + i * 2048, 2048, BF16, "p (w k) -> p w k", k=512) for i in range(NR)]
        P.pool(lambda e: e.memset(UT[:, 2048:2049], 0.0), w=[("A", "UTpad")])
        rcnt = 0
        for hf in range(2):
            Wu = carve(WA, 0, 8192, BF16, "p (k n) -> p k n", n=512)
            Wz = carve(WA, 8192, 8192, BF16, "p (k n) -> p k n", n=512)
            P.dma("pool", lambda e, hf=hf, l=l, Wu=Wu: e.dma_start(
                out=Wu, in_=win_d[l, :, hf * 512:(hf + 1) * 512].rearrange("(k p) n -> p k n", p=128)), w=[("W", "wu")])
            P.dma("pool", lambda e, hf=hf, l=l, Wz=Wz: e.dma_start(
                out=Wz, in_=win_d[l, :, 1024 + hf * 512:1024 + (hf + 1) * 512].rearrange("(k p) n -> p k n", p=128)),
                w=[("W", "wz")])
            Wo2 = carve(WO, 0, 8192, BF16, "p (f d) -> p f d", d=D)
            P.dma("pool", lambda e, hf=hf, l=l, Wo2=Wo2: e.dma_start(
                out=Wo2, in_=wout_d[l, hf * 512:(hf + 1) * 512, :].rearrange("(f p) d -> p f d", p=128)),
                w=[("WO", "wo2")])
            def zproj(c4):
                for tt in range(4):
                    b = nextbank()
                    for kc in range(8):
                        P.pe(lambda e, c4=c4, tt=tt, kc=kc, b=b: e.matmul(
                            psb[b][:, :], Wz[:, kc, c4 * 128:(c4 + 1) * 128], hT[:, kc, tt * 512:(tt + 1) * 512],
                            start=(kc == 0), stop=(kc == 7)), r=[("W", "wz"), ("hT", tt)], w=PB(b))
                    P.act(lambda e, c4=c4, tt=tt, b=b: e.activation(out=zy[:, c4, tt * 512:(tt + 1) * 512],
                                                                   in_=psb[b][:, :], func=AF.Silu),
                          r=PB(b), w=[("A", "zy", c4, tt)])

            for gl in range(2):
                g = hf * 2 + gl
                for ct in range(2):
                    c4 = gl * 2 + ct
                    for tt in range(4):
                        b = nextbank()
                        for kc in range(8):
                            P.pe(lambda e, c4=c4, tt=tt, kc=kc, b=b, Wu=Wu: e.matmul(
                                psb[b][:, :], Wu[:, kc, c4 * 128:(c4 + 1) * 128], hT[:, kc, tt * 512:(tt + 1) * 512],
                                start=(kc == 0), stop=(kc == 7)), r=[("W", "wu"), ("hT", tt)], w=PB(b))
                        P.act(lambda e, tt=tt, b=b: e.activation(out=UT[:, tt * 512:(tt + 1) * 512], in_=psb[b][:, :],
                                                                 func=AF.Copy), r=PB(b), w=[("A", "UT", tt)])
                    zproj(c4)
                    UTr = [("A", "UT", tt) for tt in range(4)] + [("A", "UTpad")]
                    P.dve(lambda e, ct=ct: e.tensor_tensor(out=Up[:, ct, :], in0=UT[:, 0:1024], in1=UT[:, 2048:1024:-1],
                                                          op=ALU.add), r=UTr, w=[("A", "Up", ct)])
                    P.pool(lambda e, ct=ct: e.tensor_tensor(out=Um[:, ct, :], in0=UT[:, 0:1024], in1=UT[:, 2048:1024:-1],
                                                           op=ALU.subtract), r=UTr, w=[("A", "Um", ct)])
                    P.dve(lambda e, ct=ct: e.tensor_copy(out=u1024[:, ct:ct + 1], in_=UT[:, 1024:1025]),
                          r=UTr, w=[("u1024", ct)])
                for j in range(8):
                    b = nextbank()
                    for which in range(2):
                        src = Up if which == 0 else Um
                        for ct in range(2):
                            P.pe(lambda e, j=j, which=which, ct=ct, b=b, src=src, g=g: e.matmul(
                                psb[b][:, which * 256:(which + 1) * 256], src[:, ct, j * 128:(j + 1) * 128],
                                MM[:, which, g, ct, :], start=(ct == 0), stop=(ct == 1)),
                                r=[("A", "Up", ct), ("A", "Um", ct), ("B", "MM")], w=PB(b))
                    P.act(lambda e, j=j, b=b, gl=gl: e.activation(
                        out=AB[:, j, :, gl * 256:(gl + 1) * 256], in_=psb[b][:, :].rearrange("p (w d) -> p w d", d=256),
                        func=AF.Copy), r=PB(b), w=[("A", "AB", j)])
                for ct in range(2):
                    P.pe(lambda e, ct=ct, g=g: e.matmul(psb[2][0:1, 0:256], u1024[:, ct:ct + 1], MM[:, 0, g, ct, :],
                                                       start=(ct == 0), stop=(ct == 1)),
                         r=[("u1024", ct), ("B", "MM")], w=PB(2))
                P.act(lambda e, gl=gl: e.activation(out=a1024[0:1, gl * 256:(gl + 1) * 256], in_=psb[2][0:1, 0:256],
                                                    func=AF.Copy), r=PB(2), w=["a1024"])
            for kt in range(4):
                hci = hf * 4 + kt
                hoist = (l + 1 < depth) and hci < 6
                if hoist:
                    WaH = carve(WA, 16384, 8192, BF16, "p (k n) -> p k n", n=512)
                    mod_chunk_dma(l + 1, hci, WaH, ("W", "waH"))
                for j in range(8):
                    rs = rcnt % NR
                    rcnt += 1
                    P.dma("sp", lambda e, j=j, kt=kt, rs=rs: e.dma_start(out=ring[rs], in_=tab_d[j, :, kt]),
                          w=[("A", "ring", rs)])
                    for c4 in range(4):
                        for which in range(2):
                            P.pe(lambda e, j=j, c4=c4, which=which, rs=rs: e.matmul(
                                psb[3 + c4][:, :], AB[:, j, which, c4 * 128:(c4 + 1) * 128], ring[rs][:, which, :],
                                start=(j == 0 and which == 0), stop=False),
                                r=[("A", "AB", j), ("A", "ring", rs)], w=PB(3 + c4))
                for c4 in range(4):
                    P.pe(lambda e, c4=c4, kt=kt: e.matmul(psb[3 + c4][:, :], a1024[0:1, c4 * 128:(c4 + 1) * 128],
                                                         alt[0:1, kt * 512:(kt + 1) * 512], start=False, stop=True),
                         r=["a1024", "alt"], w=PB(3 + c4))
                    P.dve(lambda e, c4=c4, kt=kt: e.tensor_tensor(
                        out=zy[:, c4, kt * 512:(kt + 1) * 512], in0=psb[3 + c4][:, :], in1=zy[:, c4, kt * 512:(kt + 1) * 512],
                        op=ALU.mult), r=PB(3 + c4) + [("A", "zy", c4, kt)], w=[("A", "zy", c4, kt)])
                if hoist:
                    mod_chunk_mm(l + 1, hci, WaH, ("W", "waH"))
            if hf == 1 and l + 1 < depth:
                mod_finish(l + 1)
            for dt in range(8):
                for tt in range(4):
                    b = nextobank([0, 1, 3, 4])
                    for c4 in range(4):
                        P.pe(lambda e, dt=dt, tt=tt, c4=c4, b=b, Wo2=Wo2: e.matmul(
                            psb[b][:, :], Wo2[:, c4, dt * 128:(dt + 1) * 128], zy[:, c4, tt * 512:(tt + 1) * 512],
                            start=(c4 == 0), stop=(c4 == 3)), r=[("WO", "wo2"), ("A", "zy", c4, tt)], w=PB(b))
                    residual(dt, tt, b, mT, mtok)

    P.arena_reset("A")
    onT = carve(AR, 0, 16384, F32, "p (k t) -> p k t", t=512)
    ostage = [carve(AR, 16384 + i * 4096, 4096, F32) for i in range(4)]
    xr2 = [carve(AR, 32768 + i * 2048, 2048, F32) for i in range(2)]
    finals = []
    rs_ = [None] * 4
    rs_[0] = rms_sums(0, 0, True)
    rs_[1] = rms_sums(1, 1, True)
    for tt in range(4):
        rstd = rs_[tt]
        for kc in range(8):
            s = kc % 2
            P.dve(lambda e, kc=kc, tt=tt, s=s, rstd=rstd: e.tensor_tensor(
                out=xr2[s], in0=xT[:, kc, tt * 512:(tt + 1) * 512], in1=rstd, op=ALU.mult),
                r=[("xT", kc, tt), ("A", "rstd", tt % 2)], w=[("A", "xr2", s)])
            P.act(lambda e, kc=kc, s=s: e.activation(out=onT[:, kc, :], in_=xr2[s], func=AF.Identity,
                                                     scale=vcols[:, 96 + kc:97 + kc]),
                  r=[("A", "xr2", s), "vcols"], w=[("A", "onT", kc)])
        if tt + 2 < 4:
            rs_[tt + 2] = rms_sums(tt + 2, tt % 2)
        for jj in range(4):
            j = tt * 4 + jj
            s = j % 4
            for g in range(2):
                b = nextbank()
                for q in range(4):
                    kc = g * 4 + q
                    P.pe(lambda e, b=b, q=q, kc=kc, jj=jj: e.transpose(psb[b][:, q * 128:(q + 1) * 128],
                                                                     onT[:, kc, jj * 128:(jj + 1) * 128], ident[:, :]),
                         r=[("A", "onT", kc), "ident"], w=PB(b))
                if g == 0:
                    P.act(lambda e, b=b, s=s: e.activation(out=ostage[s][:, 0:512], in_=psb[b][:, :], func=AF.Copy),
                          r=PB(b), w=[("A", "ost", s, 0)])
                else:
                    P.dve(lambda e, b=b, s=s: e.tensor_copy(out=ostage[s][:, 512:1024], in_=psb[b][:, :]),
                          r=PB(b), w=[("A", "ost", s, 1)])
            finals.append(P.dma("sp", lambda e, j=j, s=s: e.dma_start(out=y_d[j * 128:(j + 1) * 128, :], in_=ostage[s]),
                                r=[("A", "ost", s, 0), ("A", "ost", s, 1)]))
    P.emit(final_waits=finals)
    es.close()
    return nc


_NC_CACHE = {}


def _prep_inputs(x, c, norm_g, w_ada, b_ada, w_in, w_fmap, w_af, b_af, w_ab, b_ab, gla_norm_g, w_out, final_g):
    f = lambda a: np.ascontiguousarray(np.asarray(a, dtype=np.float32))
    cst = _consts()
    shared = {
        "b_ada": f(b_ada).reshape(96, 128),
        "w_ada": f(w_ada), "w_in": f(w_in), "w_fmap": f(w_fmap), "w_af": f(w_af), "w_ab": f(w_ab), "w_out": f(w_out),
    }
    shared.update(cst)
    x = f(x)
    c = f(c)
    in_maps = []
    for b in range(8):
        vrows = np.concatenate([f(norm_g).reshape(32, 128), f(gla_norm_g).reshape(32, 128), f(b_af).reshape(16, 128),
                                f(b_ab).reshape(16, 128), f(final_g).reshape(8, 128), c[b].reshape(8, 128)], axis=0)
        m = dict(shared)
        m["x"] = x[b]
        m["vrows"] = np.ascontiguousarray(vrows)
        in_maps.append(m)
    return in_maps


def kernel(x, c, norm_g, w_ada, b_ada, w_in, w_fmap, w_af, b_af, w_ab, b_ab, gla_norm_g, w_out, final_g):
    in_maps = _prep_inputs(x, c, norm_g, w_ada, b_ada, w_in, w_fmap, w_af, b_af, w_ab, b_ab, gla_norm_g, w_out, final_g)
    if "nc" not in _NC_CACHE:
        _NC_CACHE["nc"] = build_nc(DEPTH)
    nc = _NC_CACHE["nc"]
    res = run_bass_kernel_spmd(nc, in_maps, core_ids=list(range(8)))
    out = np.stack([np.asarray(r["y"], dtype=np.float32) for r in res.results], axis=0)
    return out
```

```python
import contextlib
import numpy as np
import ml_dtypes
import concourse.bass as bass
import concourse.mybir as mybir
from concourse.bass_utils import run_bass_kernel_spmd

F32 = mybir.dt.float32
BF16 = mybir.dt.bfloat16
AF = mybir.ActivationFunctionType
ALU = mybir.AluOpType
AX = mybir.AxisListType

L = 2048
D = 1024
DEPTH = 4
DIN = 5152
EPS = 1e-6
ENGS = ["pe", "act", "dve", "pool", "sp"]
UNIQUE_POOL_SEMS = False


class Op:
    __slots__ = ("fn", "deps", "dma", "sig", "cnt")

    def __init__(self, fn, deps, dma):
        self.fn = fn
        self.deps = deps
        self.dma = dma
        self.sig = False
        self.cnt = 0


class Prog:
    def __init__(self, nc, n_chan=24):
        self.nc = nc
        self.ops = {e: [] for e in ENGS}
        self.tok = {}
        self.n_chan = n_chan
        self.chan_cnt = [0] * n_chan
        self.rr = 0
        self.rr_pool = 0
        self.arena_base = {}

    def _st(self, t):
        st = self.tok.get(t)
        if st is None:
            base = []
            if isinstance(t, tuple) and t[0] in self.arena_base:
                base = list(self.arena_base[t[0]])
            st = {"w": None, "r": base}
            self.tok[t] = st
        return st

    def arena_reset(self, tag):
        ev = set(self.arena_base.get(tag, []))
        for t in list(self.tok.keys()):
            if isinstance(t, tuple) and t[0] == tag:
                st = self.tok.pop(t)
                if st["w"] is not None:
                    ev.add(st["w"])
                ev.update(st["r"])
        best = {}
        for e in ev:
            key = (e[0], e[1])
            if key not in best or e[2] > best[key][2]:
                best[key] = e
        self.arena_base[tag] = list(best.values())

    def add(self, eng, fn, reads=(), writes=(), dma=False, chan=None):
        idx = len(self.ops[eng])
        deps = set()
        d = None
        if dma:
            if chan is None and eng == "pool" and UNIQUE_POOL_SEMS:
                chan = len(self.chan_cnt)
                self.chan_cnt.append(0)
            elif chan is None:
                half = self.n_chan // 2
                if eng == "pool":
                    chan = half + self.rr_pool
                    self.rr_pool = (self.rr_pool + 1) % (self.n_chan - half)
                else:
                    chan = self.rr
                    self.rr = (self.rr + 1) % half
            k = self.chan_cnt[chan]
            self.chan_cnt[chan] += 1
            d = (chan, k)
            if k > 0:
                deps.add(("dma", chan, k - 1))
            me = ("dma", chan, k)
        else:
            me = ("eng", eng, idx)
        for t in reads:
            st = self._st(t)
            if st["w"] is not None:
                deps.add(st["w"])
        for t in writes:
            st = self._st(t)
            if st["w"] is not None:
                deps.add(st["w"])
            deps.update(st["r"])
        for t in reads:
            self.tok[t]["r"].append(me)
        for t in writes:
            st = self.tok[t]
            st["w"] = me
            st["r"] = []
        deps.discard(me)
        self.ops[eng].append(Op(fn, deps, d))
        return me

    def pe(self, fn, r=(), w=()):
        return self.add("pe", fn, r, w)

    def act(self, fn, r=(), w=()):
        return self.add("act", fn, r, w)

    def dve(self, fn, r=(), w=()):
        return self.add("dve", fn, r, w)

    def pool(self, fn, r=(), w=()):
        return self.add("pool", fn, r, w)

    def dma(self, q, fn, r=(), w=(), chan=None):
        return self.add(q, fn, r, w, dma=True, chan=chan)

    def _skip(self, e, i, dep):
        _, de, di = dep
        if de != e:
            return False
        if e == "pe":
            return True
        return (i - di) > 3

    def emit(self, final_waits=()):
        nc = self.nc
        for e in ENGS:
            for i, op in enumerate(self.ops[e]):
                for dep in op.deps:
                    if dep[0] == "eng" and not self._skip(e, i, dep):
                        self.ops[dep[1]][dep[2]].sig = True
        for e in ENGS:
            c = 0
            for op in self.ops[e]:
                if op.sig and op.dma is None:
                    c += 1
                op.cnt = c
        stack = contextlib.ExitStack()
        sems = {e: stack.enter_context(nc.semaphore("s_" + e)) for e in ENGS}
        dsems = [stack.enter_context(nc.semaphore("d_%d" % i)) for i in range(len(self.chan_cnt))]
        block = stack.enter_context(nc.Block())

        def run_engine(e, eng):
            seen = {}
            for i, op in enumerate(self.ops[e]):
                need = {}
                for dep in op.deps:
                    if dep[0] == "eng":
                        if self._skip(e, i, dep):
                            continue
                        key = ("e", dep[1])
                        val = self.ops[dep[1]][dep[2]].cnt
                    else:
                        key = ("d", dep[1])
                        val = 16 * (dep[2] + 1)
                    if val > need.get(key, 0):
                        need[key] = val
                for key, val in need.items():
                    if seen.get(key, 0) >= val:
                        continue
                    seen[key] = val
                    s = sems[key[1]] if key[0] == "e" else dsems[key[1]]
                    eng.wait_ge(s, val)
                ins = op.fn(eng)
                if op.dma is not None:
                    ins.then_inc(dsems[op.dma[0]], 16)
                elif op.sig:
                    ins.then_inc(sems[e], 1)
            if e == "sp":
                for ev in final_waits:
                    eng.wait_ge(dsems[ev[1]], 16 * (ev[2] + 1))

        @block.tensor
        def _(eng):
            run_engine("pe", eng)

        @block.scalar
        def _(eng):
            run_engine("act", eng)

        @block.vector
        def _(eng):
            run_engine("dve", eng)

        @block.gpsimd
        def _(eng):
            run_engine("pool", eng)

        @block.sync
        def _(eng):
            run_engine("sp", eng)

        stack.close()


_CONSTS = None


def _consts():
    global _CONSTS
    if _CONSTS is not None:
        return _CONSTS
    bf = ml_dtypes.bfloat16
    l = np.arange(1024, dtype=np.int64)[:, None]
    k = np.arange(2048, dtype=np.int64)[None, :]
    ang = 2.0 * np.pi * ((l * k) % 2048).astype(np.float64) / 2048.0
    sc = 1.0 / np.sqrt(2048.0)
    cosT = np.cos(ang) * sc
    nsinT = -np.sin(ang) * sc
    tab = np.stack([cosT, nsinT], axis=1)
    tab = tab.reshape(8, 128, 2, 4, 512).transpose(0, 1, 3, 2, 4)
    tab = np.ascontiguousarray(tab).astype(bf)
    alt = (((-1.0) ** np.arange(2048)) * sc).reshape(1, 2048).astype(bf)
    c = np.arange(256, dtype=np.int64)
    a2 = 2.0 * np.pi * ((c[:, None] * c[None, :]) % 256).astype(np.float64) / 256.0
    cs = np.stack([np.cos(a2) / 16.0, np.sin(a2) / 16.0], axis=1)
    cs = cs.reshape(2, 128, 2, 256).transpose(1, 0, 2, 3)
    cs256 = np.ascontiguousarray(cs).astype(bf)
    ident = np.eye(128, dtype=np.float32)
    identb = np.eye(128).astype(bf)
    s = np.arange(128)[:, None]
    t = np.arange(128)[None, :]
    maskf = (t >= s).astype(np.float32).astype(bf)
    maskb = (s > t).astype(np.float32).astype(bf)
    onesb = np.ones((128, 128), dtype=np.float32).astype(bf)
    one11 = np.ones((1, 1), dtype=np.float32)
    _CONSTS = dict(tab=tab, alt=alt, cs256=cs256, ident=ident, identb=identb, maskf=maskf,
                   maskb=maskb, onesb=onesb, one11=one11)
    return _CONSTS


def build_nc(depth=DEPTH):
    nc = bass.Bass("TRN2", target_bir_lowering=False)

    def din(name, shape, dt=F32):
        return nc.dram_tensor(name, list(shape), dt, kind="ExternalInput").ap()

    x_d = din("x", [L, D])
    vrows_d = din("vrows", [112, 128])
    bada_d = din("b_ada", [96, 128])
    wada_d = din("w_ada", [DEPTH, D, 3072])
    win_d = din("w_in", [DEPTH, D, DIN])
    wfm_d = din("w_fmap", [DEPTH, 4, 256, 256])
    waf_d = din("w_af", [DEPTH, 16, 512])
    wab_d = din("w_ab", [DEPTH, 16, 512])
    wout_d = din("w_out", [DEPTH, 2048, D])
    tab_d = din("tab", [8, 128, 4, 2, 512], BF16)
    alt_d = din("alt", [1, 2048], BF16)
    cs_d = din("cs256", [128, 2, 2, 256], BF16)
    ident_d = din("ident", [128, 128])
    identb_d = din("identb", [128, 128], BF16)
    maskf_d = din("maskf", [128, 128], BF16)
    maskb_d = din("maskb", [128, 128], BF16)
    onesb_d = din("onesb", [128, 128], BF16)
    one11_d = din("one11", [1, 1])
    y_d = nc.dram_tensor("y", [L, D], F32, kind="ExternalOutput").ap()

    es = contextlib.ExitStack()

    def sb(name, shape, dt):
        return es.enter_context(nc.sbuf_tensor("sb_" + name, list(shape), dt))

    xT = sb("xT", [128, 8, L], F32)
    hT = sb("hT", [128, 8, L], BF16)
    AR = sb("AR", [128, 13312], F32)
    WA = sb("WA", [128, 6144], F32)
    WO = sb("WO", [128, 2048], F32)
    BB = sb("BB", [128, 3072], F32)
    cs256 = sb("cs256", [128, 2, 2, 256], BF16)
    ident = sb("ident", [128, 128], F32)
    identb = sb("identb", [128, 128], BF16)
    maskf = sb("maskf", [128, 128], BF16)
    maskb = sb("maskb", [128, 128], BF16)
    onesb = sb("onesb", [128, 128], BF16)
    one11 = sb("one11", [1, 1], F32)
    alt = sb("alt", [1, 2048], BF16)
    vrows = sb("vrows", [112, 128], F32)
    vcols = sb("vcols", [128, 112], F32)
    vneg = sb("vneg", [128, 32], F32)
    cact = sb("cact", [128, 8], BF16)
    vrows2 = sb("vrows2", [96, 128], F32)
    vcols2 = sb("vcols2", [128, 96], F32)
    modT2 = sb("modT2", [128, 2, 24], F32)
    Acol2 = sb("Acol2", [128, 2, 8], F32)
    a1tmp = sb("a1tmp", [128, 8], F32)
    rpos = sb("rpos", [128, 17], F32)
    rneg = sb("rneg", [128, 17], F32)
    dtmp = sb("dtmp", [128, 16], F32)
    dcol2 = sb("dcol2", [128, 2, 16], F32)
    ssr2 = sb("ssr2", [128, 16], F32)
    ssq = sb("ssq", [128, 16], F32)
    ssr = sb("ssr", [128, 16], F32)
    a1024 = sb("a1024", [1, 512], BF16)
    u1024 = sb("u1024", [128, 2], BF16)

    psb = [es.enter_context(nc.psum_tensor("ps%d" % i, [128, 512], F32)) for i in range(7)]
    pst = es.enter_context(nc.psum_tensor("pst", [128, 1024], BF16))

    P = Prog(nc)

    def PB(b):
        return [("ps", b)]

    PSS = [4, 6, 5]
    PSO = [3, 2]
    PSR = [5, 4]
    ps6b = psb[6][:, :].bitcast(BF16)
    TRB = [(pst, 7), (ps6b, 6)]

    def carve(base, off_b, nbytes, dt, pattern=None, **kw):
        v = base[:, off_b // 4:(off_b + nbytes) // 4]
        if dt != F32:
            v = v.bitcast(dt)
        if pattern is not None:
            v = v.rearrange(pattern, **kw)
        return v

    K = 1024
    for (dst, src, tokn) in [(ident, ident_d, "ident"), (identb, identb_d, "identb"), (maskf, maskf_d, "maskf"),
                             (maskb, maskb_d, "maskb"), (onesb, onesb_d, "onesb"), (one11, one11_d, "one11"),
                             (alt, alt_d, "alt"), (vrows, vrows_d, "vrows"), (vrows2, bada_d, "vrows2")]:
        P.dma("sp", lambda e, dst=dst, src=src: e.dma_start(out=dst[:], in_=src), w=[tokn])
    P.dma("sp", lambda e: e.dma_start(out=cs256[:], in_=cs_d), w=["cs256"])
    P.pe(lambda e: e.transpose(psb[2][:, 0:112], vrows[:, :], ident[0:112, 0:112]), r=["vrows", "ident"], w=PB(2))
    P.dve(lambda e: e.tensor_copy(out=vcols[:], in_=psb[2][:, 0:112]), r=PB(2), w=["vcols"])
    P.pe(lambda e: e.transpose(psb[2][:, 0:96], vrows2[:, :], ident[0:96, 0:96]), r=["vrows2", "ident"], w=PB(2))
    P.dve(lambda e: e.tensor_copy(out=vcols2[:], in_=psb[2][:, 0:96]), r=PB(2), w=["vcols2"])
    P.dve(lambda e: e.tensor_scalar(out=vneg[:], in0=vcols[:, 64:96], scalar1=-1.0, scalar2=None, op0=ALU.mult),
          r=["vcols"], w=["vneg"])
    P.act(lambda e: e.activation(out=cact[:], in_=vcols[:, 104:112], func=AF.Silu), r=["vcols"], w=["cact"])

    tmpR = sb("tmpR", [128, 2, 512], F32)
    rescnt = [0]

    def residual(dt, tt, b, mT, mtok):
        i = rescnt[0]
        rescnt[0] += 1
        if True:
            P.dve(lambda e: e.scalar_tensor_tensor(
                out=xT[:, dt, tt * 512:(tt + 1) * 512], in0=psb[b][:, :], scalar=mT[:, 16 + dt:17 + dt],
                in1=xT[:, dt, tt * 512:(tt + 1) * 512], op0=ALU.mult, op1=ALU.add),
                r=PB(b) + [mtok, ("xT", dt, tt)], w=[("xT", dt, tt)])
        else:
            sl = (i // 2) % 2
            P.act(lambda e: e.activation(out=tmpR[:, sl, :], in_=psb[b][:, :], func=AF.Identity,
                                         scale=mT[:, 16 + dt:17 + dt]),
                  r=PB(b) + [mtok], w=[("tmpR", sl)])
            P.pool(lambda e: e.tensor_tensor(out=xT[:, dt, tt * 512:(tt + 1) * 512], in0=xT[:, dt, tt * 512:(tt + 1) * 512],
                                             in1=tmpR[:, sl, :], op=ALU.add),
                   r=[("tmpR", sl), ("xT", dt, tt)], w=[("xT", dt, tt)])

    projbank = [0]

    def nextbank():
        projbank[0] ^= 1
        return projbank[0]

    obank = [0]

    def nextobank(banks):
        obank[0] = (obank[0] + 1) % len(banks)
        return banks[obank[0]]

    RB = [2, 3]

    def rms_sums(tt, par):
        sq = [carve(AR, 40960 + (par * 2 + i) * 1024, 1024, BF16) for i in range(2)]
        rt = carve(AR, 45056 + par * 2048, 2048, F32)
        rstd = carve(AR, 49152 + par * 2048, 2048, F32)
        bk = RB[par]
        for kc in range(8):
            s_ = kc % 2
            if False:
                P.act(lambda e, kc=kc, s_=s_: e.activation(out=sq[s_], in_=xT[:, kc, tt * 512:(tt + 1) * 512], func=AF.Square),
                      r=[("xT", kc, tt)], w=[("A", "sq", par, s_)])
            else:
                P.pool(lambda e, kc=kc, s_=s_: e.tensor_tensor(out=sq[s_], in0=xT[:, kc, tt * 512:(tt + 1) * 512],
                                                              in1=xT[:, kc, tt * 512:(tt + 1) * 512], op=ALU.mult),
                       r=[("xT", kc, tt)], w=[("A", "sq", par, s_)])
            P.pe(lambda e, kc=kc, s_=s_: e.matmul(psb[bk][:, :], onesb[:, :], sq[s_], start=(kc == 0), stop=(kc == 7)),
                 r=[("A", "sq", par, s_), "onesb"], w=PB(bk))
        P.act(lambda e: e.activation(out=rt, in_=psb[bk][:, :], func=AF.Sqrt, scale=1.0 / D, bias=EPS),
              r=PB(bk), w=[("A", "rt", par)])
        P.dve(lambda e: e.reciprocal(out=rstd, in_=rt), r=[("A", "rt", par)], w=[("A", "rstd", par)])
        return rstd

    modraw = sb("modraw", [128, 24], F32)

    def mod_chunk_dma(l, ci, Wa, tok):
        P.dma("pool", lambda e: e.dma_start(
            out=Wa, in_=wada_d[l, :, ci * 512:(ci + 1) * 512].rearrange("(k p) n -> p k n", p=128)), w=[tok])

    def mod_chunk_mm(l, ci, Wa, tok):
        sl = ci % 2
        b = nextbank()
        for kc in range(8):
            P.pe(lambda e, kc=kc: e.matmul(psb[b][0:1, :], cact[:, kc:kc + 1], Wa[:, kc, :],
                                          start=(kc == 0), stop=(kc == 7)),
                 r=[tok, "cact"], w=PB(b))
        P.dve(lambda e: e.tensor_copy(out=tmpR[0:1, sl, :], in_=psb[b][0:1, :]), r=PB(b), w=[("tmpR", sl)])
        for q in range(4):
            P.pe(lambda e, q=q: e.matmul(psb[2][:, q:q + 1], tmpR[0:1, sl, q * 128:(q + 1) * 128],
                                        one11[0:1, 0:1], start=True, stop=True),
                 r=[("tmpR", sl), "one11"], w=PB(2))
        P.dve(lambda e: e.tensor_copy(out=modraw[:, ci * 4:(ci + 1) * 4], in_=psb[2][:, 0:4]), r=PB(2), w=["modraw"])

    def mod_finish(l):
        mp = l % 2
        P.dve(lambda e: e.tensor_tensor(out=modT2[:, mp, :], in0=modraw[:, :], in1=vcols2[:, l * 24:(l + 1) * 24],
                                        op=ALU.add), r=["modraw", "vcols2"], w=[("modT", mp)])
        P.dve(lambda e: e.tensor_scalar(out=a1tmp[:], in0=modT2[:, mp, 8:16], scalar1=1.0, scalar2=None, op0=ALU.add),
              r=[("modT", mp)], w=["a1tmp"])
        P.dve(lambda e: e.tensor_tensor(out=Acol2[:, mp, :], in0=a1tmp[:], in1=vcols[:, l * 8:(l + 1) * 8], op=ALU.mult),
              r=["a1tmp", "vcols"], w=[("Acol", mp)])

    if True:
        P.arena_reset("W")
        for ci in range(6):
            s_ = ci % 2
            Wa = carve(WA, s_ * 8192, 8192, BF16, "p (k n) -> p k n", n=512)
            mod_chunk_dma(0, ci, Wa, ("W", "wa", s_))
            mod_chunk_mm(0, ci, Wa, ("W", "wa", s_))
        mod_finish(0)

    P.arena_reset("A")
    xst = [carve(AR, i * 4096, 4096, F32) for i in range(4)]
    for j in range(16):
        s = j % 4
        P.dma("sp", lambda e, j=j, s=s: e.dma_start(out=xst[s], in_=x_d[j * 128:(j + 1) * 128, :]),
              w=[("A", "xst", s)])
        for g in range(2):
            b = g
            for q in range(4):
                dc = g * 4 + q
                P.pe(lambda e, b=b, q=q, dc=dc, s=s: e.transpose(psb[b][:, q * 128:(q + 1) * 128],
                                                             xst[s][:, dc * 128:(dc + 1) * 128], ident[:, :]),
                     r=[("A", "xst", s), "ident"], w=PB(b))
            wtok = [("xT", 4 * g + q, j // 4) for q in range(4)]
            if g == 0:
                P.act(lambda e, b=b, g=g, j=j: e.activation(out=xT[:, 4 * g:4 * g + 4, j * 128:(j + 1) * 128],
                                                           in_=psb[b][:, :].rearrange("p (a c) -> p a c", c=128),
                                                           func=AF.Copy), r=PB(b), w=wtok)
            else:
                P.dve(lambda e, b=b, g=g, j=j: e.tensor_copy(out=xT[:, 4 * g:4 * g + 4, j * 128:(j + 1) * 128],
                                                            in_=psb[b][:, :].rearrange("p (a c) -> p a c", c=128)),
                      r=PB(b), w=wtok)


    for l in range(depth):
        mT = modT2[:, l % 2, :]
        mtok = ("modT", l % 2)

        P.arena_reset("A")
        xr = [carve(AR, i * 2048, 2048, F32) for i in range(2)]

        def rms_apply(tt, par, rstd):
            for kc in range(8):
                s_ = kc % 2
                P.dve(lambda e, kc=kc, s_=s_: e.tensor_tensor(
                    out=xr[s_], in0=xT[:, kc, tt * 512:(tt + 1) * 512], in1=rstd, op=ALU.mult),
                    r=[("xT", kc, tt), ("A", "rstd", par)], w=[("A", "xr", s_)])
                if kc % 4 == 3:
                    P.dve(lambda e, kc=kc, s_=s_, mT=mT, lp=l % 2: e.tensor_scalar(
                        out=hT[:, kc, tt * 512:(tt + 1) * 512], in0=xr[s_], scalar1=Acol2[:, lp, kc:kc + 1],
                        scalar2=mT[:, kc:kc + 1], op0=ALU.mult, op1=ALU.add),
                        r=[("A", "xr", s_), ("Acol", l % 2), mtok], w=[("hT", tt)])
                else:
                    P.act(lambda e, kc=kc, s_=s_, mT=mT, lp=l % 2: e.activation(
                        out=hT[:, kc, tt * 512:(tt + 1) * 512], in_=xr[s_], func=AF.Identity,
                        scale=Acol2[:, lp, kc:kc + 1], bias=mT[:, kc:kc + 1]),
                        r=[("A", "xr", s_), ("Acol", l % 2), mtok], w=[("hT", tt)])

        r0 = rms_sums(0, 0)
        r1 = rms_sums(1, 1)
        rms_apply(0, 0, r0)
        r2 = rms_sums(2, 0)
        rms_apply(1, 1, r1)
        r3 = rms_sums(3, 1)
        rms_apply(2, 0, r2)
        rms_apply(3, 1, r3)

        P.arena_reset("A")
        P.arena_reset("W")
        P.arena_reset("B")
        P.arena_reset("WO")
        gT = carve(BB, 0, 8192, BF16, "p (w t) -> p w t", t=L)[0:16]
        waf = carve(BB, 8192, 2048, BF16, "p (w n) -> p w n", n=512)[0:16]
        Wg = carve(BB, 10240, 512, BF16, "p (k n) -> p k n", n=32)
        qT = carve(AR, 0, 4096, BF16)
        kT = carve(AR, 4096, 4096, BF16)
        vh = carve(AR, 8192, 8192, BF16, "p (j v) -> p j v", v=256)
        cum = carve(AR, 16384, 8192, F32)
        e1 = carve(AR, 24576, 8192, F32)
        ofwd = carve(AR, 32768, 8192, BF16, "p (j v) -> p j v", v=256)
        ygT = carve(AR, 40960, 8192, BF16, "p (f t) -> p f t", t=L)
        ygTf = carve(AR, 40960, 8192, F32)
        Pb = [carve(AR, 49152, 4096, BF16), carve(WO, 4096, 4096, BF16), carve(WA, 19968, 4096, BF16)]
        Ptag = [("A", "P1"), ("WO", "P2"), ("W", "P3")]
        Wqk = carve(WA, 0, 4096, BF16, "p (k n) -> p k n", n=256)
        Wv = carve(WA, 4096, 4096, BF16, "p (k n) -> p k n", n=256)
        Wr = carve(WA, 8192, 4096, BF16, "p (k n) -> p k n", n=256)
        Wo = carve(WO, 0, 4096, BF16, "p (f d) -> p f d", d=D)
        statef = [carve(WA, 12288 + i * 1024, 1024, F32) for i in range(2)]
        stateb = [carve(WA, 14336 + i * 512, 512, BF16) for i in range(2)]
        srt = [carve(WA, 15360 + i * 1024, 1024, F32) for i in range(2)]
        osq = [carve(WA, 17408 + i * 512, 512, BF16) for i in range(2)]
        og = [carve(WA, 18944 + i * 512, 512, BF16) for i in range(2)]
        cumv = cum.rearrange("p (n c) -> p n c", c=128)
        e1v = e1.rearrange("p (n c) -> p n c", c=128)
        ptk = lambda i, hh: Ptag[i] + (hh,)
        bothP = lambda i: [ptk(i, 0), ptk(i, 1)]

        def win_dma(dst, c0, n, tok, l=l):
            P.dma("pool", lambda e: e.dma_start(
                out=dst, in_=win_d[l, :, c0:c0 + n].rearrange("(k p) n -> p k n", p=128)), w=[tok])

        def load_wqk(h):
            win_dma(Wqk[:, :, 0:128], 2048 + h * 128, 128, ("W", "wqk"))
            win_dma(Wqk[:, :, 128:256], 2560 + h * 128, 128, ("W", "wqk"))

        def load_wv(h):
            win_dma(Wv, 3072 + h * 256, 256, ("W", "wv"))

        def load_wr(h):
            win_dma(Wr, 4096 + h * 256, 256, ("W", "wr"))

        def load_wo(h, l=l):
            P.dma("pool", lambda e: e.dma_start(
                out=Wo, in_=wout_d[l, 1024 + h * 256:1024 + (h + 1) * 256, :].rearrange("(f p) d -> p f d", p=128)),
                w=[("WO", "wo")])

        def proj_qk_tiles(h):
            tiles = []
            for which in range(2):
                dst = qT if which == 0 else kT
                for tt in range(4):
                    st = {}

                    def mm(which=which, tt=tt, st=st):
                        b = nextbank()
                        st["b"] = b
                        for kc in range(8):
                            P.pe(lambda e, kc=kc, b=b: e.matmul(
                                psb[b][:, :], Wqk[:, kc, which * 128:(which + 1) * 128], hT[:, kc, tt * 512:(tt + 1) * 512],
                                start=(kc == 0), stop=(kc == 7)), r=[("W", "wqk"), ("hT", tt)], w=PB(b))

                    def ev(which=which, tt=tt, st=st, dst=dst):
                        b = st["b"]
                        P.act(lambda e: e.activation(
                            out=dst[:, tt * 512:(tt + 1) * 512], in_=psb[b][:, :], func=AF.Copy,
                            scale=(128.0 ** -0.5 if which == 0 else 1.0)),
                            r=PB(b), w=[("A", "qk", which)])
                    tiles.append((mm, ev))
            return tiles

        def proj_qk(h):
            for mm, ev in proj_qk_tiles(h):
                mm()
                ev()

        def proj_v(h):
            for jj in range(8):
                b = nextbank()
                for sub in range(2):
                    j = jj * 2 + sub
                    for kc in range(8):
                        P.pe(lambda e, j=j, sub=sub, kc=kc, b=b: e.matmul(
                            psb[b][:, sub * 256:(sub + 1) * 256], hT[:, kc, j * 128:(j + 1) * 128], Wv[:, kc, :],
                            start=(kc == 0), stop=(kc == 7)), r=[("W", "wv"), ("hT", j // 4)], w=PB(b))
                P.dve(lambda e, jj=jj, b=b: e.tensor_copy(out=vh[:, 2 * jj:2 * jj + 2, :],
                                                         in_=psb[b][:, :].rearrange("p (a v) -> p a v", v=256)),
                      r=PB(b), w=[("A", "vh")])

        def stage1a_pre(h, dr):
            par = dr
            nbcol = (0 if dr == 0 else 16) + l * 4 + h
            for tt in range(4):
                b = nextbank()
                P.pe(lambda e, tt=tt, b=b: e.matmul(
                    psb[b][:, :], waf[0:16, dr, h * 128:(h + 1) * 128], gT[0:16, dr, tt * 512:(tt + 1) * 512],
                    start=True, stop=True), r=[("B", "waf"), ("B", "gT")], w=PB(b))
                P.act(lambda e, tt=tt, b=b: e.activation(
                    out=e1[:, tt * 512:(tt + 1) * 512], in_=psb[b][:, :], func=AF.Exp, scale=-1.0,
                    bias=vneg[:, nbcol:nbcol + 1]), r=PB(b) + ["vneg"], w=[("A", "e1")])
            P.act(lambda e: e.activation(out=e1[:, :], in_=e1[:, :], func=AF.Ln, bias=1.0),
                  r=[("A", "e1")], w=[("A", "e1")])

        def stage1a_post(h, dr):
            par = dr
            if dr == 0:
                P.dve(lambda e: e.tensor_tensor_scan(out=cum[:, :], data0=e1[:, :], data1=e1[:, :], initial=0.0,
                                                     op0=ALU.add, op1=ALU.max),
                      r=[("A", "e1")], w=[("A", "cum")])
                P.dve(lambda e: e.memset(rpos[:, 0:1], 0.0), w=["rpos"])
                P.dve(lambda e: e.tensor_copy(out=rpos[:, 1:17], in_=cum[:, 127:2048:128]),
                      r=[("A", "cum")], w=["rpos"])
            else:
                P.dve(lambda e: e.memset(cum[:, 0:1], 0.0), w=[("A", "cum")])
                P.dve(lambda e: e.tensor_tensor_scan(out=cum[:, 1:2048], data0=e1[:, 0:2047], data1=e1[:, 0:2047],
                                                     initial=0.0, op0=ALU.add, op1=ALU.max),
                      r=[("A", "e1")], w=[("A", "cum")])
                P.dve(lambda e: e.tensor_copy(out=rpos[:, 0:16], in_=cum[:, 0:2048:128]),
                      r=[("A", "cum")], w=["rpos"])
                P.dve(lambda e: e.tensor_copy(out=rpos[:, 16:17], in_=cum[:, 2047:2048]),
                      r=[("A", "cum")], w=["rpos"])
            P.dve(lambda e: e.tensor_tensor(out=dtmp[:], in0=rpos[:, 1:17], in1=rpos[:, 0:16], op=ALU.subtract),
                  r=["rpos"], w=["dtmp"])
            P.act(lambda e: e.activation(out=dcol2[:, par, :], in_=dtmp[:], func=AF.Exp, scale=-1.0 / 16),
                  r=["dtmp"], w=[("dcol", par)])
            P.dve(lambda e: e.tensor_tensor(out=e1v, in0=cumv, in1=rpos[:, 0:16].unsqueeze(2).to_broadcast([128, 16, 128]),
                                            op=ALU.subtract), r=[("A", "cum"), "rpos"], w=[("A", "e1")])
            P.dve(lambda e: e.tensor_tensor(out=cumv, in0=cumv, in1=rpos[:, 1:17].unsqueeze(2).to_broadcast([128, 16, 128]),
                                            op=ALU.subtract), r=[("A", "cum"), "rpos"], w=[("A", "cum")])
            P.act(lambda e: e.activation(out=cum[:, :], in_=cum[:, :], func=AF.Exp, scale=1.0 / 16),
                  r=[("A", "cum")], w=[("A", "cum")])

        def stage1b(h, dr):
            srcs = [qT, kT, kT] if dr == 0 else [kT, qT, qT]
            srct = [0, 1, 1] if dr == 0 else [1, 0, 0]
            ygtok = [("A", "ygT", tt) for tt in range(4)]
            P.act(lambda e: e.activation(out=ygTf[:, :], in_=e1[:, :], func=AF.Exp, scale=-1.0 / 16),
                  r=[("A", "e1")], w=ygtok)
            P.act(lambda e: e.activation(out=e1[:, :], in_=e1[:, :], func=AF.Exp, scale=1.0 / 16),
                  r=[("A", "e1")], w=[("A", "e1")])
            P.dve(lambda e, src=srcs[2]: e.tensor_tensor(out=Pb[2], in0=cum[:, :], in1=src[:, :], op=ALU.mult),
                  r=[("A", "cum"), ("A", "qk", srct[2])], w=bothP(2))
            P.dve(lambda e, src=srcs[0]: e.tensor_tensor(out=Pb[0], in0=ygTf[:, :], in1=src[:, :], op=ALU.mult),
                  r=ygtok + [("A", "qk", srct[0])], w=bothP(0))
            P.dve(lambda e, src=srcs[1]: e.tensor_tensor(out=Pb[1], in0=e1[:, :], in1=src[:, :], op=ALU.mult),
                  r=[("A", "e1"), ("A", "qk", srct[1])], w=bothP(1))

        def stage2(h, dr, side=None):
            par = dr
            side = list(side) if side else []
            pend = []
            if dr == 0:
                kside, qside, interq, ksrc = 1, 0, 0, 2
            else:
                kside, qside, interq, ksrc = 0, 1, 2, 0
            mask = maskf if dr == 0 else maskb
            for hh in range(2):
                for cc in range(8):
                    n = hh * 8 + cc
                    bnk = 2 if cc < 4 else 6
                    P.pe(lambda e, n=n, cc=cc, bnk=bnk: e.matmul(
                        psb[bnk][:, (cc % 4) * 128:(cc % 4 + 1) * 128], Pb[kside][:, n * 128:(n + 1) * 128],
                        Pb[qside][:, n * 128:(n + 1) * 128], start=True, stop=True),
                        r=[ptk(kside, hh), ptk(qside, hh)], w=PB(bnk))
                for q4 in range(2):
                    bnk = 2 if q4 == 0 else 6
                    c0 = hh * 1024 + q4 * 512
                    P.dve(lambda e, bnk=bnk, c0=c0: e.tensor_tensor(
                        out=Pb[1][:, c0:c0 + 512].rearrange("p (a c) -> p a c", c=128),
                        in0=psb[bnk][:, :].rearrange("p (a c) -> p a c", c=128),
                        in1=mask[:, :].unsqueeze(1).to_broadcast([128, 4, 128]), op=ALU.mult),
                        r=PB(bnk) + ["maskf", "maskb"], w=[ptk(1, hh)])
            for g4 in range(4):
                hh = g4 // 2
                for cc in range(4):
                    n = g4 * 4 + cc
                    P.pe(lambda e, n=n, cc=cc, g4=g4: e.transpose(TRB[g4 % 2][0][:, cc * 128:(cc + 1) * 128],
                                                                Pb[ksrc][:, n * 128:(n + 1) * 128], identb[:, :]),
                         r=[ptk(ksrc, hh), "identb"], w=PB(TRB[g4 % 2][1]))
                P.act(lambda e, g4=g4: e.activation(out=Pb[ksrc][:, g4 * 512:(g4 + 1) * 512],
                                                    in_=TRB[g4 % 2][0][:, 0:512], func=AF.Copy),
                      r=PB(TRB[g4 % 2][1]), w=[ptk(ksrc, hh)])
            order = list(range(16)) if dr == 0 else list(range(15, -1, -1))

            def smm(ci):
                n = order[ci]
                bnk = PSS[ci % 3]
                P.pe(lambda e, n=n, bnk=bnk: e.matmul(psb[bnk][:, 0:256], Pb[ksrc][:, n * 128:(n + 1) * 128], vh[:, n, :],
                                                     start=True, stop=True),
                     r=[ptk(ksrc, n // 8), ("A", "vh")], w=PB(bnk))

            smm(0)
            smm(1)
            sidx = 0
            for ci, n in enumerate(order):
                first = (ci == 0)
                last = (ci == 15)
                sl = ci % 2
                hh = n // 8
                if ci + 2 <= 14:
                    smm(ci + 2)
                if pend:
                    pend.pop(0)()
                if side and ci % 4 == 0:
                    mm_, ev_ = side.pop(0)
                    mm_()
                    pend.append(ev_)
                P.pe(lambda e, n=n, first=first, sl=sl: e.matmul(psb[PSO[sl]][:, 0:256], Pb[1][:, n * 128:(n + 1) * 128],
                                                                vh[:, n, :], start=True, stop=first),
                     r=[ptk(1, hh), ("A", "vh")], w=PB(PSO[sl]))
                if not first:
                    P.pe(lambda e, n=n, sl=sl, sidx=sidx: e.matmul(
                        psb[PSO[sl]][:, 0:256], Pb[interq][:, n * 128:(n + 1) * 128], stateb[sidx],
                        start=False, stop=True),
                        r=[ptk(interq, hh), ("W", "stateb", sidx)], w=PB(PSO[sl]))
                if not last:
                    bnk = PSS[ci % 3]
                    nidx = 1 - sidx
                    if first:
                        P.dve(lambda e, nidx=nidx, bnk=bnk: e.tensor_copy(out=statef[nidx], in_=psb[bnk][:, 0:256]),
                              r=PB(bnk), w=[("W", "statef", nidx)])
                    else:
                        P.dve(lambda e, nidx=nidx, sidx=sidx, n=n, bnk=bnk: e.scalar_tensor_tensor(
                            out=statef[nidx], in0=statef[sidx], scalar=dcol2[:, par, n:n + 1],
                            in1=psb[bnk][:, 0:256], op0=ALU.mult, op1=ALU.add),
                            r=PB(bnk) + [("W", "statef", sidx), ("dcol", par)], w=[("W", "statef", nidx)])
                    P.act(lambda e, nidx=nidx: e.activation(out=stateb[nidx], in_=statef[nidx], func=AF.Copy),
                          r=[("W", "statef", nidx)], w=[("W", "stateb", nidx)])
                    sidx = nidx
                if dr == 0:
                    P.act(lambda e, n=n, sl=sl: e.activation(out=ofwd[:, n, :], in_=psb[PSO[sl]][:, 0:256], func=AF.Copy),
                          r=PB(PSO[sl]), w=[("A", "ofwd", n)])
                else:
                    P.dve(lambda e, n=n, sl=sl: e.tensor_tensor(out=ofwd[:, n, :], in0=psb[PSO[sl]][:, 0:256],
                                                               in1=ofwd[:, n, :], op=ALU.add),
                          r=PB(PSO[sl]) + [("A", "ofwd", n)], w=[("A", "ofwd", n)])
            flush_side(side, pend)

        def flush_side(side, pend):
            for ev_ in pend:
                ev_()
            for mm_, ev_ in side:
                mm_()
                ev_()

        def finalize_pre1(h):
            for hh in range(2):
                P.act(lambda e, hh=hh: e.activation(
                    out=Pb[hh], in_=ofwd[:, hh * 8:(hh + 1) * 8, :].rearrange("p j v -> p (j v)"), func=AF.Square),
                    r=[("A", "ofwd", n) for n in range(hh * 8, hh * 8 + 8)], w=bothP(hh))

        def finalize_pre2(h):
            for hh in range(2):
                P.dve(lambda e, hh=hh: e.tensor_reduce(out=ssq[:, hh * 8:(hh + 1) * 8],
                                                      in_=Pb[hh].rearrange("p (j v) -> p j v", v=256), axis=AX.X, op=ALU.add),
                      r=bothP(hh), w=["ssq"])
            P.act(lambda e: e.activation(out=ssr[:, :], in_=ssq[:, :], func=AF.Sqrt, scale=1.0 / 256, bias=EPS),
                  r=["ssq"], w=["ssr"])
            P.dve(lambda e: e.reciprocal(out=ssr2[:, :], in_=ssr[:, :]), r=["ssr"], w=["ssr2"])

        def finalize(h):
            gc = 32 + l * 8 + h * 2

            def rproj(n):
                sl = n % 2
                for kc in range(8):
                    P.pe(lambda e, n=n, kc=kc, sl=sl: e.matmul(psb[PSR[sl]][:, 0:256], hT[:, kc, n * 128:(n + 1) * 128],
                                                              Wr[:, kc, :], start=(kc == 0), stop=(kc == 7)),
                         r=[("W", "wr"), ("hT", n // 4)], w=PB(PSR[sl]))

            def gate(n):
                sl = n % 2
                P.act(lambda e, sl=sl: e.activation(out=srt[sl], in_=psb[PSR[sl]][:, 0:256], func=AF.Silu),
                      r=PB(PSR[sl]), w=[("W", "srt", sl)])
                P.dve(lambda e, n=n, sl=sl: e.scalar_tensor_tensor(out=og[sl], in0=ofwd[:, n, :], scalar=ssr2[:, n:n + 1],
                                                                  in1=srt[sl], op0=ALU.mult, op1=ALU.mult),
                      r=[("A", "ofwd", n), "ssr2", ("W", "srt", sl)], w=[("W", "og", sl)])

            def trans(n):
                sl = n % 2
                for f2 in range(2):
                    P.pe(lambda e, f2=f2, sl=sl: e.transpose(TRB[sl][0][:, f2 * 128:(f2 + 1) * 128],
                                                            og[sl][:, f2 * 128:(f2 + 1) * 128], identb[:, :]),
                         r=[("W", "og", sl), "identb"], w=PB(TRB[sl][1]))

            def evac(n):
                sl = n % 2
                for f2 in range(2):
                    P.act(lambda e, f2=f2, n=n, sl=sl: e.activation(
                        out=ygT[:, f2, n * 128:(n + 1) * 128], in_=TRB[sl][0][:, f2 * 128:(f2 + 1) * 128],
                        func=AF.Identity, scale=vcols[:, gc + f2:gc + f2 + 1]),
                        r=PB(TRB[sl][1]) + ["vcols"], w=[("A", "ygT", n // 4)])

            rproj(0)
            rproj(1)
            gate(0)
            for n in range(16):
                trans(n)
                if n + 2 < 16:
                    rproj(n + 2)
                if n + 1 < 16:
                    gate(n + 1)
                evac(n)

        def outproj(h):
            for dt in range(8):
                for tt in range(4):
                    b = nextobank([0, 1, 3, 4])
                    for f2 in range(2):
                        P.pe(lambda e, dt=dt, tt=tt, f2=f2, b=b: e.matmul(
                            psb[b][:, :], Wo[:, f2, dt * 128:(dt + 1) * 128], ygT[:, f2, tt * 512:(tt + 1) * 512],
                            start=(f2 == 0), stop=(f2 == 1)), r=[("WO", "wo"), ("A", "ygT", tt)], w=PB(b))
                    residual(dt, tt, b, mT, mtok)

        P.dma("pool", lambda e, l=l: e.dma_start(out=Wg, in_=win_d[l, :, 5120:5152].rearrange("(k p) n -> p k n", p=128)),
              w=[("B", "Wg")])
        P.dma("pool", lambda e, l=l: e.dma_start(out=waf[:, 0, :], in_=waf_d[l]), w=[("B", "waf")])
        P.dma("pool", lambda e, l=l: e.dma_start(out=waf[:, 1, :], in_=wab_d[l]), w=[("B", "waf")])
        load_wqk(0)
        load_wv(0)
        load_wr(0)
        load_wo(0)
        for which in range(2):
            for tt in range(4):
                b = nextbank()
                for kc in range(8):
                    P.pe(lambda e, which=which, tt=tt, kc=kc, b=b: e.matmul(
                        psb[b][0:16, :], Wg[:, kc, which * 16:(which + 1) * 16], hT[:, kc, tt * 512:(tt + 1) * 512],
                        start=(kc == 0), stop=(kc == 7)), r=[("B", "Wg"), ("hT", tt)], w=PB(b))
                P.act(lambda e, which=which, tt=tt, b=b: e.activation(out=gT[:, which, tt * 512:(tt + 1) * 512],
                                                                     in_=psb[b][0:16, :], func=AF.Copy),
                      r=PB(b), w=[("B", "gT")])
        stage1a_pre(0, 0)
        stage1a_post(0, 0)
        proj_qk(0)
        load_wqk(1)
        proj_v(0)
        load_wv(1)
        for h in range(4):
            stage1b(h, 0)
            stage1a_pre(h, 1)
            stage2(h, 0)
            stage1a_post(h, 1)
            stage1b(h, 1)
            if h < 3:
                stage1a_pre(h + 1, 0)
            stage2(h, 1, side=(proj_qk_tiles(h + 1) if h < 3 else None))
            if h < 2:
                load_wqk(h + 2)
            finalize_pre1(h)
            if h < 3:
                stage1a_post(h + 1, 0)
            finalize_pre2(h)
            if h < 3:
                proj_v(h + 1)
                if h < 2:
                    load_wv(h + 2)
            finalize(h)
            if h < 3:
                load_wr(h + 1)
            outproj(h)
            if h < 3:
                load_wo(h + 1)

        P.arena_reset("A")
        P.arena_reset("W")
        P.arena_reset("B")
        P.arena_reset("WO")
        MM = carve(BB, 0, 8192, BF16, "p (w g c d) -> p w g c d", w=2, g=4, c=2)
        wfm = carve(BB, 8192, 4096, BF16, "p (g c d) -> p g c d", g=4, c=2)
        P.dma("pool", lambda e, l=l: e.dma_start(out=wfm, in_=wfm_d[l].rearrange("g (c p) d -> p g c d", p=128)),
              w=[("B", "wfm")])
        for g in range(4):
            for cto in range(2):
                b = nextbank()
                for which in range(2):
                    for ctp in range(2):
                        P.pe(lambda e, g=g, cto=cto, which=which, ctp=ctp, b=b: e.matmul(
                            psb[b][:, which * 256:(which + 1) * 256], cs256[:, ctp, which, cto * 128:(cto + 1) * 128],
                            wfm[:, g, ctp, :], start=(ctp == 0), stop=(ctp == 1)),
                            r=["cs256", ("B", "wfm")], w=PB(b))
                P.act(lambda e, g=g, cto=cto, b=b: e.activation(
                    out=MM[:, :, g, cto, :], in_=psb[b][:, :].rearrange("p (w d) -> p w d", d=256), func=AF.Copy),
                    r=PB(b), w=[("B", "MM")])
        UT = carve(AR, 0, 4104, BF16)
        Up = carve(AR, 4104, 4096, BF16, "p (c t) -> p c t", t=1024)
        Um = carve(AR, 8200, 4096, BF16, "p (c t) -> p c t", t=1024)
        AB = carve(AR, 12296, 16384, BF16, "p (j w c) -> p j w c", w=2, c=512)
        zy = carve(AR, 28680, 16384, BF16, "p (c t) -> p c t", t=L)
        NR = 3
        ring = [carve(AR, 45064 + i * 2048, 2048, BF16, "p (w k) -> p w k", k=512) for i in range(NR)]
        P.pool(lambda e: e.memset(UT[:, 2048:2049], 0.0), w=[("A", "UTpad")])
        rcnt = 0
        for hf in range(2):
            Wu = carve(WA, 0, 8192, BF16, "p (k n) -> p k n", n=512)
            Wz = carve(WA, 8192, 8192, BF16, "p (k n) -> p k n", n=512)
            P.dma("pool", lambda e, hf=hf, l=l, Wu=Wu: e.dma_start(
                out=Wu, in_=win_d[l, :, hf * 512:(hf + 1) * 512].rearrange("(k p) n -> p k n", p=128)), w=[("W", "wu")])
            P.dma("pool", lambda e, hf=hf, l=l, Wz=Wz: e.dma_start(
                out=Wz, in_=win_d[l, :, 1024 + hf * 512:1024 + (hf + 1) * 512].rearrange("(k p) n -> p k n", p=128)),
                w=[("W", "wz")])
            Wo2 = carve(WO, 0, 8192, BF16, "p (f d) -> p f d", d=D)
            P.dma("pool", lambda e, hf=hf, l=l, Wo2=Wo2: e.dma_start(
                out=Wo2, in_=wout_d[l, hf * 512:(hf + 1) * 512, :].rearrange("(f p) d -> p f d", p=128)),
                w=[("WO", "wo2")])
            def zproj(c4):
                for tt in range(4):
                    b = nextbank()
                    for kc in range(8):
                        P.pe(lambda e, c4=c4, tt=tt, kc=kc, b=b: e.matmul(
                            psb[b][:, :], Wz[:, kc, c4 * 128:(c4 + 1) * 128], hT[:, kc, tt * 512:(tt + 1) * 512],
                            start=(kc == 0), stop=(kc == 7)), r=[("W", "wz"), ("hT", tt)], w=PB(b))
                    P.act(lambda e, c4=c4, tt=tt, b=b: e.activation(out=zy[:, c4, tt * 512:(tt + 1) * 512],
                                                                   in_=psb[b][:, :], func=AF.Silu),
                          r=PB(b), w=[("A", "zy", c4, tt)])

            for gl in range(2):
                g = hf * 2 + gl
                for ct in range(2):
                    c4 = gl * 2 + ct
                    for tt in range(4):
                        b = nextbank()
                        for kc in range(8):
                            P.pe(lambda e, c4=c4, tt=tt, kc=kc, b=b, Wu=Wu: e.matmul(
                                psb[b][:, :], Wu[:, kc, c4 * 128:(c4 + 1) * 128], hT[:, kc, tt * 512:(tt + 1) * 512],
                                start=(kc == 0), stop=(kc == 7)), r=[("W", "wu"), ("hT", tt)], w=PB(b))
                        P.act(lambda e, tt=tt, b=b: e.activation(out=UT[:, tt * 512:(tt + 1) * 512], in_=psb[b][:, :],
                                                                 func=AF.Copy), r=PB(b), w=[("A", "UT", tt)])
                    zproj(c4)
                    UTr = [("A", "UT", tt) for tt in range(4)] + [("A", "UTpad")]
                    P.dve(lambda e, ct=ct: e.tensor_tensor(out=Up[:, ct, :], in0=UT[:, 0:1024], in1=UT[:, 2048:1024:-1],
                                                          op=ALU.add), r=UTr, w=[("A", "Up", ct)])
                    P.pool(lambda e, ct=ct: e.tensor_tensor(out=Um[:, ct, :], in0=UT[:, 0:1024], in1=UT[:, 2048:1024:-1],
                                                           op=ALU.subtract), r=UTr, w=[("A", "Um", ct)])
                    P.dve(lambda e, ct=ct: e.tensor_copy(out=u1024[:, ct:ct + 1], in_=UT[:, 1024:1025]),
                          r=UTr, w=[("u1024", ct)])
                for j in range(8):
                    b = nextbank()
                    for which in range(2):
                        src = Up if which == 0 else Um
                        for ct in range(2):
                            P.pe(lambda e, j=j, which=which, ct=ct, b=b, src=src, g=g: e.matmul(
                                psb[b][:, which * 256:(which + 1) * 256], src[:, ct, j * 128:(j + 1) * 128],
                                MM[:, which, g, ct, :], start=(ct == 0), stop=(ct == 1)),
                                r=[("A", "Up", ct), ("A", "Um", ct), ("B", "MM")], w=PB(b))
                    P.act(lambda e, j=j, b=b, gl=gl: e.activation(
                        out=AB[:, j, :, gl * 256:(gl + 1) * 256], in_=psb[b][:, :].rearrange("p (w d) -> p w d", d=256),
                        func=AF.Copy), r=PB(b), w=[("A", "AB", j)])
                for ct in range(2):
                    P.pe(lambda e, ct=ct, g=g: e.matmul(psb[2][0:1, 0:256], u1024[:, ct:ct + 1], MM[:, 0, g, ct, :],
                                                       start=(ct == 0), stop=(ct == 1)),
                         r=[("u1024", ct), ("B", "MM")], w=PB(2))
                P.act(lambda e, gl=gl: e.activation(out=a1024[0:1, gl * 256:(gl + 1) * 256], in_=psb[2][0:1, 0:256],
                                                    func=AF.Copy), r=PB(2), w=["a1024"])
            for kt in range(4):
                hci = hf * 4 + kt
                hoist = (l + 1 < depth) and hci < 6
                if hoist:
                    WaH = carve(WA, 16384, 8192, BF16, "p (k n) -> p k n", n=512)
                    mod_chunk_dma(l + 1, hci, WaH, ("W", "waH"))
                for j in range(8):
                    rs = rcnt % NR
                    rcnt += 1
                    P.dma("sp", lambda e, j=j, kt=kt, rs=rs: e.dma_start(out=ring[rs], in_=tab_d[j, :, kt]),
                          w=[("A", "ring", rs)])
                    for c4 in range(4):
                        for which in range(2):
                            P.pe(lambda e, j=j, c4=c4, which=which, rs=rs: e.matmul(
                                psb[3 + c4][:, :], AB[:, j, which, c4 * 128:(c4 + 1) * 128], ring[rs][:, which, :],
                                start=(j == 0 and which == 0), stop=False),
                                r=[("A", "AB", j), ("A", "ring", rs)], w=PB(3 + c4))
                for c4 in range(4):
                    P.pe(lambda e, c4=c4, kt=kt: e.matmul(psb[3 + c4][:, :], a1024[0:1, c4 * 128:(c4 + 1) * 128],
                                                         alt[0:1, kt * 512:(kt + 1) * 512], start=False, stop=True),
                         r=["a1024", "alt"], w=PB(3 + c4))
                    P.dve(lambda e, c4=c4, kt=kt: e.tensor_tensor(
                        out=zy[:, c4, kt * 512:(kt + 1) * 512], in0=psb[3 + c4][:, :], in1=zy[:, c4, kt * 512:(kt + 1) * 512],
                        op=ALU.mult), r=PB(3 + c4) + [("A", "zy", c4, kt)], w=[("A", "zy", c4, kt)])
                if hoist:
                    mod_chunk_mm(l + 1, hci, WaH, ("W", "waH"))
            if hf == 1 and l + 1 < depth:
                mod_finish(l + 1)
            for dt in range(8):
                for tt in range(4):
                    b = nextobank([0, 1, 3, 4])
                    for c4 in range(4):
                        P.pe(lambda e, dt=dt, tt=tt, c4=c4, b=b, Wo2=Wo2: e.matmul(
                            psb[b][:, :], Wo2[:, c4, dt * 128:(dt + 1) * 128], zy[:, c4, tt * 512:(tt + 1) * 512],
                            start=(c4 == 0), stop=(c4 == 3)), r=[("WO", "wo2"), ("A", "zy", c4, tt)], w=PB(b))
                    residual(dt, tt, b, mT, mtok)

    P.arena_reset("A")
    onT = carve(AR, 0, 16384, F32, "p (k t) -> p k t", t=512)
    ostage = [carve(AR, 16384 + i * 4096, 4096, F32) for i in range(4)]
    xr2 = [carve(AR, 32768 + i * 2048, 2048, F32) for i in range(2)]
    finals = []
    rs_ = [None] * 4
    rs_[0] = rms_sums(0, 0)
    rs_[1] = rms_sums(1, 1)
    for tt in range(4):
        rstd = rs_[tt]
        for kc in range(8):
            s = kc % 2
            P.dve(lambda e, kc=kc, tt=tt, s=s, rstd=rstd: e.tensor_tensor(
                out=xr2[s], in0=xT[:, kc, tt * 512:(tt + 1) * 512], in1=rstd, op=ALU.mult),
                r=[("xT", kc, tt), ("A", "rstd", tt % 2)], w=[("A", "xr2", s)])
            P.act(lambda e, kc=kc, s=s: e.activation(out=onT[:, kc, :], in_=xr2[s], func=AF.Identity,
                                                     scale=vcols[:, 96 + kc:97 + kc]),
                  r=[("A", "xr2", s), "vcols"], w=[("A", "onT", kc)])
        if tt + 2 < 4:
            rs_[tt + 2] = rms_sums(tt + 2, tt % 2)
        for jj in range(4):
            j = tt * 4 + jj
            s = j % 4
            for g in range(2):
                b = nextbank()
                for q in range(4):
                    kc = g * 4 + q
                    P.pe(lambda e, b=b, q=q, kc=kc, jj=jj: e.transpose(psb[b][:, q * 128:(q + 1) * 128],
                                                                     onT[:, kc, jj * 128:(jj + 1) * 128], ident[:, :]),
                         r=[("A", "onT", kc), "ident"], w=PB(b))
                if g == 0:
                    P.act(lambda e, b=b, s=s: e.activation(out=ostage[s][:, 0:512], in_=psb[b][:, :], func=AF.Copy),
                          r=PB(b), w=[("A", "ost", s, 0)])
                else:
                    P.dve(lambda e, b=b, s=s: e.tensor_copy(out=ostage[s][:, 512:1024], in_=psb[b][:, :]),
                          r=PB(b), w=[("A", "ost", s, 1)])
            finals.append(P.dma("sp", lambda e, j=j, s=s: e.dma_start(out=y_d[j * 128:(j + 1) * 128, :], in_=ostage[s]),
                                r=[("A", "ost", s, 0), ("A", "ost", s, 1)]))
    P.emit(final_waits=finals)
    es.close()
    return nc


_NC_CACHE = {}


def _prep_inputs(x, c, norm_g, w_ada, b_ada, w_in, w_fmap, w_af, b_af, w_ab, b_ab, gla_norm_g, w_out, final_g):
    f = lambda a: np.ascontiguousarray(np.asarray(a, dtype=np.float32))
    cst = _consts()
    shared = {
        "b_ada": f(b_ada).reshape(96, 128),
        "w_ada": f(w_ada), "w_in": f(w_in), "w_fmap": f(w_fmap), "w_af": f(w_af), "w_ab": f(w_ab), "w_out": f(w_out),
    }
    shared.update(cst)
    x = f(x)
    c = f(c)
    in_maps = []
    for b in range(8):
        vrows = np.concatenate([f(norm_g).reshape(32, 128), f(gla_norm_g).reshape(32, 128), f(b_af).reshape(16, 128),
                                f(b_ab).reshape(16, 128), f(final_g).reshape(8, 128), c[b].reshape(8, 128)], axis=0)
        m = dict(shared)
        m["x"] = x[b]
        m["vrows"] = np.ascontiguousarray(vrows)
        in_maps.append(m)
    return in_maps


def kernel(x, c, norm_g, w_ada, b_ada, w_in, w_fmap, w_af, b_af, w_ab, b_ab, gla_norm_g, w_out, final_g):
    in_maps = _prep_inputs(x, c, norm_g, w_ada, b_ada, w_in, w_fmap, w_af, b_af, w_ab, b_ab, gla_norm_g, w_out, final_g)
    if "nc" not in _NC_CACHE:
        _NC_CACHE["nc"] = build_nc(DEPTH)
    nc = _NC_CACHE["nc"]
    res = run_bass_kernel_spmd(nc, in_maps, core_ids=list(range(8)))
    out = np.stack([np.asarray(r["y"], dtype=np.float32) for r in res.results], axis=0)
    return out
```

```python
import contextlib
import numpy as np
import ml_dtypes
import concourse.bass as bass
import concourse.mybir as mybir
from concourse.bass_utils import run_bass_kernel_spmd

F32 = mybir.dt.float32
BF16 = mybir.dt.bfloat16
AF = mybir.ActivationFunctionType
ALU = mybir.AluOpType
AX = mybir.AxisListType

L = 2048
D = 1024
DEPTH = 4
DIN = 5152
EPS = 1e-6
ENGS = ["pe", "act", "dve", "pool", "sp"]
UNIQUE_POOL_SEMS = False


class Op:
    __slots__ = ("fn", "deps", "dma", "sig", "cnt")

    def __init__(self, fn, deps, dma):
        self.fn = fn
        self.deps = deps
        self.dma = dma
        self.sig = False
        self.cnt = 0


class Prog:
    def __init__(self, nc, n_chan=24):
        self.nc = nc
        self.ops = {e: [] for e in ENGS}
        self.tok = {}
        self.n_chan = n_chan
        self.chan_cnt = [0] * n_chan
        self.rr = 0
        self.rr_pool = 0
        self.arena_base = {}

    def _st(self, t):
        st = self.tok.get(t)
        if st is None:
            base = []
            if isinstance(t, tuple) and t[0] in self.arena_base:
                base = list(self.arena_base[t[0]])
            st = {"w": None, "r": base}
            self.tok[t] = st
        return st

    def arena_reset(self, tag):
        ev = set(self.arena_base.get(tag, []))
        for t in list(self.tok.keys()):
            if isinstance(t, tuple) and t[0] == tag:
                st = self.tok.pop(t)
                if st["w"] is not None:
                    ev.add(st["w"])
                ev.update(st["r"])
        best = {}
        for e in ev:
            key = (e[0], e[1])
            if key not in best or e[2] > best[key][2]:
                best[key] = e
        self.arena_base[tag] = list(best.values())

    def add(self, eng, fn, reads=(), writes=(), dma=False, chan=None):
        idx = len(self.ops[eng])
        deps = set()
        d = None
        if dma:
            if chan is None and eng == "pool" and UNIQUE_POOL_SEMS:
                chan = len(self.chan_cnt)
                self.chan_cnt.append(0)
            elif chan is None:
                half = self.n_chan // 2
                if eng == "pool":
                    chan = half + self.rr_pool
                    self.rr_pool = (self.rr_pool + 1) % (self.n_chan - half)
                else:
                    chan = self.rr
                    self.rr = (self.rr + 1) % half
            k = self.chan_cnt[chan]
            self.chan_cnt[chan] += 1
            d = (chan, k)
            if k > 0:
                deps.add(("dma", chan, k - 1))
            me = ("dma", chan, k)
        else:
            me = ("eng", eng, idx)
        for t in reads:
            st = self._st(t)
            if st["w"] is not None:
                deps.add(st["w"])
        for t in writes:
            st = self._st(t)
            if st["w"] is not None:
                deps.add(st["w"])
            deps.update(st["r"])
        for t in reads:
            self.tok[t]["r"].append(me)
        for t in writes:
            st = self.tok[t]
            st["w"] = me
            st["r"] = []
        deps.discard(me)
        self.ops[eng].append(Op(fn, deps, d))
        return me

    def pe(self, fn, r=(), w=()):
        return self.add("pe", fn, r, w)

    def act(self, fn, r=(), w=()):
        return self.add("act", fn, r, w)

    def dve(self, fn, r=(), w=()):
        return self.add("dve", fn, r, w)

    def pool(self, fn, r=(), w=()):
        return self.add("pool", fn, r, w)

    def dma(self, q, fn, r=(), w=(), chan=None):
        return self.add(q, fn, r, w, dma=True, chan=chan)

    def _skip(self, e, i, dep):
        _, de, di = dep
        if de != e:
            return False
        if e == "pe":
            return True
        return (i - di) > 3

    def emit(self, final_waits=()):
        nc = self.nc
        for e in ENGS:
            for i, op in enumerate(self.ops[e]):
                for dep in op.deps:
                    if dep[0] == "eng" and not self._skip(e, i, dep):
                        self.ops[dep[1]][dep[2]].sig = True
        for e in ENGS:
            c = 0
            for op in self.ops[e]:
                if op.sig and op.dma is None:
                    c += 1
                op.cnt = c
        stack = contextlib.ExitStack()
        sems = {e: stack.enter_context(nc.semaphore("s_" + e)) for e in ENGS}
        dsems = [stack.enter_context(nc.semaphore("d_%d" % i)) for i in range(len(self.chan_cnt))]
        block = stack.enter_context(nc.Block())

        def run_engine(e, eng):
            seen = {}
            for i, op in enumerate(self.ops[e]):
                need = {}
                for dep in op.deps:
                    if dep[0] == "eng":
                        if self._skip(e, i, dep):
                            continue
                        key = ("e", dep[1])
                        val = self.ops[dep[1]][dep[2]].cnt
                    else:
                        key = ("d", dep[1])
                        val = 16 * (dep[2] + 1)
                    if val > need.get(key, 0):
                        need[key] = val
                for key, val in need.items():
                    if seen.get(key, 0) >= val:
                        continue
                    seen[key] = val
                    s = sems[key[1]] if key[0] == "e" else dsems[key[1]]
                    eng.wait_ge(s, val)
                ins = op.fn(eng)
                if op.dma is not None:
                    ins.then_inc(dsems[op.dma[0]], 16)
                elif op.sig:
                    ins.then_inc(sems[e], 1)
            if e == "sp":
                for ev in final_waits:
                    eng.wait_ge(dsems[ev[1]], 16 * (ev[2] + 1))

        @block.tensor
        def _(eng):
            run_engine("pe", eng)

        @block.scalar
        def _(eng):
            run_engine("act", eng)

        @block.vector
        def _(eng):
            run_engine("dve", eng)

        @block.gpsimd
        def _(eng):
            run_engine("pool", eng)

        @block.sync
        def _(eng):
            run_engine("sp", eng)

        stack.close()


_CONSTS = None


def _consts():
    global _CONSTS
    if _CONSTS is not None:
        return _CONSTS
    bf = ml_dtypes.bfloat16
    l = np.arange(1024, dtype=np.int64)[:, None]
    k = np.arange(2048, dtype=np.int64)[None, :]
    ang = 2.0 * np.pi * ((l * k) % 2048).astype(np.float64) / 2048.0
    sc = 1.0 / np.sqrt(2048.0)
    cosT = np.cos(ang) * sc
    nsinT = -np.sin(ang) * sc
    tab = np.stack([cosT, nsinT], axis=1)
    tab = tab.reshape(8, 128, 2, 4, 512).transpose(0, 1, 3, 2, 4)
    tab = np.ascontiguousarray(tab).astype(bf)
    alt = (((-1.0) ** np.arange(2048)) * sc).reshape(1, 2048).astype(bf)
    c = np.arange(256, dtype=np.int64)
    a2 = 2.0 * np.pi * ((c[:, None] * c[None, :]) % 256).astype(np.float64) / 256.0
    cs = np.stack([np.cos(a2) / 16.0, np.sin(a2) / 16.0], axis=1)
    cs = cs.reshape(2, 128, 2, 256).transpose(1, 0, 2, 3)
    cs256 = np.ascontiguousarray(cs).astype(bf)
    ident = np.eye(128, dtype=np.float32)
    identb = np.eye(128).astype(bf)
    s = np.arange(128)[:, None]
    t = np.arange(128)[None, :]
    maskf = (t >= s).astype(np.float32).astype(bf)
    maskb = (s > t).astype(np.float32).astype(bf)
    onesb = np.ones((128, 128), dtype=np.float32).astype(bf)
    one11 = np.ones((1, 1), dtype=np.float32)
    _CONSTS = dict(tab=tab, alt=alt, cs256=cs256, ident=ident, identb=identb, maskf=maskf,
                   maskb=maskb, onesb=onesb, one11=one11)
    return _CONSTS


def build_nc(depth=DEPTH):
    nc = bass.Bass("TRN2", target_bir_lowering=False)

    def din(name, shape, dt=F32):
        return nc.dram_tensor(name, list(shape), dt, kind="ExternalInput").ap()

    x_d = din("x", [L, D])
    vrows_d = din("vrows", [112, 128])
    bada_d = din("b_ada", [96, 128])
    wada_d = din("w_ada", [DEPTH, D, 3072])
    win_d = din("w_in", [DEPTH, D, DIN])
    wfm_d = din("w_fmap", [DEPTH, 4, 256, 256])
    waf_d = din("w_af", [DEPTH, 16, 512])
    wab_d = din("w_ab", [DEPTH, 16, 512])
    wout_d = din("w_out", [DEPTH, 2048, D])
    tab_d = din("tab", [8, 128, 4, 2, 512], BF16)
    alt_d = din("alt", [1, 2048], BF16)
    cs_d = din("cs256", [128, 2, 2, 256], BF16)
    ident_d = din("ident", [128, 128])
    identb_d = din("identb", [128, 128], BF16)
    maskf_d = din("maskf", [128, 128], BF16)
    maskb_d = din("maskb", [128, 128], BF16)
    onesb_d = din("onesb", [128, 128], BF16)
    one11_d = din("one11", [1, 1])
    y_d = nc.dram_tensor("y", [L, D], F32, kind="ExternalOutput").ap()

    es = contextlib.ExitStack()

    def sb(name, shape, dt):
        return es.enter_context(nc.sbuf_tensor("sb_" + name, list(shape), dt))

    xT = sb("xT", [128, 8, L], F32)
    hT = sb("hT", [128, 8, L], BF16)
    AR = sb("AR", [128, 13312], F32)
    WA = sb("WA", [128, 6144], F32)
    WO = sb("WO", [128, 2048], F32)
    BB = sb("BB", [128, 3072], F32)
    cs256 = sb("cs256", [128, 2, 2, 256], BF16)
    ident = sb("ident", [128, 128], F32)
    identb = sb("identb", [128, 128], BF16)
    maskf = sb("maskf", [128, 128], BF16)
    maskb = sb("maskb", [128, 128], BF16)
    onesb = sb("onesb", [128, 128], BF16)
    one11 = sb("one11", [1, 1], F32)
    alt = sb("alt", [1, 2048], BF16)
    vrows = sb("vrows", [112, 128], F32)
    vcols = sb("vcols", [128, 112], F32)
    vneg = sb("vneg", [128, 32], F32)
    cact = sb("cact", [128, 8], BF16)
    vrows2 = sb("vrows2", [96, 128], F32)
    vcols2 = sb("vcols2", [128, 96], F32)
    modT2 = sb("modT2", [128, 2, 24], F32)
    Acol2 = sb("Acol2", [128, 2, 8], F32)
    a1tmp = sb("a1tmp", [128, 8], F32)
    rpos = sb("rpos", [128, 17], F32)
    rneg = sb("rneg", [128, 17], F32)
    dtmp = sb("dtmp", [128, 16], F32)
    dcol2 = sb("dcol2", [128, 2, 16], F32)
    ssr2 = sb("ssr2", [128, 16], F32)
    ssq = sb("ssq", [128, 16], F32)
    ssr = sb("ssr", [128, 16], F32)
    a1024 = sb("a1024", [1, 512], BF16)
    u1024 = sb("u1024", [128, 2], BF16)

    psb = [es.enter_context(nc.psum_tensor("ps%d" % i, [128, 512], F32)) for i in range(7)]
    pst = es.enter_context(nc.psum_tensor("pst", [128, 1024], BF16))

    P = Prog(nc)

    def PB(b):
        return [("ps", b)]

    PSS = [4, 6, 5]
    PSO = [3, 2]
    PSR = [5, 4]
    ps6b = psb[6][:, :].bitcast(BF16)
    TRB = [(pst, 7), (ps6b, 6)]

    def carve(base, off_b, nbytes, dt, pattern=None, **kw):
        v = base[:, off_b // 4:(off_b + nbytes) // 4]
        if dt != F32:
            v = v.bitcast(dt)
        if pattern is not None:
            v = v.rearrange(pattern, **kw)
        return v

    K = 1024
    for (dst, src, tokn) in [(ident, ident_d, "ident"), (identb, identb_d, "identb"), (maskf, maskf_d, "maskf"),
                             (maskb, maskb_d, "maskb"), (onesb, onesb_d, "onesb"), (one11, one11_d, "one11"),
                             (alt, alt_d, "alt"), (vrows, vrows_d, "vrows"), (vrows2, bada_d, "vrows2")]:
        P.dma("sp", lambda e, dst=dst, src=src: e.dma_start(out=dst[:], in_=src), w=[tokn])
    P.dma("sp", lambda e: e.dma_start(out=cs256[:], in_=cs_d), w=["cs256"])
    P.pe(lambda e: e.transpose(psb[2][:, 0:112], vrows[:, :], ident[0:112, 0:112]), r=["vrows", "ident"], w=PB(2))
    P.dve(lambda e: e.tensor_copy(out=vcols[:], in_=psb[2][:, 0:112]), r=PB(2), w=["vcols"])
    P.pe(lambda e: e.transpose(psb[2][:, 0:96], vrows2[:, :], ident[0:96, 0:96]), r=["vrows2", "ident"], w=PB(2))
    P.dve(lambda e: e.tensor_copy(out=vcols2[:], in_=psb[2][:, 0:96]), r=PB(2), w=["vcols2"])
    P.dve(lambda e: e.tensor_scalar(out=vneg[:], in0=vcols[:, 64:96], scalar1=-1.0, scalar2=None, op0=ALU.mult),
          r=["vcols"], w=["vneg"])
    P.act(lambda e: e.activation(out=cact[:], in_=vcols[:, 104:112], func=AF.Silu), r=["vcols"], w=["cact"])

    tmpR = sb("tmpR", [128, 2, 512], F32)
    rescnt = [0]

    def residual(dt, tt, b, mT, mtok):
        i = rescnt[0]
        rescnt[0] += 1
        if True:
            P.dve(lambda e: e.scalar_tensor_tensor(
                out=xT[:, dt, tt * 512:(tt + 1) * 512], in0=psb[b][:, :], scalar=mT[:, 16 + dt:17 + dt],
                in1=xT[:, dt, tt * 512:(tt + 1) * 512], op0=ALU.mult, op1=ALU.add),
                r=PB(b) + [mtok, ("xT", dt, tt)], w=[("xT", dt, tt)])
        else:
            sl = (i // 2) % 2
            P.act(lambda e: e.activation(out=tmpR[:, sl, :], in_=psb[b][:, :], func=AF.Identity,
                                         scale=mT[:, 16 + dt:17 + dt]),
                  r=PB(b) + [mtok], w=[("tmpR", sl)])
            P.pool(lambda e: e.tensor_tensor(out=xT[:, dt, tt * 512:(tt + 1) * 512], in0=xT[:, dt, tt * 512:(tt + 1) * 512],
                                             in1=tmpR[:, sl, :], op=ALU.add),
                   r=[("tmpR", sl), ("xT", dt, tt)], w=[("xT", dt, tt)])

    projbank = [0]

    def nextbank():
        projbank[0] ^= 1
        return projbank[0]

    obank = [0]

    def nextobank(banks):
        obank[0] = (obank[0] + 1) % len(banks)
        return banks[obank[0]]

    RB = [2, 3]

    def rms_sums(tt, par):
        sq = [carve(AR, 40960 + (par * 2 + i) * 1024, 1024, BF16) for i in range(2)]
        rt = carve(AR, 45056 + par * 2048, 2048, F32)
        rstd = carve(AR, 49152 + par * 2048, 2048, F32)
        bk = RB[par]
        for kc in range(8):
            s_ = kc % 2
            if False:
                P.act(lambda e, kc=kc, s_=s_: e.activation(out=sq[s_], in_=xT[:, kc, tt * 512:(tt + 1) * 512], func=AF.Square),
                      r=[("xT", kc, tt)], w=[("A", "sq", par, s_)])
            else:
                P.pool(lambda e, kc=kc, s_=s_: e.tensor_tensor(out=sq[s_], in0=xT[:, kc, tt * 512:(tt + 1) * 512],
                                                              in1=xT[:, kc, tt * 512:(tt + 1) * 512], op=ALU.mult),
                       r=[("xT", kc, tt)], w=[("A", "sq", par, s_)])
            P.pe(lambda e, kc=kc, s_=s_: e.matmul(psb[bk][:, :], onesb[:, :], sq[s_], start=(kc == 0), stop=(kc == 7)),
                 r=[("A", "sq", par, s_), "onesb"], w=PB(bk))
        P.act(lambda e: e.activation(out=rt, in_=psb[bk][:, :], func=AF.Sqrt, scale=1.0 / D, bias=EPS),
              r=PB(bk), w=[("A", "rt", par)])
        P.dve(lambda e: e.reciprocal(out=rstd, in_=rt), r=[("A", "rt", par)], w=[("A", "rstd", par)])
        return rstd

    modraw = sb("modraw", [128, 24], F32)

    def mod_chunk_dma(l, ci, Wa, tok):
        P.dma("pool", lambda e: e.dma_start(
            out=Wa, in_=wada_d[l, :, ci * 512:(ci + 1) * 512].rearrange("(k p) n -> p k n", p=128)), w=[tok])

    def mod_chunk_mm(l, ci, Wa, tok):
        sl = ci % 2
        b = nextbank()
        for kc in range(8):
            P.pe(lambda e, kc=kc: e.matmul(psb[b][0:1, :], cact[:, kc:kc + 1], Wa[:, kc, :],
                                          start=(kc == 0), stop=(kc == 7)),
                 r=[tok, "cact"], w=PB(b))
        P.dve(lambda e: e.tensor_copy(out=tmpR[0:1, sl, :], in_=psb[b][0:1, :]), r=PB(b), w=[("tmpR", sl)])
        for q in range(4):
            P.pe(lambda e, q=q: e.matmul(psb[2][:, q:q + 1], tmpR[0:1, sl, q * 128:(q + 1) * 128],
                                        one11[0:1, 0:1], start=True, stop=True),
                 r=[("tmpR", sl), "one11"], w=PB(2))
        P.dve(lambda e: e.tensor_copy(out=modraw[:, ci * 4:(ci + 1) * 4], in_=psb[2][:, 0:4]), r=PB(2), w=["modraw"])

    def mod_finish(l):
        mp = l % 2
        P.dve(lambda e: e.tensor_tensor(out=modT2[:, mp, :], in0=modraw[:, :], in1=vcols2[:, l * 24:(l + 1) * 24],
                                        op=ALU.add), r=["modraw", "vcols2"], w=[("modT", mp)])
        P.dve(lambda e: e.tensor_scalar(out=a1tmp[:], in0=modT2[:, mp, 8:16], scalar1=1.0, scalar2=None, op0=ALU.add),
              r=[("modT", mp)], w=["a1tmp"])
        P.dve(lambda e: e.tensor_tensor(out=Acol2[:, mp, :], in0=a1tmp[:], in1=vcols[:, l * 8:(l + 1) * 8], op=ALU.mult),
              r=["a1tmp", "vcols"], w=[("Acol", mp)])

    if True:
        P.arena_reset("W")
        for ci in range(6):
            s_ = ci % 2
            Wa = carve(WA, s_ * 8192, 8192, BF16, "p (k n) -> p k n", n=512)
            mod_chunk_dma(0, ci, Wa, ("W", "wa", s_))
            mod_chunk_mm(0, ci, Wa, ("W", "wa", s_))
        mod_finish(0)

    P.arena_reset("A")
    xst = [carve(AR, i * 4096, 4096, F32) for i in range(4)]
    for j in range(16):
        s = j % 4
        P.dma("sp", lambda e, j=j, s=s: e.dma_start(out=xst[s], in_=x_d[j * 128:(j + 1) * 128, :]),
              w=[("A", "xst", s)])
        for g in range(2):
            b = g
            for q in range(4):
                dc = g * 4 + q
                P.pe(lambda e, b=b, q=q, dc=dc, s=s: e.transpose(psb[b][:, q * 128:(q + 1) * 128],
                                                             xst[s][:, dc * 128:(dc + 1) * 128], ident[:, :]),
                     r=[("A", "xst", s), "ident"], w=PB(b))
            wtok = [("xT", 4 * g + q, j // 4) for q in range(4)]
            if g == 0:
                P.act(lambda e, b=b, g=g, j=j: e.activation(out=xT[:, 4 * g:4 * g + 4, j * 128:(j + 1) * 128],
                                                           in_=psb[b][:, :].rearrange("p (a c) -> p a c", c=128),
                                                           func=AF.Copy), r=PB(b), w=wtok)
            else:
                P.dve(lambda e, b=b, g=g, j=j: e.tensor_copy(out=xT[:, 4 * g:4 * g + 4, j * 128:(j + 1) * 128],
                                                            in_=psb[b][:, :].rearrange("p (a c) -> p a c", c=128)),
                      r=PB(b), w=wtok)


    for l in range(depth):
        mT = modT2[:, l % 2, :]
        mtok = ("modT", l % 2)

        P.arena_reset("A")
        xr = [carve(AR, i * 2048, 2048, F32) for i in range(2)]

        def rms_apply(tt, par, rstd):
            for kc in range(8):
                s_ = kc % 2
                P.dve(lambda e, kc=kc, s_=s_: e.tensor_tensor(
                    out=xr[s_], in0=xT[:, kc, tt * 512:(tt + 1) * 512], in1=rstd, op=ALU.mult),
                    r=[("xT", kc, tt), ("A", "rstd", par)], w=[("A", "xr", s_)])
                P.act(lambda e, kc=kc, s_=s_, mT=mT, lp=l % 2: e.activation(
                    out=hT[:, kc, tt * 512:(tt + 1) * 512], in_=xr[s_], func=AF.Identity,
                    scale=Acol2[:, lp, kc:kc + 1], bias=mT[:, kc:kc + 1]),
                    r=[("A", "xr", s_), ("Acol", l % 2), mtok], w=[("hT", tt)])

        r0 = rms_sums(0, 0)
        r1 = rms_sums(1, 1)
        rms_apply(0, 0, r0)
        r2 = rms_sums(2, 0)
        rms_apply(1, 1, r1)
        r3 = rms_sums(3, 1)
        rms_apply(2, 0, r2)
        rms_apply(3, 1, r3)

        P.arena_reset("A")
        P.arena_reset("W")
        P.arena_reset("B")
        P.arena_reset("WO")
        gT = carve(BB, 0, 8192, BF16, "p (w t) -> p w t", t=L)[0:16]
        waf = carve(BB, 8192, 2048, BF16, "p (w n) -> p w n", n=512)[0:16]
        Wg = carve(BB, 10240, 512, BF16, "p (k n) -> p k n", n=32)
        qT = carve(AR, 0, 4096, BF16)
        kT = carve(AR, 4096, 4096, BF16)
        vh = carve(AR, 8192, 8192, BF16, "p (j v) -> p j v", v=256)
        cum = carve(AR, 16384, 8192, F32)
        e1 = carve(AR, 24576, 8192, F32)
        ofwd = carve(AR, 32768, 8192, BF16, "p (j v) -> p j v", v=256)
        ygT = carve(AR, 40960, 8192, BF16, "p (f t) -> p f t", t=L)
        ygTf = carve(AR, 40960, 8192, F32)
        Pb = [carve(AR, 49152, 4096, BF16), carve(WO, 4096, 4096, BF16), carve(WA, 19968, 4096, BF16)]
        Ptag = [("A", "P1"), ("WO", "P2"), ("W", "P3")]
        Wqk = carve(WA, 0, 4096, BF16, "p (k n) -> p k n", n=256)
        Wv = carve(WA, 4096, 4096, BF16, "p (k n) -> p k n", n=256)
        Wr = carve(WA, 8192, 4096, BF16, "p (k n) -> p k n", n=256)
        Wo = carve(WO, 0, 4096, BF16, "p (f d) -> p f d", d=D)
        statef = [carve(WA, 12288 + i * 1024, 1024, F32) for i in range(2)]
        stateb = [carve(WA, 14336 + i * 512, 512, BF16) for i in range(2)]
        srt = [carve(WA, 15360 + i * 1024, 1024, F32) for i in range(2)]
        osq = [carve(WA, 17408 + i * 512, 512, BF16) for i in range(2)]
        og = [carve(WA, 18944 + i * 512, 512, BF16) for i in range(2)]
        cumv = cum.rearrange("p (n c) -> p n c", c=128)
        e1v = e1.rearrange("p (n c) -> p n c", c=128)
        ptk = lambda i, hh: Ptag[i] + (hh,)
        bothP = lambda i: [ptk(i, 0), ptk(i, 1)]

        def win_dma(dst, c0, n, tok, l=l):
            P.dma("pool", lambda e: e.dma_start(
                out=dst, in_=win_d[l, :, c0:c0 + n].rearrange("(k p) n -> p k n", p=128)), w=[tok])

        def load_wqk(h):
            win_dma(Wqk[:, :, 0:128], 2048 + h * 128, 128, ("W", "wqk"))
            win_dma(Wqk[:, :, 128:256], 2560 + h * 128, 128, ("W", "wqk"))

        def load_wv(h):
            win_dma(Wv, 3072 + h * 256, 256, ("W", "wv"))

        def load_wr(h):
            win_dma(Wr, 4096 + h * 256, 256, ("W", "wr"))

        def load_wo(h, l=l):
            P.dma("pool", lambda e: e.dma_start(
                out=Wo, in_=wout_d[l, 1024 + h * 256:1024 + (h + 1) * 256, :].rearrange("(f p) d -> p f d", p=128)),
                w=[("WO", "wo")])

        def proj_qk_tiles(h):
            tiles = []
            for which in range(2):
                dst = qT if which == 0 else kT
                for tt in range(4):
                    st = {}

                    def mm(which=which, tt=tt, st=st):
                        b = nextbank()
                        st["b"] = b
                        for kc in range(8):
                            P.pe(lambda e, kc=kc, b=b: e.matmul(
                                psb[b][:, :], Wqk[:, kc, which * 128:(which + 1) * 128], hT[:, kc, tt * 512:(tt + 1) * 512],
                                start=(kc == 0), stop=(kc == 7)), r=[("W", "wqk"), ("hT", tt)], w=PB(b))

                    def ev(which=which, tt=tt, st=st, dst=dst):
                        b = st["b"]
                        P.act(lambda e: e.activation(
                            out=dst[:, tt * 512:(tt + 1) * 512], in_=psb[b][:, :], func=AF.Copy,
                            scale=(128.0 ** -0.5 if which == 0 else 1.0)),
                            r=PB(b), w=[("A", "qk", which)])
                    tiles.append((mm, ev))
            return tiles

        def proj_qk(h):
            for mm, ev in proj_qk_tiles(h):
                mm()
                ev()

        def proj_v(h):
            for jj in range(8):
                b = nextbank()
                for sub in range(2):
                    j = jj * 2 + sub
                    for kc in range(8):
                        P.pe(lambda e, j=j, sub=sub, kc=kc, b=b: e.matmul(
                            psb[b][:, sub * 256:(sub + 1) * 256], hT[:, kc, j * 128:(j + 1) * 128], Wv[:, kc, :],
                            start=(kc == 0), stop=(kc == 7)), r=[("W", "wv"), ("hT", j // 4)], w=PB(b))
                P.dve(lambda e, jj=jj, b=b: e.tensor_copy(out=vh[:, 2 * jj:2 * jj + 2, :],
                                                         in_=psb[b][:, :].rearrange("p (a v) -> p a v", v=256)),
                      r=PB(b), w=[("A", "vh")])

        def stage1a_pre(h, dr):
            par = dr
            nbcol = (0 if dr == 0 else 16) + l * 4 + h
            for tt in range(4):
                b = nextbank()
                P.pe(lambda e, tt=tt, b=b: e.matmul(
                    psb[b][:, :], waf[0:16, dr, h * 128:(h + 1) * 128], gT[0:16, dr, tt * 512:(tt + 1) * 512],
                    start=True, stop=True), r=[("B", "waf"), ("B", "gT")], w=PB(b))
                P.act(lambda e, tt=tt, b=b: e.activation(
                    out=e1[:, tt * 512:(tt + 1) * 512], in_=psb[b][:, :], func=AF.Exp, scale=-1.0,
                    bias=vneg[:, nbcol:nbcol + 1]), r=PB(b) + ["vneg"], w=[("A", "e1")])
            P.act(lambda e: e.activation(out=e1[:, :], in_=e1[:, :], func=AF.Ln, bias=1.0),
                  r=[("A", "e1")], w=[("A", "e1")])

        def stage1a_post(h, dr):
            par = dr
            if dr == 0:
                P.dve(lambda e: e.tensor_tensor_scan(out=cum[:, :], data0=e1[:, :], data1=e1[:, :], initial=0.0,
                                                     op0=ALU.add, op1=ALU.max),
                      r=[("A", "e1")], w=[("A", "cum")])
                P.dve(lambda e: e.memset(rpos[:, 0:1], 0.0), w=["rpos"])
                P.dve(lambda e: e.tensor_copy(out=rpos[:, 1:17], in_=cum[:, 127:2048:128]),
                      r=[("A", "cum")], w=["rpos"])
            else:
                P.dve(lambda e: e.memset(cum[:, 0:1], 0.0), w=[("A", "cum")])
                P.dve(lambda e: e.tensor_tensor_scan(out=cum[:, 1:2048], data0=e1[:, 0:2047], data1=e1[:, 0:2047],
                                                     initial=0.0, op0=ALU.add, op1=ALU.max),
                      r=[("A", "e1")], w=[("A", "cum")])
                P.dve(lambda e: e.tensor_copy(out=rpos[:, 0:16], in_=cum[:, 0:2048:128]),
                      r=[("A", "cum")], w=["rpos"])
                P.dve(lambda e: e.tensor_copy(out=rpos[:, 16:17], in_=cum[:, 2047:2048]),
                      r=[("A", "cum")], w=["rpos"])
            P.dve(lambda e: e.tensor_tensor(out=dtmp[:], in0=rpos[:, 1:17], in1=rpos[:, 0:16], op=ALU.subtract),
                  r=["rpos"], w=["dtmp"])
            P.act(lambda e: e.activation(out=dcol2[:, par, :], in_=dtmp[:], func=AF.Exp, scale=-1.0 / 16),
                  r=["dtmp"], w=[("dcol", par)])
            P.dve(lambda e: e.tensor_tensor(out=e1v, in0=cumv, in1=rpos[:, 0:16].unsqueeze(2).to_broadcast([128, 16, 128]),
                                            op=ALU.subtract), r=[("A", "cum"), "rpos"], w=[("A", "e1")])

        def stage1b(h, dr):
            par = dr
            srcs = [qT, kT, kT] if dr == 0 else [kT, qT, qT]
            srct = [0, 1, 1] if dr == 0 else [1, 0, 0]
            P.act(lambda e: e.activation(out=cum[:, :], in_=e1[:, :], func=AF.Exp, scale=-1.0 / 16),
                  r=[("A", "e1")], w=[("A", "cum")])
            P.act(lambda e: e.activation(out=e1[:, :], in_=e1[:, :], func=AF.Exp, scale=1.0 / 16),
                  r=[("A", "e1")], w=[("A", "e1")])
            P.dve(lambda e, src=srcs[0]: e.tensor_tensor(out=Pb[0], in0=cum[:, :], in1=src[:, :], op=ALU.mult),
                  r=[("A", "cum"), ("A", "qk", srct[0])], w=bothP(0))
            P.dve(lambda e, src=srcs[1]: e.tensor_tensor(out=Pb[1], in0=e1[:, :], in1=src[:, :], op=ALU.mult),
                  r=[("A", "e1"), ("A", "qk", srct[1])], w=bothP(1))
            P.dve(lambda e: e.tensor_tensor(
                out=Pb[2].rearrange("p (n c) -> p n c", c=128), in0=Pb[1].rearrange("p (n c) -> p n c", c=128),
                in1=dcol2[:, par, 0:16].unsqueeze(2).to_broadcast([128, 16, 128]), op=ALU.mult),
                r=bothP(1) + [("dcol", par)], w=bothP(2))

        def stage2(h, dr, side=None):
            par = dr
            side = list(side) if side else []
            pend = []
            if dr == 0:
                kside, qside, interq, ksrc = 1, 0, 0, 2
            else:
                kside, qside, interq, ksrc = 0, 1, 2, 0
            mask = maskf if dr == 0 else maskb
            for hh in range(2):
                for cc in range(8):
                    n = hh * 8 + cc
                    bnk = 2 if cc < 4 else 6
                    P.pe(lambda e, n=n, cc=cc, bnk=bnk: e.matmul(
                        psb[bnk][:, (cc % 4) * 128:(cc % 4 + 1) * 128], Pb[kside][:, n * 128:(n + 1) * 128],
                        Pb[qside][:, n * 128:(n + 1) * 128], start=True, stop=True),
                        r=[ptk(kside, hh), ptk(qside, hh)], w=PB(bnk))
                for q4 in range(2):
                    bnk = 2 if q4 == 0 else 6
                    c0 = hh * 1024 + q4 * 512
                    P.dve(lambda e, bnk=bnk, c0=c0: e.tensor_tensor(
                        out=Pb[1][:, c0:c0 + 512].rearrange("p (a c) -> p a c", c=128),
                        in0=psb[bnk][:, :].rearrange("p (a c) -> p a c", c=128),
                        in1=mask[:, :].unsqueeze(1).to_broadcast([128, 4, 128]), op=ALU.mult),
                        r=PB(bnk) + ["maskf", "maskb"], w=[ptk(1, hh)])
            for g4 in range(4):
                hh = g4 // 2
                for cc in range(4):
                    n = g4 * 4 + cc
                    P.pe(lambda e, n=n, cc=cc, g4=g4: e.transpose(TRB[g4 % 2][0][:, cc * 128:(cc + 1) * 128],
                                                                Pb[ksrc][:, n * 128:(n + 1) * 128], identb[:, :]),
                         r=[ptk(ksrc, hh), "identb"], w=PB(TRB[g4 % 2][1]))
                P.act(lambda e, g4=g4: e.activation(out=Pb[ksrc][:, g4 * 512:(g4 + 1) * 512],
                                                    in_=TRB[g4 % 2][0][:, 0:512], func=AF.Copy),
                      r=PB(TRB[g4 % 2][1]), w=[ptk(ksrc, hh)])
            order = list(range(16)) if dr == 0 else list(range(15, -1, -1))

            def smm(ci):
                n = order[ci]
                bnk = PSS[ci % 3]
                P.pe(lambda e, n=n, bnk=bnk: e.matmul(psb[bnk][:, 0:256], Pb[ksrc][:, n * 128:(n + 1) * 128], vh[:, n, :],
                                                     start=True, stop=True),
                     r=[ptk(ksrc, n // 8), ("A", "vh")], w=PB(bnk))

            smm(0)
            smm(1)
            sidx = 0
            for ci, n in enumerate(order):
                first = (ci == 0)
                last = (ci == 15)
                sl = ci % 2
                hh = n // 8
                if ci + 2 <= 14:
                    smm(ci + 2)
                if pend:
                    pend.pop(0)()
                if side and ci % 4 == 0:
                    mm_, ev_ = side.pop(0)
                    mm_()
                    pend.append(ev_)
                P.pe(lambda e, n=n, first=first, sl=sl: e.matmul(psb[PSO[sl]][:, 0:256], Pb[1][:, n * 128:(n + 1) * 128],
                                                                vh[:, n, :], start=True, stop=first),
                     r=[ptk(1, hh), ("A", "vh")], w=PB(PSO[sl]))
                if not first:
                    P.pe(lambda e, n=n, sl=sl, sidx=sidx: e.matmul(
                        psb[PSO[sl]][:, 0:256], Pb[interq][:, n * 128:(n + 1) * 128], stateb[sidx],
                        start=False, stop=True),
                        r=[ptk(interq, hh), ("W", "stateb", sidx)], w=PB(PSO[sl]))
                if not last:
                    bnk = PSS[ci % 3]
                    nidx = 1 - sidx
                    if first:
                        P.dve(lambda e, nidx=nidx, bnk=bnk: e.tensor_copy(out=statef[nidx], in_=psb[bnk][:, 0:256]),
                              r=PB(bnk), w=[("W", "statef", nidx)])
                    else:
                        P.dve(lambda e, nidx=nidx, sidx=sidx, n=n, bnk=bnk: e.scalar_tensor_tensor(
                            out=statef[nidx], in0=statef[sidx], scalar=dcol2[:, par, n:n + 1],
                            in1=psb[bnk][:, 0:256], op0=ALU.mult, op1=ALU.add),
                            r=PB(bnk) + [("W", "statef", sidx), ("dcol", par)], w=[("W", "statef", nidx)])
                    P.act(lambda e, nidx=nidx: e.activation(out=stateb[nidx], in_=statef[nidx], func=AF.Copy),
                          r=[("W", "statef", nidx)], w=[("W", "stateb", nidx)])
                    sidx = nidx
                if dr == 0:
                    P.act(lambda e, n=n, sl=sl: e.activation(out=ofwd[:, n, :], in_=psb[PSO[sl]][:, 0:256], func=AF.Copy),
                          r=PB(PSO[sl]), w=[("A", "ofwd", n)])
                else:
                    P.dve(lambda e, n=n, sl=sl: e.tensor_tensor(out=ofwd[:, n, :], in0=psb[PSO[sl]][:, 0:256],
                                                               in1=ofwd[:, n, :], op=ALU.add),
                          r=PB(PSO[sl]) + [("A", "ofwd", n)], w=[("A", "ofwd", n)])
            flush_side(side, pend)

        def flush_side(side, pend):
            for ev_ in pend:
                ev_()
            for mm_, ev_ in side:
                mm_()
                ev_()

        def finalize_pre1(h):
            for hh in range(2):
                P.act(lambda e, hh=hh: e.activation(
                    out=Pb[hh], in_=ofwd[:, hh * 8:(hh + 1) * 8, :].rearrange("p j v -> p (j v)"), func=AF.Square),
                    r=[("A", "ofwd", n) for n in range(hh * 8, hh * 8 + 8)], w=bothP(hh))

        def finalize_pre2(h):
            for hh in range(2):
                P.dve(lambda e, hh=hh: e.tensor_reduce(out=ssq[:, hh * 8:(hh + 1) * 8],
                                                      in_=Pb[hh].rearrange("p (j v) -> p j v", v=256), axis=AX.X, op=ALU.add),
                      r=bothP(hh), w=["ssq"])
            P.act(lambda e: e.activation(out=ssr[:, :], in_=ssq[:, :], func=AF.Sqrt, scale=1.0 / 256, bias=EPS),
                  r=["ssq"], w=["ssr"])
            P.dve(lambda e: e.reciprocal(out=ssr2[:, :], in_=ssr[:, :]), r=["ssr"], w=["ssr2"])

        def finalize(h):
            gc = 32 + l * 8 + h * 2

            def rproj(n):
                sl = n % 2
                for kc in range(8):
                    P.pe(lambda e, n=n, kc=kc, sl=sl: e.matmul(psb[PSR[sl]][:, 0:256], hT[:, kc, n * 128:(n + 1) * 128],
                                                              Wr[:, kc, :], start=(kc == 0), stop=(kc == 7)),
                         r=[("W", "wr"), ("hT", n // 4)], w=PB(PSR[sl]))

            def gate(n):
                sl = n % 2
                P.act(lambda e, sl=sl: e.activation(out=srt[sl], in_=psb[PSR[sl]][:, 0:256], func=AF.Silu),
                      r=PB(PSR[sl]), w=[("W", "srt", sl)])
                P.dve(lambda e, n=n, sl=sl: e.scalar_tensor_tensor(out=og[sl], in0=ofwd[:, n, :], scalar=ssr2[:, n:n + 1],
                                                                  in1=srt[sl], op0=ALU.mult, op1=ALU.mult),
                      r=[("A", "ofwd", n), "ssr2", ("W", "srt", sl)], w=[("W", "og", sl)])

            def trans(n):
                sl = n % 2
                for f2 in range(2):
                    P.pe(lambda e, f2=f2, sl=sl: e.transpose(TRB[sl][0][:, f2 * 128:(f2 + 1) * 128],
                                                            og[sl][:, f2 * 128:(f2 + 1) * 128], identb[:, :]),
                         r=[("W", "og", sl), "identb"], w=PB(TRB[sl][1]))

            def evac(n):
                sl = n % 2
                for f2 in range(2):
                    P.act(lambda e, f2=f2, n=n, sl=sl: e.activation(
                        out=ygT[:, f2, n * 128:(n + 1) * 128], in_=TRB[sl][0][:, f2 * 128:(f2 + 1) * 128],
                        func=AF.Identity, scale=vcols[:, gc + f2:gc + f2 + 1]),
                        r=PB(TRB[sl][1]) + ["vcols"], w=[("A", "ygT", n // 4)])

            rproj(0)
            rproj(1)
            gate(0)
            for n in range(16):
                trans(n)
                if n + 2 < 16:
                    rproj(n + 2)
                if n + 1 < 16:
                    gate(n + 1)
                evac(n)

        def outproj(h):
            for dt in range(8):
                for tt in range(4):
                    b = nextobank([0, 1, 3, 4])
                    for f2 in range(2):
                        P.pe(lambda e, dt=dt, tt=tt, f2=f2, b=b: e.matmul(
                            psb[b][:, :], Wo[:, f2, dt * 128:(dt + 1) * 128], ygT[:, f2, tt * 512:(tt + 1) * 512],
                            start=(f2 == 0), stop=(f2 == 1)), r=[("WO", "wo"), ("A", "ygT", tt)], w=PB(b))
                    residual(dt, tt, b, mT, mtok)

        P.dma("pool", lambda e, l=l: e.dma_start(out=Wg, in_=win_d[l, :, 5120:5152].rearrange("(k p) n -> p k n", p=128)),
              w=[("B", "Wg")])
        P.dma("pool", lambda e, l=l: e.dma_start(out=waf[:, 0, :], in_=waf_d[l]), w=[("B", "waf")])
        P.dma("pool", lambda e, l=l: e.dma_start(out=waf[:, 1, :], in_=wab_d[l]), w=[("B", "waf")])
        load_wqk(0)
        load_wv(0)
        load_wr(0)
        load_wo(0)
        for which in range(2):
            for tt in range(4):
                b = nextbank()
                for kc in range(8):
                    P.pe(lambda e, which=which, tt=tt, kc=kc, b=b: e.matmul(
                        psb[b][0:16, :], Wg[:, kc, which * 16:(which + 1) * 16], hT[:, kc, tt * 512:(tt + 1) * 512],
                        start=(kc == 0), stop=(kc == 7)), r=[("B", "Wg"), ("hT", tt)], w=PB(b))
                P.act(lambda e, which=which, tt=tt, b=b: e.activation(out=gT[:, which, tt * 512:(tt + 1) * 512],
                                                                     in_=psb[b][0:16, :], func=AF.Copy),
                      r=PB(b), w=[("B", "gT")])
        stage1a_pre(0, 0)
        stage1a_post(0, 0)
        proj_qk(0)
        load_wqk(1)
        proj_v(0)
        load_wv(1)
        for h in range(4):
            stage1b(h, 0)
            stage1a_pre(h, 1)
            stage2(h, 0)
            stage1a_post(h, 1)
            stage1b(h, 1)
            if h < 3:
                stage1a_pre(h + 1, 0)
            stage2(h, 1, side=(proj_qk_tiles(h + 1) if h < 3 else None))
            if h < 2:
                load_wqk(h + 2)
            finalize_pre1(h)
            if h < 3:
                stage1a_post(h + 1, 0)
            finalize_pre2(h)
            if h < 3:
                proj_v(h + 1)
                if h < 2:
                    load_wv(h + 2)
            finalize(h)
            if h < 3:
                load_wr(h + 1)
            outproj(h)
            if h < 3:
                load_wo(h + 1)

        P.arena_reset("A")
        P.arena_reset("W")
        P.arena_reset("B")
        P.arena_reset("WO")
        MM = carve(BB, 0, 8192, BF16, "p (w g c d) -> p w g c d", w=2, g=4, c=2)
        wfm = carve(BB, 8192, 4096, BF16, "p (g c d) -> p g c d", g=4, c=2)
        P.dma("pool", lambda e, l=l: e.dma_start(out=wfm, in_=wfm_d[l].rearrange("g (c p) d -> p g c d", p=128)),
              w=[("B", "wfm")])
        for g in range(4):
            for cto in range(2):
                b = nextbank()
                for which in range(2):
                    for ctp in range(2):
                        P.pe(lambda e, g=g, cto=cto, which=which, ctp=ctp, b=b: e.matmul(
                            psb[b][:, which * 256:(which + 1) * 256], cs256[:, ctp, which, cto * 128:(cto + 1) * 128],
                            wfm[:, g, ctp, :], start=(ctp == 0), stop=(ctp == 1)),
                            r=["cs256", ("B", "wfm")], w=PB(b))
                P.act(lambda e, g=g, cto=cto, b=b: e.activation(
                    out=MM[:, :, g, cto, :], in_=psb[b][:, :].rearrange("p (w d) -> p w d", d=256), func=AF.Copy),
                    r=PB(b), w=[("B", "MM")])
        UT = carve(AR, 0, 4104, BF16)
        Up = carve(AR, 4104, 4096, BF16, "p (c t) -> p c t", t=1024)
        Um = carve(AR, 8200, 4096, BF16, "p (c t) -> p c t", t=1024)
        AB = carve(AR, 12296, 16384, BF16, "p (j w c) -> p j w c", w=2, c=512)
        zy = carve(AR, 28680, 16384, BF16, "p (c t) -> p c t", t=L)
        NR = 3
        ring = [carve(AR, 45064 + i * 2048, 2048, BF16, "p (w k) -> p w k", k=512) for i in range(NR)]
        P.pool(lambda e: e.memset(UT[:, 2048:2049], 0.0), w=[("A", "UTpad")])
        rcnt = 0
        for hf in range(2):
            Wu = carve(WA, 0, 8192, BF16, "p (k n) -> p k n", n=512)
            Wz = carve(WA, 8192, 8192, BF16, "p (k n) -> p k n", n=512)
            P.dma("pool", lambda e, hf=hf, l=l, Wu=Wu: e.dma_start(
                out=Wu, in_=win_d[l, :, hf * 512:(hf + 1) * 512].rearrange("(k p) n -> p k n", p=128)), w=[("W", "wu")])
            P.dma("pool", lambda e, hf=hf, l=l, Wz=Wz: e.dma_start(
                out=Wz, in_=win_d[l, :, 1024 + hf * 512:1024 + (hf + 1) * 512].rearrange("(k p) n -> p k n", p=128)),
                w=[("W", "wz")])
            Wo2 = carve(WO, 0, 8192, BF16, "p (f d) -> p f d", d=D)
            P.dma("pool", lambda e, hf=hf, l=l, Wo2=Wo2: e.dma_start(
                out=Wo2, in_=wout_d[l, hf * 512:(hf + 1) * 512, :].rearrange("(f p) d -> p f d", p=128)),
                w=[("WO", "wo2")])
            def zproj(c4):
                for tt in range(4):
                    b = nextbank()
                    for kc in range(8):
                        P.pe(lambda e, c4=c4, tt=tt, kc=kc, b=b: e.matmul(
                            psb[b][:, :], Wz[:, kc, c4 * 128:(c4 + 1) * 128], hT[:, kc, tt * 512:(tt + 1) * 512],
                            start=(kc == 0), stop=(kc == 7)), r=[("W", "wz"), ("hT", tt)], w=PB(b))
                    P.act(lambda e, c4=c4, tt=tt, b=b: e.activation(out=zy[:, c4, tt * 512:(tt + 1) * 512],
                                                                   in_=psb[b][:, :], func=AF.Silu),
                          r=PB(b), w=[("A", "zy", c4, tt)])

            for gl in range(2):
                g = hf * 2 + gl
                for ct in range(2):
                    c4 = gl * 2 + ct
                    for tt in range(4):
                        b = nextbank()
                        for kc in range(8):
                            P.pe(lambda e, c4=c4, tt=tt, kc=kc, b=b, Wu=Wu: e.matmul(
                                psb[b][:, :], Wu[:, kc, c4 * 128:(c4 + 1) * 128], hT[:, kc, tt * 512:(tt + 1) * 512],
                                start=(kc == 0), stop=(kc == 7)), r=[("W", "wu"), ("hT", tt)], w=PB(b))
                        P.act(lambda e, tt=tt, b=b: e.activation(out=UT[:, tt * 512:(tt + 1) * 512], in_=psb[b][:, :],
                                                                 func=AF.Copy), r=PB(b), w=[("A", "UT", tt)])
                    zproj(c4)
                    UTr = [("A", "UT", tt) for tt in range(4)] + [("A", "UTpad")]
                    P.dve(lambda e, ct=ct: e.tensor_tensor(out=Up[:, ct, :], in0=UT[:, 0:1024], in1=UT[:, 2048:1024:-1],
                                                          op=ALU.add), r=UTr, w=[("A", "Up", ct)])
                    P.pool(lambda e, ct=ct: e.tensor_tensor(out=Um[:, ct, :], in0=UT[:, 0:1024], in1=UT[:, 2048:1024:-1],
                                                           op=ALU.subtract), r=UTr, w=[("A", "Um", ct)])
                    P.dve(lambda e, ct=ct: e.tensor_copy(out=u1024[:, ct:ct + 1], in_=UT[:, 1024:1025]),
                          r=UTr, w=[("u1024", ct)])
                for j in range(8):
                    b = nextbank()
                    for which in range(2):
                        src = Up if which == 0 else Um
                        for ct in range(2):
                            P.pe(lambda e, j=j, which=which, ct=ct, b=b, src=src, g=g: e.matmul(
                                psb[b][:, which * 256:(which + 1) * 256], src[:, ct, j * 128:(j + 1) * 128],
                                MM[:, which, g, ct, :], start=(ct == 0), stop=(ct == 1)),
                                r=[("A", "Up", ct), ("A", "Um", ct), ("B", "MM")], w=PB(b))
                    P.act(lambda e, j=j, b=b, gl=gl: e.activation(
                        out=AB[:, j, :, gl * 256:(gl + 1) * 256], in_=psb[b][:, :].rearrange("p (w d) -> p w d", d=256),
                        func=AF.Copy), r=PB(b), w=[("A", "AB", j)])
                for ct in range(2):
                    P.pe(lambda e, ct=ct, g=g: e.matmul(psb[2][0:1, 0:256], u1024[:, ct:ct + 1], MM[:, 0, g, ct, :],
                                                       start=(ct == 0), stop=(ct == 1)),
                         r=[("u1024", ct), ("B", "MM")], w=PB(2))
                P.act(lambda e, gl=gl: e.activation(out=a1024[0:1, gl * 256:(gl + 1) * 256], in_=psb[2][0:1, 0:256],
                                                    func=AF.Copy), r=PB(2), w=["a1024"])
            for kt in range(4):
                hci = hf * 4 + kt
                hoist = (l + 1 < depth) and hci < 6
                if hoist:
                    WaH = carve(WA, 16384, 8192, BF16, "p (k n) -> p k n", n=512)
                    mod_chunk_dma(l + 1, hci, WaH, ("W", "waH"))
                for j in range(8):
                    rs = rcnt % NR
                    rcnt += 1
                    P.dma("sp", lambda e, j=j, kt=kt, rs=rs: e.dma_start(out=ring[rs], in_=tab_d[j, :, kt]),
                          w=[("A", "ring", rs)])
                    for c4 in range(4):
                        for which in range(2):
                            P.pe(lambda e, j=j, c4=c4, which=which, rs=rs: e.matmul(
                                psb[3 + c4][:, :], AB[:, j, which, c4 * 128:(c4 + 1) * 128], ring[rs][:, which, :],
                                start=(j == 0 and which == 0), stop=False),
                                r=[("A", "AB", j), ("A", "ring", rs)], w=PB(3 + c4))
                for c4 in range(4):
                    P.pe(lambda e, c4=c4, kt=kt: e.matmul(psb[3 + c4][:, :], a1024[0:1, c4 * 128:(c4 + 1) * 128],
                                                         alt[0:1, kt * 512:(kt + 1) * 512], start=False, stop=True),
                         r=["a1024", "alt"], w=PB(3 + c4))
                    P.dve(lambda e, c4=c4, kt=kt: e.tensor_tensor(
                        out=zy[:, c4, kt * 512:(kt + 1) * 512], in0=psb[3 + c4][:, :], in1=zy[:, c4, kt * 512:(kt + 1) * 512],
                        op=ALU.mult), r=PB(3 + c4) + [("A", "zy", c4, kt)], w=[("A", "zy", c4, kt)])
                if hoist:
                    mod_chunk_mm(l + 1, hci, WaH, ("W", "waH"))
            if hf == 1 and l + 1 < depth:
                mod_finish(l + 1)
            for dt in range(8):
                for tt in range(4):
                    b = nextobank([0, 1, 3, 4])
                    for c4 in range(4):
                        P.pe(lambda e, dt=dt, tt=tt, c4=c4, b=b, Wo2=Wo2: e.matmul(
                            psb[b][:, :], Wo2[:, c4, dt * 128:(dt + 1) * 128], zy[:, c4, tt * 512:(tt + 1) * 512],
                            start=(c4 == 0), stop=(c4 == 3)), r=[("WO", "wo2"), ("A", "zy", c4, tt)], w=PB(b))
                    residual(dt, tt, b, mT, mtok)

    P.arena_reset("A")
    onT = carve(AR, 0, 16384, F32, "p (k t) -> p k t", t=512)
    ostage = [carve(AR, 16384 + i * 4096, 4096, F32) for i in range(4)]
    xr2 = [carve(AR, 32768 + i * 2048, 2048, F32) for i in range(2)]
    finals = []
    rs_ = [None] * 4
    rs_[0] = rms_sums(0, 0)
    rs_[1] = rms_sums(1, 1)
    for tt in range(4):
        rstd = rs_[tt]
        for kc in range(8):
            s = kc % 2
            P.dve(lambda e, kc=kc, tt=tt, s=s, rstd=rstd: e.tensor_tensor(
                out=xr2[s], in0=xT[:, kc, tt * 512:(tt + 1) * 512], in1=rstd, op=ALU.mult),
                r=[("xT", kc, tt), ("A", "rstd", tt % 2)], w=[("A", "xr2", s)])
            P.act(lambda e, kc=kc, s=s: e.activation(out=onT[:, kc, :], in_=xr2[s], func=AF.Identity,
                                                     scale=vcols[:, 96 + kc:97 + kc]),
                  r=[("A", "xr2", s), "vcols"], w=[("A", "onT", kc)])
        if tt + 2 < 4:
            rs_[tt + 2] = rms_sums(tt + 2, tt % 2)
        for jj in range(4):
            j = tt * 4 + jj
            s = j % 4
            for g in range(2):
                b = nextbank()
                for q in range(4):
                    kc = g * 4 + q
                    P.pe(lambda e, b=b, q=q, kc=kc, jj=jj: e.transpose(psb[b][:, q * 128:(q + 1) * 128],
                                                                     onT[:, kc, jj * 128:(jj + 1) * 128], ident[:, :]),
                         r=[("A", "onT", kc), "ident"], w=PB(b))
                if g == 0:
                    P.act(lambda e, b=b, s=s: e.activation(out=ostage[s][:, 0:512], in_=psb[b][:, :], func=AF.Copy),
                          r=PB(b), w=[("A", "ost", s, 0)])
                else:
                    P.dve(lambda e, b=b, s=s: e.tensor_copy(out=ostage[s][:, 512:1024], in_=psb[b][:, :]),
                          r=PB(b), w=[("A", "ost", s, 1)])
            finals.append(P.dma("sp", lambda e, j=j, s=s: e.dma_start(out=y_d[j * 128:(j + 1) * 128, :], in_=ostage[s]),
                                r=[("A", "ost", s, 0), ("A", "ost", s, 1)]))
    P.emit(final_waits=finals)
    es.close()
    return nc


_NC_CACHE = {}


def _prep_inputs(x, c, norm_g, w_ada, b_ada, w_in, w_fmap, w_af, b_af, w_ab, b_ab, gla_norm_g, w_out, final_g):
    f = lambda a: np.ascontiguousarray(np.asarray(a, dtype=np.float32))
    cst = _consts()
    shared = {
        "b_ada": f(b_ada).reshape(96, 128),
        "w_ada": f(w_ada), "w_in": f(w_in), "w_fmap": f(w_fmap), "w_af": f(w_af), "w_ab": f(w_ab), "w_out": f(w_out),
    }
    shared.update(cst)
    x = f(x)
    c = f(c)
    in_maps = []
    for b in range(8):
        vrows = np.concatenate([f(norm_g).reshape(32, 128), f(gla_norm_g).reshape(32, 128), f(b_af).reshape(16, 128),
                                f(b_ab).reshape(16, 128), f(final_g).reshape(8, 128), c[b].reshape(8, 128)], axis=0)
        m = dict(shared)
        m["x"] = x[b]
        m["vrows"] = np.ascontiguousarray(vrows)
        in_maps.append(m)
    return in_maps


def kernel(x, c, norm_g, w_ada, b_ada, w_in, w_fmap, w_af, b_af, w_ab, b_ab, gla_norm_g, w_out, final_g):
    in_maps = _prep_inputs(x, c, norm_g, w_ada, b_ada, w_in, w_fmap, w_af, b_af, w_ab, b_ab, gla_norm_g, w_out, final_g)
    if "nc" not in _NC_CACHE:
        _NC_CACHE["nc"] = build_nc(DEPTH)
    nc = _NC_CACHE["nc"]
    res = run_bass_kernel_spmd(nc, in_maps, core_ids=list(range(8)))
    out = np.stack([np.asarray(r["y"], dtype=np.float32) for r in res.results], axis=0)
    return out
```

```python
import contextlib
import numpy as np
import ml_dtypes
import concourse.bass as bass
import concourse.mybir as mybir
from concourse.bass_utils import run_bass_kernel_spmd

F32 = mybir.dt.float32
BF16 = mybir.dt.bfloat16
AF = mybir.ActivationFunctionType
ALU = mybir.AluOpType
AX = mybir.AxisListType

L = 2048
D = 1024
DEPTH = 4
DIN = 5152
EPS = 1e-6
ENGS = ["pe", "act", "dve", "pool", "sp"]
UNIQUE_POOL_SEMS = False


class Op:
    __slots__ = ("fn", "deps", "dma", "sig", "cnt")

    def __init__(self, fn, deps, dma):
        self.fn = fn
        self.deps = deps
        self.dma = dma
        self.sig = False
        self.cnt = 0


class Prog:
    def __init__(self, nc, n_chan=24):
        self.nc = nc
        self.ops = {e: [] for e in ENGS}
        self.tok = {}
        self.n_chan = n_chan
        self.chan_cnt = [0] * n_chan
        self.rr = 0
        self.rr_pool = 0
        self.arena_base = {}

    def _st(self, t):
        st = self.tok.get(t)
        if st is None:
            base = []
            if isinstance(t, tuple) and t[0] in self.arena_base:
                base = list(self.arena_base[t[0]])
            st = {"w": None, "r": base}
            self.tok[t] = st
        return st

    def arena_reset(self, tag):
        ev = set(self.arena_base.get(tag, []))
        for t in list(self.tok.keys()):
            if isinstance(t, tuple) and t[0] == tag:
                st = self.tok.pop(t)
                if st["w"] is not None:
                    ev.add(st["w"])
                ev.update(st["r"])
        best = {}
        for e in ev:
            key = (e[0], e[1])
            if key not in best or e[2] > best[key][2]:
                best[key] = e
        self.arena_base[tag] = list(best.values())

    def add(self, eng, fn, reads=(), writes=(), dma=False, chan=None):
        idx = len(self.ops[eng])
        deps = set()
        d = None
        if dma:
            if chan is None and eng == "pool" and UNIQUE_POOL_SEMS:
                chan = len(self.chan_cnt)
                self.chan_cnt.append(0)
            elif chan is None:
                half = self.n_chan // 2
                if eng == "pool":
                    chan = half + self.rr_pool
                    self.rr_pool = (self.rr_pool + 1) % (self.n_chan - half)
                else:
                    chan = self.rr
                    self.rr = (self.rr + 1) % half
            k = self.chan_cnt[chan]
            self.chan_cnt[chan] += 1
            d = (chan, k)
            if k > 0:
                deps.add(("dma", chan, k - 1))
            me = ("dma", chan, k)
        else:
            me = ("eng", eng, idx)
        for t in reads:
            st = self._st(t)
            if st["w"] is not None:
                deps.add(st["w"])
        for t in writes:
            st = self._st(t)
            if st["w"] is not None:
                deps.add(st["w"])
            deps.update(st["r"])
        for t in reads:
            self.tok[t]["r"].append(me)
        for t in writes:
            st = self.tok[t]
            st["w"] = me
            st["r"] = []
        deps.discard(me)
        self.ops[eng].append(Op(fn, deps, d))
        return me

    def pe(self, fn, r=(), w=()):
        return self.add("pe", fn, r, w)

    def act(self, fn, r=(), w=()):
        return self.add("act", fn, r, w)

    def dve(self, fn, r=(), w=()):
        return self.add("dve", fn, r, w)

    def pool(self, fn, r=(), w=()):
        return self.add("pool", fn, r, w)

    def dma(self, q, fn, r=(), w=(), chan=None):
        return self.add(q, fn, r, w, dma=True, chan=chan)

    def _skip(self, e, i, dep):
        _, de, di = dep
        if de != e:
            return False
        if e == "pe":
            return True
        return (i - di) > 3

    def emit(self, final_waits=()):
        nc = self.nc
        for e in ENGS:
            for i, op in enumerate(self.ops[e]):
                for dep in op.deps:
                    if dep[0] == "eng" and not self._skip(e, i, dep):
                        self.ops[dep[1]][dep[2]].sig = True
        for e in ENGS:
            c = 0
            for op in self.ops[e]:
                if op.sig and op.dma is None:
                    c += 1
                op.cnt = c
        stack = contextlib.ExitStack()
        sems = {e: stack.enter_context(nc.semaphore("s_" + e)) for e in ENGS}
        dsems = [stack.enter_context(nc.semaphore("d_%d" % i)) for i in range(len(self.chan_cnt))]
        block = stack.enter_context(nc.Block())

        def run_engine(e, eng):
            seen = {}
            for i, op in enumerate(self.ops[e]):
                need = {}
                for dep in op.deps:
                    if dep[0] == "eng":
                        if self._skip(e, i, dep):
                            continue
                        key = ("e", dep[1])
                        val = self.ops[dep[1]][dep[2]].cnt
                    else:
                        key = ("d", dep[1])
                        val = 16 * (dep[2] + 1)
                    if val > need.get(key, 0):
                        need[key] = val
                for key, val in need.items():
                    if seen.get(key, 0) >= val:
                        continue
                    seen[key] = val
                    s = sems[key[1]] if key[0] == "e" else dsems[key[1]]
                    eng.wait_ge(s, val)
                ins = op.fn(eng)
                if op.dma is not None:
                    ins.then_inc(dsems[op.dma[0]], 16)
                elif op.sig:
                    ins.then_inc(sems[e], 1)
            if e == "sp":
                for ev in final_waits:
                    eng.wait_ge(dsems[ev[1]], 16 * (ev[2] + 1))

        @block.tensor
        def _(eng):
            run_engine("pe", eng)

        @block.scalar
        def _(eng):
            run_engine("act", eng)

        @block.vector
        def _(eng):
            run_engine("dve", eng)

        @block.gpsimd
        def _(eng):
            run_engine("pool", eng)

        @block.sync
        def _(eng):
            run_engine("sp", eng)

        stack.close()


_CONSTS = None


def _consts():
    global _CONSTS
    if _CONSTS is not None:
        return _CONSTS
    bf = ml_dtypes.bfloat16
    l = np.arange(1024, dtype=np.int64)[:, None]
    k = np.arange(2048, dtype=np.int64)[None, :]
    ang = 2.0 * np.pi * ((l * k) % 2048).astype(np.float64) / 2048.0
    sc = 1.0 / np.sqrt(2048.0)
    cosT = np.cos(ang) * sc
    nsinT = -np.sin(ang) * sc
    tab = np.stack([cosT, nsinT], axis=1)
    tab = tab.reshape(8, 128, 2, 4, 512).transpose(0, 1, 3, 2, 4)
    tab = np.ascontiguousarray(tab).astype(bf)
    alt = (((-1.0) ** np.arange(2048)) * sc).reshape(1, 2048).astype(bf)
    c = np.arange(256, dtype=np.int64)
    a2 = 2.0 * np.pi * ((c[:, None] * c[None, :]) % 256).astype(np.float64) / 256.0
    cs = np.stack([np.cos(a2) / 16.0, np.sin(a2) / 16.0], axis=1)
    cs = cs.reshape(2, 128, 2, 256).transpose(1, 0, 2, 3)
    cs256 = np.ascontiguousarray(cs).astype(bf)
    ident = np.eye(128, dtype=np.float32)
    identb = np.eye(128).astype(bf)
    s = np.arange(128)[:, None]
    t = np.arange(128)[None, :]
    maskf = (t >= s).astype(np.float32).astype(bf)
    maskb = (s > t).astype(np.float32).astype(bf)
    onesb = np.ones((128, 128), dtype=np.float32).astype(bf)
    one11 = np.ones((1, 1), dtype=np.float32)
    _CONSTS = dict(tab=tab, alt=alt, cs256=cs256, ident=ident, identb=identb, maskf=maskf,
                   maskb=maskb, onesb=onesb, one11=one11)
    return _CONSTS


def build_nc(depth=DEPTH):
    nc = bass.Bass("TRN2", target_bir_lowering=False)

    def din(name, shape, dt=F32):
        return nc.dram_tensor(name, list(shape), dt, kind="ExternalInput").ap()

    x_d = din("x", [L, D])
    vrows_d = din("vrows", [112, 128])
    bada_d = din("b_ada", [96, 128])
    wada_d = din("w_ada", [DEPTH, D, 3072])
    win_d = din("w_in", [DEPTH, D, DIN])
    wfm_d = din("w_fmap", [DEPTH, 4, 256, 256])
    waf_d = din("w_af", [DEPTH, 16, 512])
    wab_d = din("w_ab", [DEPTH, 16, 512])
    wout_d = din("w_out", [DEPTH, 2048, D])
    tab_d = din("tab", [8, 128, 4, 2, 512], BF16)
    alt_d = din("alt", [1, 2048], BF16)
    cs_d = din("cs256", [128, 2, 2, 256], BF16)
    ident_d = din("ident", [128, 128])
    identb_d = din("identb", [128, 128], BF16)
    maskf_d = din("maskf", [128, 128], BF16)
    maskb_d = din("maskb", [128, 128], BF16)
    onesb_d = din("onesb", [128, 128], BF16)
    one11_d = din("one11", [1, 1])
    y_d = nc.dram_tensor("y", [L, D], F32, kind="ExternalOutput").ap()

    es = contextlib.ExitStack()

    def sb(name, shape, dt):
        return es.enter_context(nc.sbuf_tensor("sb_" + name, list(shape), dt))

    xT = sb("xT", [128, 8, L], F32)
    hT = sb("hT", [128, 8, L], BF16)
    AR = sb("AR", [128, 13312], F32)
    WA = sb("WA", [128, 6144], F32)
    WO = sb("WO", [128, 2048], F32)
    BB = sb("BB", [128, 3072], F32)
    cs256 = sb("cs256", [128, 2, 2, 256], BF16)
    ident = sb("ident", [128, 128], F32)
    identb = sb("identb", [128, 128], BF16)
    maskf = sb("maskf", [128, 128], BF16)
    maskb = sb("maskb", [128, 128], BF16)
    onesb = sb("onesb", [128, 128], BF16)
    one11 = sb("one11", [1, 1], F32)
    alt = sb("alt", [1, 2048], BF16)
    vrows = sb("vrows", [112, 128], F32)
    vcols = sb("vcols", [128, 112], F32)
    vneg = sb("vneg", [128, 32], F32)
    cact = sb("cact", [128, 8], BF16)
    vrows2 = sb("vrows2", [96, 128], F32)
    vcols2 = sb("vcols2", [128, 96], F32)
    modT2 = sb("modT2", [128, 2, 24], F32)
    Acol2 = sb("Acol2", [128, 2, 8], F32)
    a1tmp = sb("a1tmp", [128, 8], F32)
    rpos = sb("rpos", [128, 17], F32)
    rneg = sb("rneg", [128, 17], F32)
    dtmp = sb("dtmp", [128, 16], F32)
    dcol2 = sb("dcol2", [128, 2, 16], F32)
    ssr2 = sb("ssr2", [128, 16], F32)
    ssq = sb("ssq", [128, 16], F32)
    ssr = sb("ssr", [128, 16], F32)
    a1024 = sb("a1024", [1, 512], BF16)
    u1024 = sb("u1024", [128, 2], BF16)

    psb = [es.enter_context(nc.psum_tensor("ps%d" % i, [128, 512], F32)) for i in range(7)]
    pst = es.enter_context(nc.psum_tensor("pst", [128, 1024], BF16))

    P = Prog(nc)

    def PB(b):
        return [("ps", b)]

    PSS = [4, 6, 5]
    PSO = [3, 2]
    PSR = [5, 4]
    ps6b = psb[6][:, :].bitcast(BF16)
    TRB = [(pst, 7), (ps6b, 6)]

    def carve(base, off_b, nbytes, dt, pattern=None, **kw):
        v = base[:, off_b // 4:(off_b + nbytes) // 4]
        if dt != F32:
            v = v.bitcast(dt)
        if pattern is not None:
            v = v.rearrange(pattern, **kw)
        return v

    K = 1024
    for (dst, src, tokn) in [(ident, ident_d, "ident"), (identb, identb_d, "identb"), (maskf, maskf_d, "maskf"),
                             (maskb, maskb_d, "maskb"), (onesb, onesb_d, "onesb"), (one11, one11_d, "one11"),
                             (alt, alt_d, "alt"), (vrows, vrows_d, "vrows"), (vrows2, bada_d, "vrows2")]:
        P.dma("sp", lambda e, dst=dst, src=src: e.dma_start(out=dst[:], in_=src), w=[tokn])
    P.dma("sp", lambda e: e.dma_start(out=cs256[:], in_=cs_d), w=["cs256"])
    P.pe(lambda e: e.transpose(psb[2][:, 0:112], vrows[:, :], ident[0:112, 0:112]), r=["vrows", "ident"], w=PB(2))
    P.dve(lambda e: e.tensor_copy(out=vcols[:], in_=psb[2][:, 0:112]), r=PB(2), w=["vcols"])
    P.pe(lambda e: e.transpose(psb[2][:, 0:96], vrows2[:, :], ident[0:96, 0:96]), r=["vrows2", "ident"], w=PB(2))
    P.dve(lambda e: e.tensor_copy(out=vcols2[:], in_=psb[2][:, 0:96]), r=PB(2), w=["vcols2"])
    P.dve(lambda e: e.tensor_scalar(out=vneg[:], in0=vcols[:, 64:96], scalar1=-1.0, scalar2=None, op0=ALU.mult),
          r=["vcols"], w=["vneg"])
    P.act(lambda e: e.activation(out=cact[:], in_=vcols[:, 104:112], func=AF.Silu), r=["vcols"], w=["cact"])

    tmpR = sb("tmpR", [128, 2, 512], F32)
    rescnt = [0]

    def residual(dt, tt, b, mT, mtok):
        i = rescnt[0]
        rescnt[0] += 1
        if True:
            P.dve(lambda e: e.scalar_tensor_tensor(
                out=xT[:, dt, tt * 512:(tt + 1) * 512], in0=psb[b][:, :], scalar=mT[:, 16 + dt:17 + dt],
                in1=xT[:, dt, tt * 512:(tt + 1) * 512], op0=ALU.mult, op1=ALU.add),
                r=PB(b) + [mtok, ("xT", dt, tt)], w=[("xT", dt, tt)])
        else:
            sl = (i // 2) % 2
            P.act(lambda e: e.activation(out=tmpR[:, sl, :], in_=psb[b][:, :], func=AF.Identity,
                                         scale=mT[:, 16 + dt:17 + dt]),
                  r=PB(b) + [mtok], w=[("tmpR", sl)])
            P.pool(lambda e: e.tensor_tensor(out=xT[:, dt, tt * 512:(tt + 1) * 512], in0=xT[:, dt, tt * 512:(tt + 1) * 512],
                                             in1=tmpR[:, sl, :], op=ALU.add),
                   r=[("tmpR", sl), ("xT", dt, tt)], w=[("xT", dt, tt)])

    projbank = [0]

    def nextbank():
        projbank[0] ^= 1
        return projbank[0]

    obank = [0]

    def nextobank(banks):
        obank[0] = (obank[0] + 1) % len(banks)
        return banks[obank[0]]

    RB = [2, 3]

    def rms_sums(tt, par):
        sq = [carve(AR, 40960 + (par * 2 + i) * 1024, 1024, BF16) for i in range(2)]
        rt = carve(AR, 45056 + par * 2048, 2048, F32)
        rstd = carve(AR, 49152 + par * 2048, 2048, F32)
        bk = RB[par]
        for kc in range(8):
            s_ = kc % 2
            if False:
                P.act(lambda e, kc=kc, s_=s_: e.activation(out=sq[s_], in_=xT[:, kc, tt * 512:(tt + 1) * 512], func=AF.Square),
                      r=[("xT", kc, tt)], w=[("A", "sq", par, s_)])
            else:
                P.pool(lambda e, kc=kc, s_=s_: e.tensor_tensor(out=sq[s_], in0=xT[:, kc, tt * 512:(tt + 1) * 512],
                                                              in1=xT[:, kc, tt * 512:(tt + 1) * 512], op=ALU.mult),
                       r=[("xT", kc, tt)], w=[("A", "sq", par, s_)])
            P.pe(lambda e, kc=kc, s_=s_: e.matmul(psb[bk][:, :], onesb[:, :], sq[s_], start=(kc == 0), stop=(kc == 7)),
                 r=[("A", "sq", par, s_), "onesb"], w=PB(bk))
        P.act(lambda e: e.activation(out=rt, in_=psb[bk][:, :], func=AF.Sqrt, scale=1.0 / D, bias=EPS),
              r=PB(bk), w=[("A", "rt", par)])
        P.dve(lambda e: e.reciprocal(out=rstd, in_=rt), r=[("A", "rt", par)], w=[("A", "rstd", par)])
        return rstd

    modraw = sb("modraw", [128, 24], F32)

    def mod_chunk_dma(l, ci, Wa, tok):
        P.dma("pool", lambda e: e.dma_start(
            out=Wa, in_=wada_d[l, :, ci * 512:(ci + 1) * 512].rearrange("(k p) n -> p k n", p=128)), w=[tok])

    def mod_chunk_mm(l, ci, Wa, tok):
        sl = ci % 2
        b = nextbank()
        for kc in range(8):
            P.pe(lambda e, kc=kc: e.matmul(psb[b][0:1, :], cact[:, kc:kc + 1], Wa[:, kc, :],
                                          start=(kc == 0), stop=(kc == 7)),
                 r=[tok, "cact"], w=PB(b))
        P.dve(lambda e: e.tensor_copy(out=tmpR[0:1, sl, :], in_=psb[b][0:1, :]), r=PB(b), w=[("tmpR", sl)])
        for q in range(4):
            P.pe(lambda e, q=q: e.matmul(psb[2][:, q:q + 1], tmpR[0:1, sl, q * 128:(q + 1) * 128],
                                        one11[0:1, 0:1], start=True, stop=True),
                 r=[("tmpR", sl), "one11"], w=PB(2))
        P.dve(lambda e: e.tensor_copy(out=modraw[:, ci * 4:(ci + 1) * 4], in_=psb[2][:, 0:4]), r=PB(2), w=["modraw"])

    def mod_finish(l):
        mp = l % 2
        P.dve(lambda e: e.tensor_tensor(out=modT2[:, mp, :], in0=modraw[:, :], in1=vcols2[:, l * 24:(l + 1) * 24],
                                        op=ALU.add), r=["modraw", "vcols2"], w=[("modT", mp)])
        P.dve(lambda e: e.tensor_scalar(out=a1tmp[:], in0=modT2[:, mp, 8:16], scalar1=1.0, scalar2=None, op0=ALU.add),
              r=[("modT", mp)], w=["a1tmp"])
        P.dve(lambda e: e.tensor_tensor(out=Acol2[:, mp, :], in0=a1tmp[:], in1=vcols[:, l * 8:(l + 1) * 8], op=ALU.mult),
              r=["a1tmp", "vcols"], w=[("Acol", mp)])

    if True:
        P.arena_reset("W")
        for ci in range(6):
            s_ = ci % 2
            Wa = carve(WA, s_ * 8192, 8192, BF16, "p (k n) -> p k n", n=512)
            mod_chunk_dma(0, ci, Wa, ("W", "wa", s_))
            mod_chunk_mm(0, ci, Wa, ("W", "wa", s_))
        mod_finish(0)

    P.arena_reset("A")
    xst = [carve(AR, i * 4096, 4096, F32) for i in range(4)]
    for j in range(16):
        s = j % 4
        P.dma("sp", lambda e, j=j, s=s: e.dma_start(out=xst[s], in_=x_d[j * 128:(j + 1) * 128, :]),
              w=[("A", "xst", s)])
        for g in range(2):
            b = g
            for q in range(4):
                dc = g * 4 + q
                P.pe(lambda e, b=b, q=q, dc=dc, s=s: e.transpose(psb[b][:, q * 128:(q + 1) * 128],
                                                             xst[s][:, dc * 128:(dc + 1) * 128], ident[:, :]),
                     r=[("A", "xst", s), "ident"], w=PB(b))
            wtok = [("xT", 4 * g + q, j // 4) for q in range(4)]
            if g == 0:
                P.act(lambda e, b=b, g=g, j=j: e.activation(out=xT[:, 4 * g:4 * g + 4, j * 128:(j + 1) * 128],
                                                           in_=psb[b][:, :].rearrange("p (a c) -> p a c", c=128),
                                                           func=AF.Copy), r=PB(b), w=wtok)
            else:
                P.dve(lambda e, b=b, g=g, j=j: e.tensor_copy(out=xT[:, 4 * g:4 * g + 4, j * 128:(j + 1) * 128],
                                                            in_=psb[b][:, :].rearrange("p (a c) -> p a c", c=128)),
                      r=PB(b), w=wtok)


    for l in range(depth):
        mT = modT2[:, l % 2, :]
        mtok = ("modT", l % 2)

        P.arena_reset("A")
        xr = [carve(AR, i * 2048, 2048, F32) for i in range(2)]

        def rms_apply(tt, par, rstd):
            for kc in range(8):
                s_ = kc % 2
                P.dve(lambda e, kc=kc, s_=s_: e.tensor_tensor(
                    out=xr[s_], in0=xT[:, kc, tt * 512:(tt + 1) * 512], in1=rstd, op=ALU.mult),
                    r=[("xT", kc, tt), ("A", "rstd", par)], w=[("A", "xr", s_)])
                P.act(lambda e, kc=kc, s_=s_, mT=mT, lp=l % 2: e.activation(
                    out=hT[:, kc, tt * 512:(tt + 1) * 512], in_=xr[s_], func=AF.Identity,
                    scale=Acol2[:, lp, kc:kc + 1], bias=mT[:, kc:kc + 1]),
                    r=[("A", "xr", s_), ("Acol", l % 2), mtok], w=[("hT", tt)])

        r0 = rms_sums(0, 0)
        r1 = rms_sums(1, 1)
        rms_apply(0, 0, r0)
        r2 = rms_sums(2, 0)
        rms_apply(1, 1, r1)
        r3 = rms_sums(3, 1)
        rms_apply(2, 0, r2)
        rms_apply(3, 1, r3)

        P.arena_reset("A")
        P.arena_reset("W")
        P.arena_reset("B")
        P.arena_reset("WO")
        gT = carve(BB, 0, 8192, BF16, "p (w t) -> p w t", t=L)[0:16]
        waf = carve(BB, 8192, 2048, BF16, "p (w n) -> p w n", n=512)[0:16]
        Wg = carve(BB, 10240, 512, BF16, "p (k n) -> p k n", n=32)
        qT = carve(AR, 0, 4096, BF16)
        kT = carve(AR, 4096, 4096, BF16)
        vh = carve(AR, 8192, 8192, BF16, "p (j v) -> p j v", v=256)
        cum = carve(AR, 16384, 8192, F32)
        e1 = carve(AR, 24576, 8192, F32)
        ofwd = carve(AR, 32768, 8192, BF16, "p (j v) -> p j v", v=256)
        ygT = carve(AR, 40960, 8192, BF16, "p (f t) -> p f t", t=L)
        ygTf = carve(AR, 40960, 8192, F32)
        Pb = [carve(AR, 49152, 4096, BF16), carve(WO, 4096, 4096, BF16), carve(WA, 19968, 4096, BF16)]
        Ptag = [("A", "P1"), ("WO", "P2"), ("W", "P3")]
        Wqk = carve(WA, 0, 4096, BF16, "p (k n) -> p k n", n=256)
        Wv = carve(WA, 4096, 4096, BF16, "p (k n) -> p k n", n=256)
        Wr = carve(WA, 8192, 4096, BF16, "p (k n) -> p k n", n=256)
        Wo = carve(WO, 0, 4096, BF16, "p (f d) -> p f d", d=D)
        statef = [carve(WA, 12288 + i * 1024, 1024, F32) for i in range(2)]
        stateb = [carve(WA, 14336 + i * 512, 512, BF16) for i in range(2)]
        srt = [carve(WA, 15360 + i * 1024, 1024, F32) for i in range(2)]
        osq = [carve(WA, 17408 + i * 512, 512, BF16) for i in range(2)]
        og = [carve(WA, 18944 + i * 512, 512, BF16) for i in range(2)]
        cumv = cum.rearrange("p (n c) -> p n c", c=128)
        e1v = e1.rearrange("p (n c) -> p n c", c=128)
        ptk = lambda i, hh: Ptag[i] + (hh,)
        bothP = lambda i: [ptk(i, 0), ptk(i, 1)]

        def win_dma(dst, c0, n, tok, l=l):
            P.dma("pool", lambda e: e.dma_start(
                out=dst, in_=win_d[l, :, c0:c0 + n].rearrange("(k p) n -> p k n", p=128)), w=[tok])

        def load_wqk(h):
            win_dma(Wqk[:, :, 0:128], 2048 + h * 128, 128, ("W", "wqk"))
            win_dma(Wqk[:, :, 128:256], 2560 + h * 128, 128, ("W", "wqk"))

        def load_wv(h):
            win_dma(Wv, 3072 + h * 256, 256, ("W", "wv"))

        def load_wr(h):
            win_dma(Wr, 4096 + h * 256, 256, ("W", "wr"))

        def load_wo(h, l=l):
            P.dma("pool", lambda e: e.dma_start(
                out=Wo, in_=wout_d[l, 1024 + h * 256:1024 + (h + 1) * 256, :].rearrange("(f p) d -> p f d", p=128)),
                w=[("WO", "wo")])

        def proj_qk_tiles(h):
            tiles = []
            for which in range(2):
                dst = qT if which == 0 else kT
                for tt in range(4):
                    st = {}

                    def mm(which=which, tt=tt, st=st):
                        b = nextbank()
                        st["b"] = b
                        for kc in range(8):
                            P.pe(lambda e, kc=kc, b=b: e.matmul(
                                psb[b][:, :], Wqk[:, kc, which * 128:(which + 1) * 128], hT[:, kc, tt * 512:(tt + 1) * 512],
                                start=(kc == 0), stop=(kc == 7)), r=[("W", "wqk"), ("hT", tt)], w=PB(b))

                    def ev(which=which, tt=tt, st=st, dst=dst):
                        b = st["b"]
                        P.act(lambda e: e.activation(
                            out=dst[:, tt * 512:(tt + 1) * 512], in_=psb[b][:, :], func=AF.Copy,
                            scale=(128.0 ** -0.5 if which == 0 else 1.0)),
                            r=PB(b), w=[("A", "qk", which)])
                    tiles.append((mm, ev))
            return tiles

        def proj_qk(h):
            for mm, ev in proj_qk_tiles(h):
                mm()
                ev()

        def proj_v(h):
            for jj in range(8):
                b = nextbank()
                for sub in range(2):
                    j = jj * 2 + sub
                    for kc in range(8):
                        P.pe(lambda e, j=j, sub=sub, kc=kc, b=b: e.matmul(
                            psb[b][:, sub * 256:(sub + 1) * 256], hT[:, kc, j * 128:(j + 1) * 128], Wv[:, kc, :],
                            start=(kc == 0), stop=(kc == 7)), r=[("W", "wv"), ("hT", j // 4)], w=PB(b))
                P.dve(lambda e, jj=jj, b=b: e.tensor_copy(out=vh[:, 2 * jj:2 * jj + 2, :],
                                                         in_=psb[b][:, :].rearrange("p (a v) -> p a v", v=256)),
                      r=PB(b), w=[("A", "vh")])

        def stage1a_pre(h, dr):
            par = dr
            nbcol = (0 if dr == 0 else 16) + l * 4 + h
            for tt in range(4):
                b = nextbank()
                P.pe(lambda e, tt=tt, b=b: e.matmul(
                    psb[b][:, :], waf[0:16, dr, h * 128:(h + 1) * 128], gT[0:16, dr, tt * 512:(tt + 1) * 512],
                    start=True, stop=True), r=[("B", "waf"), ("B", "gT")], w=PB(b))
                P.act(lambda e, tt=tt, b=b: e.activation(
                    out=e1[:, tt * 512:(tt + 1) * 512], in_=psb[b][:, :], func=AF.Exp, scale=-1.0,
                    bias=vneg[:, nbcol:nbcol + 1]), r=PB(b) + ["vneg"], w=[("A", "e1")])
            P.act(lambda e: e.activation(out=e1[:, :], in_=e1[:, :], func=AF.Ln, bias=1.0),
                  r=[("A", "e1")], w=[("A", "e1")])

        def stage1a_post(h, dr):
            par = dr
            if dr == 0:
                P.dve(lambda e: e.tensor_tensor_scan(out=cum[:, :], data0=e1[:, :], data1=e1[:, :], initial=0.0,
                                                     op0=ALU.add, op1=ALU.max),
                      r=[("A", "e1")], w=[("A", "cum")])
                P.dve(lambda e: e.memset(rpos[:, 0:1], 0.0), w=["rpos"])
                P.dve(lambda e: e.tensor_copy(out=rpos[:, 1:17], in_=cum[:, 127:2048:128]),
                      r=[("A", "cum")], w=["rpos"])
            else:
                P.dve(lambda e: e.memset(cum[:, 0:1], 0.0), w=[("A", "cum")])
                P.dve(lambda e: e.tensor_tensor_scan(out=cum[:, 1:2048], data0=e1[:, 0:2047], data1=e1[:, 0:2047],
                                                     initial=0.0, op0=ALU.add, op1=ALU.max),
                      r=[("A", "e1")], w=[("A", "cum")])
                P.dve(lambda e: e.tensor_copy(out=rpos[:, 0:16], in_=cum[:, 0:2048:128]),
                      r=[("A", "cum")], w=["rpos"])
                P.dve(lambda e: e.tensor_copy(out=rpos[:, 16:17], in_=cum[:, 2047:2048]),
                      r=[("A", "cum")], w=["rpos"])
            P.dve(lambda e: e.tensor_tensor(out=dtmp[:], in0=rpos[:, 1:17], in1=rpos[:, 0:16], op=ALU.subtract),
                  r=["rpos"], w=["dtmp"])
            P.act(lambda e: e.activation(out=dcol2[:, par, :], in_=dtmp[:], func=AF.Exp, scale=-1.0 / 16),
                  r=["dtmp"], w=[("dcol", par)])
            P.dve(lambda e: e.tensor_tensor(out=e1v, in0=cumv, in1=rpos[:, 0:16].unsqueeze(2).to_broadcast([128, 16, 128]),
                                            op=ALU.subtract), r=[("A", "cum"), "rpos"], w=[("A", "e1")])

        def stage1b(h, dr):
            par = dr
            srcs = [qT, kT, kT] if dr == 0 else [kT, qT, qT]
            srct = [0, 1, 1] if dr == 0 else [1, 0, 0]
            P.act(lambda e: e.activation(out=cum[:, :], in_=e1[:, :], func=AF.Exp, scale=-1.0 / 16),
                  r=[("A", "e1")], w=[("A", "cum")])
            P.act(lambda e: e.activation(out=e1[:, :], in_=e1[:, :], func=AF.Exp, scale=1.0 / 16),
                  r=[("A", "e1")], w=[("A", "e1")])
            P.dve(lambda e, src=srcs[0]: e.tensor_tensor(out=Pb[0], in0=cum[:, :], in1=src[:, :], op=ALU.mult),
                  r=[("A", "cum"), ("A", "qk", srct[0])], w=bothP(0))
            P.dve(lambda e, src=srcs[1]: e.tensor_tensor(out=Pb[1], in0=e1[:, :], in1=src[:, :], op=ALU.mult),
                  r=[("A", "e1"), ("A", "qk", srct[1])], w=bothP(1))
            P.dve(lambda e: e.tensor_tensor(
                out=Pb[2].rearrange("p (n c) -> p n c", c=128), in0=Pb[1].rearrange("p (n c) -> p n c", c=128),
                in1=dcol2[:, par, 0:16].unsqueeze(2).to_broadcast([128, 16, 128]), op=ALU.mult),
                r=bothP(1) + [("dcol", par)], w=bothP(2))

        def stage2(h, dr, side=None):
            par = dr
            side = list(side) if side else []
            pend = []
            if dr == 0:
                kside, qside, interq, ksrc = 1, 0, 0, 2
            else:
                kside, qside, interq, ksrc = 0, 1, 2, 0
            mask = maskf if dr == 0 else maskb
            for hh in range(2):
                for cc in range(8):
                    n = hh * 8 + cc
                    bnk = 2 if cc < 4 else 6
                    P.pe(lambda e, n=n, cc=cc, bnk=bnk: e.matmul(
                        psb[bnk][:, (cc % 4) * 128:(cc % 4 + 1) * 128], Pb[kside][:, n * 128:(n + 1) * 128],
                        Pb[qside][:, n * 128:(n + 1) * 128], start=True, stop=True),
                        r=[ptk(kside, hh), ptk(qside, hh)], w=PB(bnk))
                for q4 in range(2):
                    bnk = 2 if q4 == 0 else 6
                    c0 = hh * 1024 + q4 * 512
                    P.dve(lambda e, bnk=bnk, c0=c0: e.tensor_tensor(
                        out=Pb[1][:, c0:c0 + 512].rearrange("p (a c) -> p a c", c=128),
                        in0=psb[bnk][:, :].rearrange("p (a c) -> p a c", c=128),
                        in1=mask[:, :].unsqueeze(1).to_broadcast([128, 4, 128]), op=ALU.mult),
                        r=PB(bnk) + ["maskf", "maskb"], w=[ptk(1, hh)])
            for g4 in range(4):
                hh = g4 // 2
                for cc in range(4):
                    n = g4 * 4 + cc
                    P.pe(lambda e, n=n, cc=cc, g4=g4: e.transpose(TRB[g4 % 2][0][:, cc * 128:(cc + 1) * 128],
                                                                Pb[ksrc][:, n * 128:(n + 1) * 128], identb[:, :]),
                         r=[ptk(ksrc, hh), "identb"], w=PB(TRB[g4 % 2][1]))
                P.act(lambda e, g4=g4: e.activation(out=Pb[ksrc][:, g4 * 512:(g4 + 1) * 512],
                                                    in_=TRB[g4 % 2][0][:, 0:512], func=AF.Copy),
                      r=PB(TRB[g4 % 2][1]), w=[ptk(ksrc, hh)])
            order = list(range(16)) if dr == 0 else list(range(15, -1, -1))

            def smm(ci):
                n = order[ci]
                bnk = PSS[ci % 3]
                P.pe(lambda e, n=n, bnk=bnk: e.matmul(psb[bnk][:, 0:256], Pb[ksrc][:, n * 128:(n + 1) * 128], vh[:, n, :],
                                                     start=True, stop=True),
                     r=[ptk(ksrc, n // 8), ("A", "vh")], w=PB(bnk))

            smm(0)
            smm(1)
            sidx = 0
            for ci, n in enumerate(order):
                first = (ci == 0)
                last = (ci == 15)
                sl = ci % 2
                hh = n // 8
                if ci + 2 <= 14:
                    smm(ci + 2)
                if pend:
                    pend.pop(0)()
                if side and ci % 4 == 0:
                    mm_, ev_ = side.pop(0)
                    mm_()
                    pend.append(ev_)
                P.pe(lambda e, n=n, first=first, sl=sl: e.matmul(psb[PSO[sl]][:, 0:256], Pb[1][:, n * 128:(n + 1) * 128],
                                                                vh[:, n, :], start=True, stop=first),
                     r=[ptk(1, hh), ("A", "vh")], w=PB(PSO[sl]))
                if not first:
                    P.pe(lambda e, n=n, sl=sl, sidx=sidx: e.matmul(
                        psb[PSO[sl]][:, 0:256], Pb[interq][:, n * 128:(n + 1) * 128], stateb[sidx],
                        start=False, stop=True),
                        r=[ptk(interq, hh), ("W", "stateb", sidx)], w=PB(PSO[sl]))
                if not last:
                    bnk = PSS[ci % 3]
                    nidx = 1 - sidx
                    if first:
                        P.dve(lambda e, nidx=nidx, bnk=bnk: e.tensor_copy(out=statef[nidx], in_=psb[bnk][:, 0:256]),
                              r=PB(bnk), w=[("W", "statef", nidx)])
                    else:
                        P.dve(lambda e, nidx=nidx, sidx=sidx, n=n, bnk=bnk: e.scalar_tensor_tensor(
                            out=statef[nidx], in0=statef[sidx], scalar=dcol2[:, par, n:n + 1],
                            in1=psb[bnk][:, 0:256], op0=ALU.mult, op1=ALU.add),
                            r=PB(bnk) + [("W", "statef", sidx), ("dcol", par)], w=[("W", "statef", nidx)])
                    P.act(lambda e, nidx=nidx: e.activation(out=stateb[nidx], in_=statef[nidx], func=AF.Copy),
                          r=[("W", "statef", nidx)], w=[("W", "stateb", nidx)])
                    sidx = nidx
                if dr == 0:
                    P.act(lambda e, n=n, sl=sl: e.activation(out=ofwd[:, n, :], in_=psb[PSO[sl]][:, 0:256], func=AF.Copy),
                          r=PB(PSO[sl]), w=[("A", "ofwd", n)])
                else:
                    P.dve(lambda e, n=n, sl=sl: e.tensor_tensor(out=ofwd[:, n, :], in0=psb[PSO[sl]][:, 0:256],
                                                               in1=ofwd[:, n, :], op=ALU.add),
                          r=PB(PSO[sl]) + [("A", "ofwd", n)], w=[("A", "ofwd", n)])
            flush_side(side, pend)

        def flush_side(side, pend):
            for ev_ in pend:
                ev_()
            for mm_, ev_ in side:
                mm_()
                ev_()

        def finalize_pre1(h):
            for hh in range(2):
                P.act(lambda e, hh=hh: e.activation(
                    out=Pb[hh], in_=ofwd[:, hh * 8:(hh + 1) * 8, :].rearrange("p j v -> p (j v)"), func=AF.Square),
                    r=[("A", "ofwd", n) for n in range(hh * 8, hh * 8 + 8)], w=bothP(hh))

        def finalize_pre2(h):
            for hh in range(2):
                P.dve(lambda e, hh=hh: e.tensor_reduce(out=ssq[:, hh * 8:(hh + 1) * 8],
                                                      in_=Pb[hh].rearrange("p (j v) -> p j v", v=256), axis=AX.X, op=ALU.add),
                      r=bothP(hh), w=["ssq"])
            P.act(lambda e: e.activation(out=ssr[:, :], in_=ssq[:, :], func=AF.Sqrt, scale=1.0 / 256, bias=EPS),
                  r=["ssq"], w=["ssr"])
            P.dve(lambda e: e.reciprocal(out=ssr2[:, :], in_=ssr[:, :]), r=["ssr"], w=["ssr2"])

        def finalize(h):
            gc = 32 + l * 8 + h * 2

            def rproj(n):
                sl = n % 2
                for kc in range(8):
                    P.pe(lambda e, n=n, kc=kc, sl=sl: e.matmul(psb[PSR[sl]][:, 0:256], hT[:, kc, n * 128:(n + 1) * 128],
                                                              Wr[:, kc, :], start=(kc == 0), stop=(kc == 7)),
                         r=[("W", "wr"), ("hT", n // 4)], w=PB(PSR[sl]))

            def gate(n):
                sl = n % 2
                P.act(lambda e, sl=sl: e.activation(out=srt[sl], in_=psb[PSR[sl]][:, 0:256], func=AF.Silu),
                      r=PB(PSR[sl]), w=[("W", "srt", sl)])
                P.dve(lambda e, n=n, sl=sl: e.scalar_tensor_tensor(out=og[sl], in0=ofwd[:, n, :], scalar=ssr2[:, n:n + 1],
                                                                  in1=srt[sl], op0=ALU.mult, op1=ALU.mult),
                      r=[("A", "ofwd", n), "ssr2", ("W", "srt", sl)], w=[("W", "og", sl)])

            def trans(n):
                sl = n % 2
                for f2 in range(2):
                    P.pe(lambda e, f2=f2, sl=sl: e.transpose(TRB[sl][0][:, f2 * 128:(f2 + 1) * 128],
                                                            og[sl][:, f2 * 128:(f2 + 1) * 128], identb[:, :]),
                         r=[("W", "og", sl), "identb"], w=PB(TRB[sl][1]))

            def evac(n):
                sl = n % 2
                for f2 in range(2):
                    P.act(lambda e, f2=f2, n=n, sl=sl: e.activation(
                        out=ygT[:, f2, n * 128:(n + 1) * 128], in_=TRB[sl][0][:, f2 * 128:(f2 + 1) * 128],
                        func=AF.Identity, scale=vcols[:, gc + f2:gc + f2 + 1]),
                        r=PB(TRB[sl][1]) + ["vcols"], w=[("A", "ygT", n // 4)])

            rproj(0)
            rproj(1)
            gate(0)
            for n in range(16):
                trans(n)
                if n + 2 < 16:
                    rproj(n + 2)
                if n + 1 < 16:
                    gate(n + 1)
                evac(n)

        def outproj(h):
            for dt in range(8):
                for tt in range(4):
                    b = nextobank([0, 1, 3, 4])
                    for f2 in range(2):
                        P.pe(lambda e, dt=dt, tt=tt, f2=f2, b=b: e.matmul(
                            psb[b][:, :], Wo[:, f2, dt * 128:(dt + 1) * 128], ygT[:, f2, tt * 512:(tt + 1) * 512],
                            start=(f2 == 0), stop=(f2 == 1)), r=[("WO", "wo"), ("A", "ygT", tt)], w=PB(b))
                    residual(dt, tt, b, mT, mtok)

        P.dma("pool", lambda e, l=l: e.dma_start(out=Wg, in_=win_d[l, :, 5120:5152].rearrange("(k p) n -> p k n", p=128)),
              w=[("B", "Wg")])
        P.dma("pool", lambda e, l=l: e.dma_start(out=waf[:, 0, :], in_=waf_d[l]), w=[("B", "waf")])
        P.dma("pool", lambda e, l=l: e.dma_start(out=waf[:, 1, :], in_=wab_d[l]), w=[("B", "waf")])
        load_wqk(0)
        load_wv(0)
        load_wr(0)
        load_wo(0)
        for which in range(2):
            for tt in range(4):
                b = nextbank()
                for kc in range(8):
                    P.pe(lambda e, which=which, tt=tt, kc=kc, b=b: e.matmul(
                        psb[b][0:16, :], Wg[:, kc, which * 16:(which + 1) * 16], hT[:, kc, tt * 512:(tt + 1) * 512],
                        start=(kc == 0), stop=(kc == 7)), r=[("B", "Wg"), ("hT", tt)], w=PB(b))
                P.act(lambda e, which=which, tt=tt, b=b: e.activation(out=gT[:, which, tt * 512:(tt + 1) * 512],
                                                                     in_=psb[b][0:16, :], func=AF.Copy),
                      r=PB(b), w=[("B", "gT")])
        stage1a_pre(0, 0)
        stage1a_post(0, 0)
        proj_qk(0)
        load_wqk(1)
        proj_v(0)
        load_wv(1)
        for h in range(4):
            if h == 0:
                stage1b(h, 0)
            stage1a_pre(h, 1)
            stage2(h, 0)
            stage1a_post(h, 1)
            stage1b(h, 1)
            if h < 3:
                stage1a_pre(h + 1, 0)
            stage2(h, 1, side=(proj_qk_tiles(h + 1) if h < 3 else None))
            if h < 2:
                load_wqk(h + 2)
            finalize_pre1(h)
            if h < 3:
                stage1a_post(h + 1, 0)
            finalize_pre2(h)
            if h < 3:
                proj_v(h + 1)
                if h < 2:
                    load_wv(h + 2)
            if h < 3:
                stage1b(h + 1, 0)
            finalize(h)
            if h < 3:
                load_wr(h + 1)
            outproj(h)
            if h < 3:
                load_wo(h + 1)

        P.arena_reset("A")
        P.arena_reset("W")
        P.arena_reset("B")
        P.arena_reset("WO")
        MM = carve(BB, 0, 8192, BF16, "p (w g c d) -> p w g c d", w=2, g=4, c=2)
        wfm = carve(BB, 8192, 4096, BF16, "p (g c d) -> p g c d", g=4, c=2)
        P.dma("pool", lambda e, l=l: e.dma_start(out=wfm, in_=wfm_d[l].rearrange("g (c p) d -> p g c d", p=128)),
              w=[("B", "wfm")])
        for g in range(4):
            for cto in range(2):
                b = nextbank()
                for which in range(2):
                    for ctp in range(2):
                        P.pe(lambda e, g=g, cto=cto, which=which, ctp=ctp, b=b: e.matmul(
                            psb[b][:, which * 256:(which + 1) * 256], cs256[:, ctp, which, cto * 128:(cto + 1) * 128],
                            wfm[:, g, ctp, :], start=(ctp == 0), stop=(ctp == 1)),
                            r=["cs256", ("B", "wfm")], w=PB(b))
                P.act(lambda e, g=g, cto=cto, b=b: e.activation(
                    out=MM[:, :, g, cto, :], in_=psb[b][:, :].rearrange("p (w d) -> p w d", d=256), func=AF.Copy),
                    r=PB(b), w=[("B", "MM")])
        UT = carve(AR, 0, 4104, BF16)
        Up = carve(AR, 4104, 4096, BF16, "p (c t) -> p c t", t=1024)
        Um = carve(AR, 8200, 4096, BF16, "p (c t) -> p c t", t=1024)
        AB = carve(AR, 12296, 16384, BF16, "p (j w c) -> p j w c", w=2, c=512)
        zy = carve(AR, 28680, 16384, BF16, "p (c t) -> p c t", t=L)
        NR = 3
        ring = [carve(AR, 45064 + i * 2048, 2048, BF16, "p (w k) -> p w k", k=512) for i in range(NR)]
        P.pool(lambda e: e.memset(UT[:, 2048:2049], 0.0), w=[("A", "UTpad")])
        rcnt = 0
        for hf in range(2):
            Wu = carve(WA, 0, 8192, BF16, "p (k n) -> p k n", n=512)
            Wz = carve(WA, 8192, 8192, BF16, "p (k n) -> p k n", n=512)
            P.dma("pool", lambda e, hf=hf, l=l, Wu=Wu: e.dma_start(
                out=Wu, in_=win_d[l, :, hf * 512:(hf + 1) * 512].rearrange("(k p) n -> p k n", p=128)), w=[("W", "wu")])
            P.dma("pool", lambda e, hf=hf, l=l, Wz=Wz: e.dma_start(
                out=Wz, in_=win_d[l, :, 1024 + hf * 512:1024 + (hf + 1) * 512].rearrange("(k p) n -> p k n", p=128)),
                w=[("W", "wz")])
            Wo2 = carve(WO, 0, 8192, BF16, "p (f d) -> p f d", d=D)
            P.dma("pool", lambda e, hf=hf, l=l, Wo2=Wo2: e.dma_start(
                out=Wo2, in_=wout_d[l, hf * 512:(hf + 1) * 512, :].rearrange("(f p) d -> p f d", p=128)),
                w=[("WO", "wo2")])
            def zproj(c4):
                for tt in range(4):
                    b = nextbank()
                    for kc in range(8):
                        P.pe(lambda e, c4=c4, tt=tt, kc=kc, b=b: e.matmul(
                            psb[b][:, :], Wz[:, kc, c4 * 128:(c4 + 1) * 128], hT[:, kc, tt * 512:(tt + 1) * 512],
                            start=(kc == 0), stop=(kc == 7)), r=[("W", "wz"), ("hT", tt)], w=PB(b))
                    P.act(lambda e, c4=c4, tt=tt, b=b: e.activation(out=zy[:, c4, tt * 512:(tt + 1) * 512],
                                                                   in_=psb[b][:, :], func=AF.Silu),
                          r=PB(b), w=[("A", "zy", c4, tt)])

            for gl in range(2):
                g = hf * 2 + gl
                for ct in range(2):
                    c4 = gl * 2 + ct
                    for tt in range(4):
                        b = nextbank()
                        for kc in range(8):
                            P.pe(lambda e, c4=c4, tt=tt, kc=kc, b=b, Wu=Wu: e.matmul(
                                psb[b][:, :], Wu[:, kc, c4 * 128:(c4 + 1) * 128], hT[:, kc, tt * 512:(tt + 1) * 512],
                                start=(kc == 0), stop=(kc == 7)), r=[("W", "wu"), ("hT", tt)], w=PB(b))
                        P.act(lambda e, tt=tt, b=b: e.activation(out=UT[:, tt * 512:(tt + 1) * 512], in_=psb[b][:, :],
                                                                 func=AF.Copy), r=PB(b), w=[("A", "UT", tt)])
                    zproj(c4)
                    UTr = [("A", "UT", tt) for tt in range(4)] + [("A", "UTpad")]
                    P.dve(lambda e, ct=ct: e.tensor_tensor(out=Up[:, ct, :], in0=UT[:, 0:1024], in1=UT[:, 2048:1024:-1],
                                                          op=ALU.add), r=UTr, w=[("A", "Up", ct)])
                    P.pool(lambda e, ct=ct: e.tensor_tensor(out=Um[:, ct, :], in0=UT[:, 0:1024], in1=UT[:, 2048:1024:-1],
                                                           op=ALU.subtract), r=UTr, w=[("A", "Um", ct)])
                    P.dve(lambda e, ct=ct: e.tensor_copy(out=u1024[:, ct:ct + 1], in_=UT[:, 1024:1025]),
                          r=UTr, w=[("u1024", ct)])
                for j in range(8):
                    b = nextbank()
                    for which in range(2):
                        src = Up if which == 0 else Um
                        for ct in range(2):
                            P.pe(lambda e, j=j, which=which, ct=ct, b=b, src=src, g=g: e.matmul(
                                psb[b][:, which * 256:(which + 1) * 256], src[:, ct, j * 128:(j + 1) * 128],
                                MM[:, which, g, ct, :], start=(ct == 0), stop=(ct == 1)),
                                r=[("A", "Up", ct), ("A", "Um", ct), ("B", "MM")], w=PB(b))
                    P.act(lambda e, j=j, b=b, gl=gl: e.activation(
                        out=AB[:, j, :, gl * 256:(gl + 1) * 256], in_=psb[b][:, :].rearrange("p (w d) -> p w d", d=256),
                        func=AF.Copy), r=PB(b), w=[("A", "AB", j)])
                for ct in range(2):
                    P.pe(lambda e, ct=ct, g=g: e.matmul(psb[2][0:1, 0:256], u1024[:, ct:ct + 1], MM[:, 0, g, ct, :],
                                                       start=(ct == 0), stop=(ct == 1)),
                         r=[("u1024", ct), ("B", "MM")], w=PB(2))
                P.act(lambda e, gl=gl: e.activation(out=a1024[0:1, gl * 256:(gl + 1) * 256], in_=psb[2][0:1, 0:256],
                                                    func=AF.Copy), r=PB(2), w=["a1024"])
            for kt in range(4):
                hci = hf * 4 + kt
                hoist = (l + 1 < depth) and hci < 6
                if hoist:
                    WaH = carve(WA, 16384, 8192, BF16, "p (k n) -> p k n", n=512)
                    mod_chunk_dma(l + 1, hci, WaH, ("W", "waH"))
                for j in range(8):
                    rs = rcnt % NR
                    rcnt += 1
                    P.dma("sp", lambda e, j=j, kt=kt, rs=rs: e.dma_start(out=ring[rs], in_=tab_d[j, :, kt]),
                          w=[("A", "ring", rs)])
                    for c4 in range(4):
                        for which in range(2):
                            P.pe(lambda e, j=j, c4=c4, which=which, rs=rs: e.matmul(
                                psb[3 + c4][:, :], AB[:, j, which, c4 * 128:(c4 + 1) * 128], ring[rs][:, which, :],
                                start=(j == 0 and which == 0), stop=False),
                                r=[("A", "AB", j), ("A", "ring", rs)], w=PB(3 + c4))
                for c4 in range(4):
                    P.pe(lambda e, c4=c4, kt=kt: e.matmul(psb[3 + c4][:, :], a1024[0:1, c4 * 128:(c4 + 1) * 128],
                                                         alt[0:1, kt * 512:(kt + 1) * 512], start=False, stop=True),
                         r=["a1024", "alt"], w=PB(3 + c4))
                    P.dve(lambda e, c4=c4, kt=kt: e.tensor_tensor(
                        out=zy[:, c4, kt * 512:(kt + 1) * 512], in0=psb[3 + c4][:, :], in1=zy[:, c4, kt * 512:(kt + 1) * 512],
                        op=ALU.mult), r=PB(3 + c4) + [("A", "zy", c4, kt)], w=[("A", "zy", c4, kt)])
                if hoist:
                    mod_chunk_mm(l + 1, hci, WaH, ("W", "waH"))
            if hf == 1 and l + 1 < depth:
                mod_finish(l + 1)
            for dt in range(8):
                for tt in range(4):
                    b = nextobank([0, 1, 3, 4])
                    for c4 in range(4):
                        P.pe(lambda e, dt=dt, tt=tt, c4=c4, b=b, Wo2=Wo2: e.matmul(
                            psb[b][:, :], Wo2[:, c4, dt * 128:(dt + 1) * 128], zy[:, c4, tt * 512:(tt + 1) * 512],
                            start=(c4 == 0), stop=(c4 == 3)), r=[("WO", "wo2"), ("A", "zy", c4, tt)], w=PB(b))
                    residual(dt, tt, b, mT, mtok)

    P.arena_reset("A")
    onT = carve(AR, 0, 16384, F32, "p (k t) -> p k t", t=512)
    ostage = [carve(AR, 16384 + i * 4096, 4096, F32) for i in range(4)]
    xr2 = [carve(AR, 32768 + i * 2048, 2048, F32) for i in range(2)]
    finals = []
    rs_ = [None] * 4
    rs_[0] = rms_sums(0, 0)
    rs_[1] = rms_sums(1, 1)
    for tt in range(4):
        rstd = rs_[tt]
        for kc in range(8):
            s = kc % 2
            P.dve(lambda e, kc=kc, tt=tt, s=s, rstd=rstd: e.tensor_tensor(
                out=xr2[s], in0=xT[:, kc, tt * 512:(tt + 1) * 512], in1=rstd, op=ALU.mult),
                r=[("xT", kc, tt), ("A", "rstd", tt % 2)], w=[("A", "xr2", s)])
            P.act(lambda e, kc=kc, s=s: e.activation(out=onT[:, kc, :], in_=xr2[s], func=AF.Identity,
                                                     scale=vcols[:, 96 + kc:97 + kc]),
                  r=[("A", "xr2", s), "vcols"], w=[("A", "onT", kc)])
        if tt + 2 < 4:
            rs_[tt + 2] = rms_sums(tt + 2, tt % 2)
        for jj in range(4):
            j = tt * 4 + jj
            s = j % 4
            for g in range(2):
                b = nextbank()
                for q in range(4):
                    kc = g * 4 + q
                    P.pe(lambda e, b=b, q=q, kc=kc, jj=jj: e.transpose(psb[b][:, q * 128:(q + 1) * 128],
                                                                     onT[:, kc, jj * 128:(jj + 1) * 128], ident[:, :]),
                         r=[("A", "onT", kc), "ident"], w=PB(b))
                if g == 0:
                    P.act(lambda e, b=b, s=s: e.activation(out=ostage[s][:, 0:512], in_=psb[b][:, :], func=AF.Copy),
                          r=PB(b), w=[("A", "ost", s, 0)])
                else:
                    P.dve(lambda e, b=b, s=s: e.tensor_copy(out=ostage[s][:, 512:1024], in_=psb[b][:, :]),
                          r=PB(b), w=[("A", "ost", s, 1)])
            finals.append(P.dma("sp", lambda e, j=j, s=s: e.dma_start(out=y_d[j * 128:(j + 1) * 128, :], in_=ostage[s]),
                                r=[("A", "ost", s, 0), ("A", "ost", s, 1)]))
    P.emit(final_waits=finals)
    es.close()
    return nc


_NC_CACHE = {}


def _prep_inputs(x, c, norm_g, w_ada, b_ada, w_in, w_fmap, w_af, b_af, w_ab, b_ab, gla_norm_g, w_out, final_g):
    f = lambda a: np.ascontiguousarray(np.asarray(a, dtype=np.float32))
    cst = _consts()
    shared = {
        "b_ada": f(b_ada).reshape(96, 128),
        "w_ada": f(w_ada), "w_in": f(w_in), "w_fmap": f(w_fmap), "w_af": f(w_af), "w_ab": f(w_ab), "w_out": f(w_out),
    }
    shared.update(cst)
    x = f(x)
    c = f(c)
    in_maps = []
    for b in range(8):
        vrows = np.concatenate([f(norm_g).reshape(32, 128), f(gla_norm_g).reshape(32, 128), f(b_af).reshape(16, 128),
                                f(b_ab).reshape(16, 128), f(final_g).reshape(8, 128), c[b].reshape(8, 128)], axis=0)
        m = dict(shared)
        m["x"] = x[b]
        m["vrows"] = np.ascontiguousarray(vrows)
        in_maps.append(m)
    return in_maps


def kernel(x, c, norm_g, w_ada, b_ada, w_in, w_fmap, w_af, b_af, w_ab, b_ab, gla_norm_g, w_out, final_g):
    in_maps = _prep_inputs(x, c, norm_g, w_ada, b_ada, w_in, w_fmap, w_af, b_af, w_ab, b_ab, gla_norm_g, w_out, final_g)
    if "nc" not in _NC_CACHE:
        _NC_CACHE["nc"] = build_nc(DEPTH)
    nc = _NC_CACHE["nc"]
    res = run_bass_kernel_spmd(nc, in_maps, core_ids=list(range(8)))
    out = np.stack([np.asarray(r["y"], dtype=np.float32) for r in res.results], axis=0)
    return out
```
